# Optimizing a Trainium2 kernel written in Bass

```python
import math
import jax, jax.numpy as jnp
from jax import lax
import numpy as np

D_MODEL = 1024
BATCH = 8
SEQ = 2048
DEPTH = 4
DEC_BATCH = 128
DEC_SEQ = 1
PAST_LEN = 16384
PAGE_SIZE = 128

N_MIXERS = 2
N_CONV_LAYERS = (DEPTH + 1) // 2
N_SSD_LAYERS = DEPTH // 2
D_FF = 2816
PLE_DIM = 256
CM_KERNEL = 31
EXPAND = 2
D_INNER = EXPAND * D_MODEL
HEAD_DIM = 64
N_SSD_HEADS = D_INNER // HEAD_DIM
N_GROUPS = 8
HEADS_PER_GROUP = N_SSD_HEADS // N_GROUPS
D_STATE = 128
SSD_CONV_K = 4
SSD_CHUNK = 128
CONV_DIM = D_INNER + 2 * N_GROUPS * D_STATE
SSD_IN_DIM = D_INNER + CONV_DIM + N_SSD_HEADS
EPS = 1e-6

kernel_name = "macaron_conformer_ssd_hybrid_step"


def rmsnorm(x, g):
    xf = x.astype(jnp.float32)
    y = xf * lax.rsqrt(jnp.mean(xf * xf, axis=-1, keepdims=True) + EPS) * g.astype(jnp.float32)
    return y.astype(x.dtype)


def layernorm(x, g, b):
    xf = x.astype(jnp.float32)
    mu = jnp.mean(xf, axis=-1, keepdims=True)
    var = jnp.mean(jnp.square(xf - mu), axis=-1, keepdims=True)
    y = (xf - mu) * lax.rsqrt(var + EPS) * g.astype(jnp.float32) + b.astype(jnp.float32)
    return y.astype(x.dtype)


def swiglu(x, wg, wu, wd):
    return (jax.nn.silu(x @ wg) * (x @ wu)) @ wd


def causal_dwconv(u, buf, w, b):
    k = w.shape[0]
    full = jnp.concatenate([buf.astype(u.dtype), u], axis=1)
    y = lax.conv_general_dilated(full, w[:, None, :].astype(u.dtype), window_strides=(1,), padding='VALID',
                                 dimension_numbers=('NWC', 'WIO', 'NWC'), feature_group_count=u.shape[-1])
    return y + b, full[:, full.shape[1] - (k - 1):]


def ssd_chunked(x, dt, A, Bm, Cm, h0, chunk):
    b, l, H, P = x.shape
    G, N = Bm.shape[2], Bm.shape[3]
    R = H // G
    c = l // chunk
    xdt = (x * dt[..., None]).reshape(b, c, chunk, G, R, P)
    a = (dt * A).reshape(b, c, chunk, G, R)
    Bc = Bm.reshape(b, c, chunk, G, N)
    Cc = Cm.reshape(b, c, chunk, G, N)
    a_cum = jnp.cumsum(a, axis=2)
    seg = a_cum[:, :, :, None] - a_cum[:, :, None, :]
    causal = jnp.tril(jnp.ones((chunk, chunk), dtype=bool))[:, :, None, None]
    Lmat = jnp.exp(jnp.where(causal, seg, -jnp.inf))
    cb = jnp.einsum('bcqgn,bckgn->bcqkg', Cc, Bc)
    y_diag = jnp.einsum('bcqkg,bcqkgr,bckgrp->bcqgrp', cb, Lmat, xdt)
    decay = jnp.exp(a_cum[:, :, -1:] - a_cum)
    states = jnp.einsum('bckgn,bckgr,bckgrp->bcgrpn', Bc, decay, xdt)
    chunk_decay = jnp.exp(a_cum[:, :, -1])

    def step(h, inp):
        s, d = inp
        return d[..., None, None] * h + s, h

    h_final, h_prev = lax.scan(step, h0, (jnp.swapaxes(states, 0, 1), jnp.swapaxes(chunk_decay, 0, 1)))
    h_prev = jnp.swapaxes(h_prev, 0, 1)
    y_off = jnp.einsum('bcqgn,bcgrpn,bcqgr->bcqgrp', Cc, h_prev, jnp.exp(a_cum))
    return (y_diag + y_off).reshape(b, l, H, P), h_final


def conv_mixer(h, buf, w_in, b_in, dw, dw_b, ln_g, ln_b, w_out, b_out):
    u = h @ w_in + b_in
    a, g = jnp.split(u, 2, axis=-1)
    u = a * jax.nn.sigmoid(g)
    v, new_buf = causal_dwconv(u, buf, dw, dw_b)
    v = jax.nn.silu(layernorm(v, ln_g, ln_b))
    return v @ w_out + b_out, new_buf


def ssd_mixer(h, buf, h0, w_in, conv_w, conv_b, dt_bias, A_log, D_skip, norm_g, w_out, chunk):
    b, l, _ = h.shape
    proj = h @ w_in
    z = proj[..., :D_INNER]
    xbc = proj[..., D_INNER:D_INNER + CONV_DIM]
    dt_raw = proj[..., D_INNER + CONV_DIM:]
    xbc, new_buf = causal_dwconv(xbc, buf, conv_w, conv_b)
    xbc = jax.nn.silu(xbc).astype(jnp.float32)
    GN = N_GROUPS * D_STATE
    xs = xbc[..., :D_INNER].reshape(b, l, N_SSD_HEADS, HEAD_DIM)
    Bm = xbc[..., D_INNER:D_INNER + GN].reshape(b, l, N_GROUPS, D_STATE)
    Cm = xbc[..., D_INNER + GN:].reshape(b, l, N_GROUPS, D_STATE)
    dt = jax.nn.softplus(dt_raw.astype(jnp.float32) + dt_bias.astype(jnp.float32))
    A = -jnp.exp(A_log.astype(jnp.float32))
    h0f = h0.astype(jnp.float32).reshape(b, N_GROUPS, HEADS_PER_GROUP, HEAD_DIM, D_STATE)
    y, h_fin = ssd_chunked(xs, dt, A, Bm, Cm, h0f, chunk)
    y = y + D_skip.astype(jnp.float32)[:, None] * xs
    y = y.reshape(b, l, D_INNER) * jax.nn.silu(z.astype(jnp.float32))
    yg = y.reshape(b, l, N_GROUPS, D_INNER // N_GROUPS)
    yg = yg * lax.rsqrt(jnp.mean(yg * yg, axis=-1, keepdims=True) + EPS)
    y = (yg.reshape(b, l, D_INNER) * norm_g.astype(jnp.float32)).astype(h.dtype)
    return y @ w_out, new_buf, h_fin.reshape(b, N_SSD_HEADS, HEAD_DIM, D_STATE).astype(h.dtype)


def trunk(x, p, conv_state, ssd_conv_state, ssd_state,
          norm_ffn1, w_ffn1_gate, w_ffn1_up, w_ffn1_down, norm_mix,
          norm_ffn2, w_ffn2_gate, w_ffn2_up, w_ffn2_down, norm_ple, w_ple_gate, w_ple_proj,
          cm_w_in, cm_b_in, cm_dw, cm_dw_b, cm_ln_g, cm_ln_b, cm_w_out, cm_b_out,
          ssd_w_in, ssd_conv_w, ssd_conv_b, ssd_dt_bias, ssd_A_log, ssd_D, ssd_norm, ssd_w_out, final_norm):
    L = x.shape[1]
    chunk = SSD_CHUNK if L % SSD_CHUNK == 0 else L
    conv_out, xbc_out, ssm_out = [], [], []
    for i in range(DEPTH):
        x = x + 0.5 * swiglu(rmsnorm(x, norm_ffn1[i]), w_ffn1_gate[i], w_ffn1_up[i], w_ffn1_down[i])
        hn = rmsnorm(x, norm_mix[i])
        j = i // N_MIXERS
        if i % N_MIXERS == 0:
            out, nb = conv_mixer(hn, conv_state[j], cm_w_in[j], cm_b_in[j], cm_dw[j], cm_dw_b[j],
                                 cm_ln_g[j], cm_ln_b[j], cm_w_out[j], cm_b_out[j])
            conv_out.append(nb)
        else:
            out, nb, nh = ssd_mixer(hn, ssd_conv_state[j], ssd_state[j], ssd_w_in[j], ssd_conv_w[j], ssd_conv_b[j],
                                    ssd_dt_bias[j], ssd_A_log[j], ssd_D[j], ssd_norm[j], ssd_w_out[j], chunk)
            xbc_out.append(nb)
            ssm_out.append(nh)
        x = x + out
        x = x + 0.5 * swiglu(rmsnorm(x, norm_ffn2[i]), w_ffn2_gate[i], w_ffn2_up[i], w_ffn2_down[i])
        gate = jax.nn.sigmoid(rmsnorm(x, norm_ple[i]) @ w_ple_gate[i])
        x = x + gate * (p[i].astype(x.dtype) @ w_ple_proj[i])
    return rmsnorm(x, final_norm), jnp.stack(conv_out), jnp.stack(xbc_out), jnp.stack(ssm_out)


def setup_inputs(seed: int = 0) -> dict:
    key = jax.random.key(seed)
    ks = iter(jax.random.split(key, 64))
    f32 = jnp.float32

    def nrm(shape, scale):
        return jax.random.normal(next(ks), shape, f32) * scale

    def gain(shape):
        return 1.0 + nrm(shape, 0.02)

    NC, NS = N_CONV_LAYERS, N_SSD_LAYERS
    dt0 = jnp.exp(jax.random.uniform(next(ks), (NS, N_SSD_HEADS), f32, math.log(1e-3), math.log(1e-1)))
    return {
        "x_prompt": nrm((BATCH, SEQ, D_MODEL), 1.0),
        "x_sample": nrm((DEC_BATCH, DEC_SEQ, D_MODEL), 1.0),
        "state_conv": nrm((NC, DEC_BATCH, CM_KERNEL - 1, D_MODEL), 0.5),
        "state_ssd_conv": nrm((NS, DEC_BATCH, SSD_CONV_K - 1, CONV_DIM), 1.0),
        "state_ssd": nrm((NS, DEC_BATCH, N_SSD_HEADS, HEAD_DIM, D_STATE), 0.1),
        "p_prompt": nrm((DEPTH, BATCH, SEQ, PLE_DIM), 1.0),
        "p_sample": nrm((DEPTH, DEC_BATCH, DEC_SEQ, PLE_DIM), 1.0),
        "norm_ffn1": gain((DEPTH, D_MODEL)),
        "w_ffn1_gate": nrm((DEPTH, D_MODEL, D_FF), D_MODEL ** -0.5),
        "w_ffn1_up": nrm((DEPTH, D_MODEL, D_FF), D_MODEL ** -0.5),
        "w_ffn1_down": nrm((DEPTH, D_FF, D_MODEL), D_FF ** -0.5),
        "norm_mix": gain((DEPTH, D_MODEL)),
        "norm_ffn2": gain((DEPTH, D_MODEL)),
        "w_ffn2_gate": nrm((DEPTH, D_MODEL, D_FF), D_MODEL ** -0.5),
        "w_ffn2_up": nrm((DEPTH, D_MODEL, D_FF), D_MODEL ** -0.5),
        "w_ffn2_down": nrm((DEPTH, D_FF, D_MODEL), D_FF ** -0.5),
        "norm_ple": gain((DEPTH, D_MODEL)),
        "w_ple_gate": nrm((DEPTH, D_MODEL, D_MODEL), D_MODEL ** -0.5),
        "w_ple_proj": nrm((DEPTH, PLE_DIM, D_MODEL), PLE_DIM ** -0.5),
        "cm_w_in": nrm((NC, D_MODEL, 2 * D_MODEL), D_MODEL ** -0.5),
        "cm_b_in": nrm((NC, 2 * D_MODEL), 0.02),
        "cm_dw": nrm((NC, CM_KERNEL, D_MODEL), CM_KERNEL ** -0.5),
        "cm_dw_b": nrm((NC, D_MODEL), 0.02),
        "cm_ln_g": gain((NC, D_MODEL)),
        "cm_ln_b": nrm((NC, D_MODEL), 0.02),
        "cm_w_out": nrm((NC, D_MODEL, D_MODEL), D_MODEL ** -0.5),
        "cm_b_out": nrm((NC, D_MODEL), 0.02),
        "ssd_w_in": nrm((NS, D_MODEL, SSD_IN_DIM), D_MODEL ** -0.5),
        "ssd_conv_w": nrm((NS, SSD_CONV_K, CONV_DIM), SSD_CONV_K ** -0.5),
        "ssd_conv_b": nrm((NS, CONV_DIM), 0.02),
        "ssd_dt_bias": dt0 + jnp.log(-jnp.expm1(-dt0)),
        "ssd_A_log": jnp.log(jax.random.uniform(next(ks), (NS, N_SSD_HEADS), f32, 1.0, 16.0)),
        "ssd_D": gain((NS, N_SSD_HEADS)),
        "ssd_norm": gain((NS, D_INNER)),
        "ssd_w_out": nrm((NS, D_INNER, D_MODEL), D_INNER ** -0.5),
        "final_norm": gain((D_MODEL,)),
    }


def reference(x_prompt, x_sample, state_conv, state_ssd_conv, state_ssd, p_prompt, p_sample,
              norm_ffn1, w_ffn1_gate, w_ffn1_up, w_ffn1_down, norm_mix,
              norm_ffn2, w_ffn2_gate, w_ffn2_up, w_ffn2_down, norm_ple, w_ple_gate, w_ple_proj,
              cm_w_in, cm_b_in, cm_dw, cm_dw_b, cm_ln_g, cm_ln_b, cm_w_out, cm_b_out,
              ssd_w_in, ssd_conv_w, ssd_conv_b, ssd_dt_bias, ssd_A_log, ssd_D, ssd_norm, ssd_w_out, final_norm):
    weights = (norm_ffn1, w_ffn1_gate, w_ffn1_up, w_ffn1_down, norm_mix,
               norm_ffn2, w_ffn2_gate, w_ffn2_up, w_ffn2_down, norm_ple, w_ple_gate, w_ple_proj,
               cm_w_in, cm_b_in, cm_dw, cm_dw_b, cm_ln_g, cm_ln_b, cm_w_out, cm_b_out,
               ssd_w_in, ssd_conv_w, ssd_conv_b, ssd_dt_bias, ssd_A_log, ssd_D, ssd_norm, ssd_w_out, final_norm)
    dt = x_prompt.dtype
    zc = jnp.zeros((N_CONV_LAYERS, BATCH, CM_KERNEL - 1, D_MODEL), dt)
    zx = jnp.zeros((N_SSD_LAYERS, BATCH, SSD_CONV_K - 1, CONV_DIM), dt)
    zs = jnp.zeros((N_SSD_LAYERS, BATCH, N_SSD_HEADS, HEAD_DIM, D_STATE), dt)
    y_prompt, conv_p, xbc_p, ssm_p = trunk(x_prompt, p_prompt, zc, zx, zs, *weights)
    y_sample, conv_s, xbc_s, ssm_s = trunk(x_sample, p_sample, state_conv, state_ssd_conv, state_ssd, *weights)
    return (y_prompt, y_sample, conv_p, xbc_p, ssm_p, conv_s, xbc_s, ssm_s)
```

```python
import numpy as np
from contextlib import ExitStack, contextmanager
import concourse.bass as bass
import concourse.mybir as mybir
from concourse.bass_utils import run_bass_kernel_spmd

F32 = mybir.dt.float32
BF16 = mybir.dt.bfloat16
AF = mybir.ActivationFunctionType
ALU = mybir.AluOpType
AX = mybir.AxisListType
P = 128
D = 1024
DFF = 2816
T = 2048
NSMP = 16
NT = T + NSMP
DEPTH = 4
DIN = 2048
CONVD = 4096
NH = 32
PLE = 256
CK = 31
EPS = 1e-6
NSLOT = 5
SLOTE = 2048
NCORES = 8
TILES = [(0, 512), (512, 512), (1024, 512), (1536, 512), (2048, 16)]


def fm(v):
    v = np.asarray(v, np.float32).reshape(-1)
    return np.ascontiguousarray(v.reshape(-1, 128).T)


def vec_entries(inp):
    E = []

    def add(name, n, fn):
        if inp is None:
            E.append((name, n, None))
        else:
            a = np.zeros((128, n), np.float32)
            b = fn()
            a[: b.shape[0], :] = b
            E.append((name, n, a))

    for i in range(DEPTH):
        for nm in ("norm_ffn1", "norm_mix", "norm_ffn2", "norm_ple"):
            add(f"{nm}{i}", 8, lambda nm=nm, i=i: fm(inp[nm][i]))
    add("final_norm", 8, lambda: fm(inp["final_norm"]))
    for j in range(2):
        add(f"cbin{j}", 16, lambda j=j: fm(inp["cm_b_in"][j]))
        add(f"cdw{j}", 8 * CK, lambda j=j: np.asarray(inp["cm_dw"][j], np.float32).T.reshape(8, 128, CK).transpose(1, 0, 2).reshape(128, 8 * CK))
        add(f"cdwb{j}", 8, lambda j=j: fm(inp["cm_dw_b"][j]))
        add(f"clng{j}", 8, lambda j=j: fm(inp["cm_ln_g"][j]))
        add(f"clnb{j}", 8, lambda j=j: fm(inp["cm_ln_b"][j]))
        add(f"cbout{j}", 8, lambda j=j: fm(inp["cm_b_out"][j]))
    for j in range(2):
        add(f"scw{j}", 32 * 4, lambda j=j: np.asarray(inp["ssd_conv_w"][j], np.float32).T.reshape(32, 128, 4).transpose(1, 0, 2).reshape(128, 128))
        add(f"scb{j}", 32, lambda j=j: fm(inp["ssd_conv_b"][j]))
        add(f"snorm{j}", 16, lambda j=j: fm(inp["ssd_norm"][j]))
        add(f"sD{j}", 16, lambda j=j: np.repeat(np.asarray(inp["ssd_D"][j], np.float32).reshape(16, 2, 1), 64, axis=2).transpose(1, 2, 0).reshape(128, 16))
        add(f"dtb32{j}", 1, lambda j=j: np.asarray(inp["ssd_dt_bias"][j], np.float32).reshape(32, 1))
        add(f"alog32{j}", 1, lambda j=j: np.asarray(inp["ssd_A_log"][j], np.float32).reshape(32, 1))
        add(f"dtbbc{j}", 32, lambda j=j: np.tile(np.asarray(inp["ssd_dt_bias"][j], np.float32).reshape(1, 32), (128, 1)))
        add(f"alogbc{j}", 32, lambda j=j: np.tile(np.asarray(inp["ssd_A_log"][j], np.float32).reshape(1, 32), (128, 1)))
    return E


def vec_offsets():
    off = {}
    o = 0
    for name, n, _ in vec_entries(None):
        off[name] = o
        o += n
    return off, o


def const_array():
    c = np.zeros((128, 512), np.float32)
    i = np.arange(128)
    c[:, 0:128] = np.eye(128, dtype=np.float32)
    c[:, 128:256] = (i[:, None] <= i[None, :]).astype(np.float32)
    c[:, 256:384] = (i[:, None] > i[None, :]).astype(np.float32)
    c[:, 384:512] = 1.0
    return c


def sel_array():
    s = np.zeros((32, 16, 128), np.float32)
    for hh in range(16):
        for m in range(128):
            s[2 * hh + m // 64, hh, m] = 1.0
    return s.reshape(32, 2048)


class Buf:
    def __init__(self, t, name):
        self.t = t
        self.name = name
        self.w = None
        self.r = {}
        self.dsem = None


class Eng:
    def __init__(self, raw, sem, name):
        self.raw = raw
        self.sem = sem
        self.name = name
        self.cnt = 0
        self.seen = {}


class K:
    def __init__(self, nc, es):
        self.nc = nc
        self.es = es
        self.dry = False
        self.alloc = es
        self.phase_bufs = None
        self.sem_free = []
        self.sem_tot = {}
        self.pending = {}
        self.final = {}
        self.nsem = 0
        self.uid = 0

    def init_engines(self):
        nc, es = self.nc, self.es

        def mk(raw, name):
            return Eng(raw, es.enter_context(nc.semaphore("sem_" + name)), name)

        self.pe = mk(nc.tensor, "pe")
        self.act = mk(nc.scalar, "act")
        self.dve = mk(nc.vector, "dve")
        self.pool = mk(nc.gpsimd, "pool")
        self.sp = mk(nc.sync, "sp")
        self.engs = [self.pe, self.act, self.dve, self.pool, self.sp]
        for e in self.engs:
            e.cnt = 0
            e.seen = {}

    def buf(self, name, shape, dt):
        self.uid += 1
        t = self.alloc.enter_context(self.nc.sbuf_tensor(f"{name}_{self.uid}", list(shape), dt))
        b = Buf(t, name)
        if self.phase_bufs is not None:
            self.phase_bufs.append(b)
            b.in_phase = True
        return b

    def sub(self, b, n, tag=""):
        out = []
        for i in range(n):
            s = Buf(b.t, f"{b.name}{tag}{i}")
            s.in_phase = getattr(b, "in_phase", False)
            if self.phase_bufs is not None and s.in_phase:
                self.phase_bufs.append(s)
            out.append(s)
        return out

    def _getsem(self, b):
        if b.dsem is None:
            if self.sem_free and getattr(b, "in_phase", False):
                b.dsem = self.sem_free.pop()
            else:
                self.nsem += 1
                sem = self.es.enter_context(self.nc.semaphore(f"dsem{self.nsem}"))
                b.dsem = (f"dsem{self.nsem}", sem)
                self.sem_tot[b.dsem[0]] = 0
        return b.dsem

    def _wait(self, e, dep):
        if dep is None:
            return
        key, sem, val = dep
        if e.seen.get(key, 0) >= val:
            return
        e.raw.wait_ge(sem, val)
        e.seen[key] = val

    def _deps(self, e, reads, writes, isdma=False):
        for b in reads:
            self._wait(e, b.w)
        same_ok = (not isdma) and e.name == "pe"
        for b in writes:
            if b.w is not None and not (same_ok and b.w[0] == e.name):
                self._wait(e, b.w)
            for k, (sem, val) in b.r.items():
                if not (same_ok and k == e.name):
                    self._wait(e, (k, sem, val))

    def op(self, e, fn, reads=(), writes=(), inc=True):
        if self.dry:
            return None
        self._deps(e, reads, writes)
        ins = fn()
        self.opidx = getattr(self, "opidx", 0) + 1
        for b in reads:
            b.last_read = self.opidx
        for b in writes:
            b.last_write = self.opidx
        if e.name == "pe":
            self.npe = getattr(self, "npe", 0) + 1
        if inc:
            e.cnt += 1
            ins.then_inc(e.sem, 1)
            c = e.cnt
        else:
            c = e.cnt + 1
        for b in reads:
            b.r[e.name] = (e.sem, c)
        for b in writes:
            b.w = (e.name, e.sem, c)
            b.r = {}
        return ins

    def dma(self, q, out, in_, reads=(), writes=(), final=False, **kw):
        if self.dry:
            return
        self._deps(q, reads, writes, isdma=True)
        ins = q.raw.dma_start(out=out, in_=in_, **kw)
        b = (list(writes) + list(reads))[0]
        key, sem = self._getsem(b)
        self.sem_tot[key] += 16
        tot = self.sem_tot[key]
        ins.then_inc(sem, 16)
        for w in writes:
            w.w = (key, sem, tot)
            w.r = {}
        for r in reads:
            r.r[key] = (sem, tot)
        self.pending[key] = (key, sem, tot)
        if final:
            self.final[key] = (key, sem, tot)

    def mark(self, label):
        if not self.dry:
            self.marks = getattr(self, "marks", [])
            self.marks.append((label, getattr(self, "npe", 0)))

    def barrier(self):
        if self.dry:
            return
        engs = [self.pe, self.act, self.dve, self.sp, self.pool]
        for e in engs:
            for e2 in engs:
                if e2 is not e and e2.cnt > 0:
                    self._wait(e, (e2.name, e2.sem, e2.cnt))
            for dep in self.pending.values():
                self._wait(e, dep)
        self.pending = {}

    @contextmanager
    def phase(self):
        old_alloc, old_bufs = self.alloc, self.phase_bufs
        with ExitStack() as st:
            self.alloc = st
            self.phase_bufs = []
            yield
            self.barrier()
            for b in self.phase_bufs:
                if b.dsem is not None:
                    self.sem_free.append(b.dsem)
        self.alloc, self.phase_bufs = old_alloc, old_bufs


def build():
    nc = bass.Bass("TRN2", target_bir_lowering=False)
    voff, NV = vec_offsets()

    def din(name, shape):
        return nc.dram_tensor(name, list(shape), F32, kind="ExternalInput").ap()

    def dout(name, shape):
        return nc.dram_tensor(name, list(shape), F32, kind="ExternalOutput").ap()

    xp = din("xp", [T, D])
    xs = din("xs", [NSMP, D])
    stc = din("stc", [2, NSMP * 30, D])
    stsc = din("stsc", [2, NSMP * 3, CONVD])
    stss = din("stss", [2, NSMP, NH * 64, 128])
    pp = din("pp", [DEPTH, T, PLE])
    psm = din("psm", [DEPTH, NSMP, PLE])
    W = {}
    for nm, shp in [("w_ffn1_gate", [DEPTH, D, DFF]), ("w_ffn1_up", [DEPTH, D, DFF]), ("w_ffn1_down", [DEPTH, DFF, D]),
                    ("w_ffn2_gate", [DEPTH, D, DFF]), ("w_ffn2_up", [DEPTH, D, DFF]), ("w_ffn2_down", [DEPTH, DFF, D]),
                    ("w_ple_gate", [DEPTH, D, D]), ("w_ple_proj", [DEPTH, PLE, D]),
                    ("cm_w_in", [2, D, 2 * D]), ("cm_w_out", [2, D, D]),
                    ("ssd_w_in", [2, D, 6176]), ("ssd_w_out", [2, DIN, D])]:
        W[nm] = din(nm, shp)
    vecs_d = din("vecs", [128, NV])
    cst_d = din("cst", [128, 512])
    sel_d = din("sel", [32, 2048])
    yp = dout("yp", [T, D])
    ysd = dout("ys", [NSMP, D])
    ocp = dout("ocp", [2, 30, D])
    oxp = dout("oxp", [2, 3, CONVD])
    osp = dout("osp", [2, NH * 64, 128])
    ocs = dout("ocs", [2, NSMP, 30, D])
    oxs = dout("oxs", [2, NSMP, 3, CONVD])
    oss = dout("oss", [2, NSMP, NH * 64, 128])

    with ExitStack() as es:
        k = K(nc, es)
        X = k.buf("X", [128, 8, NT], F32)
        VEC = k.buf("VEC", [128, NV], F32)
        CST = k.buf("CST", [128, 512], F32)
        CB = k.buf("CB", [128, 256], BF16)
        k.wslots = [k.buf(f"wslot{i}", [128, SLOTE], BF16) for i in range(NSLOT)]
        PS = [Buf(es.enter_context(nc.psum_tensor(f"psum{i}", [128, 512], F32)), f"psum{i}") for i in range(8)]
        dsem_dd = es.enter_context(nc.semaphore("dsem_dd"))
        k.init_engines()
        k.psi = 0

        identf = CST.t[:, 0:128]
        trif = CST.t[:, 128:256]
        ustrf = CST.t[:, 256:384]
        onesf = CST.t[:, 384:512]
        identb = CB.t[:, 0:128]
        onesb = CB.t[:, 128:256]

        Xb = [k.sub(X, 5, f"c{m}t") for m in range(8)]

        def tix_of(col):
            return min(col // 512, 4)

        def XT(col):
            return [Xb[m][tix_of(col)] for m in range(8)]

        def X1(m, col):
            return Xb[m][tix_of(col)]

        def V(name, c=0, n=1):
            o = voff[name] + c
            return VEC.t[:, o:o + n]

        def ps():
            if k.dry:
                return PS[0]
            free = [b for b in PS if getattr(b, "last_read", 0) >= getattr(b, "last_write", 0)]
            if free:
                b = min(free, key=lambda b: getattr(b, "last_read", 0))
            else:
                b = min(PS, key=lambda b: getattr(b, "last_write", 0))
            k.opidx = getattr(k, "opidx", 0) + 1
            b.last_write = k.opidx
            return b

        def ACT(out, in_, func, R, Wr, **kw):
            k.op(k.act, lambda: nc.scalar.activation(out=out, in_=in_, func=func, **kw), R, Wr)

        def TT(out, in0, in1, op, R, Wr):
            k.op(k.dve, lambda: nc.vector.tensor_tensor(out=out, in0=in0, in1=in1, op=op), R, Wr)

        def STT(out, in0, scalar, in1, op0, op1, R, Wr):
            k.op(k.dve, lambda: nc.vector.scalar_tensor_tensor(out=out, in0=in0, scalar=scalar, in1=in1, op0=op0, op1=op1), R, Wr)

        def TS(out, in0, s1, s2, op0, op1, R, Wr):
            if op1 is None:
                k.op(k.dve, lambda: nc.vector.tensor_scalar(out=out, in0=in0, scalar1=s1, scalar2=None, op0=op0), R, Wr)
            else:
                k.op(k.dve, lambda: nc.vector.tensor_scalar(out=out, in0=in0, scalar1=s1, scalar2=s2, op0=op0, op1=op1), R, Wr)

        def CP(out, in_, R, Wr):
            k.op(k.dve, lambda: nc.vector.tensor_copy(out=out, in_=in_), R, Wr)

        def RSTD(rt, n, src_ap, srcB, scale):
            ACT(rt.t[:, :n], src_ap, AF.Ln, [srcB], [rt], scale=scale, bias=EPS)
            ACT(rt.t[:, :n], rt.t[:, :n], AF.Exp, [rt], [rt], scale=-0.5)

        def RCP(out, in_, R, Wr):
            k.op(k.dve, lambda: nc.vector.reciprocal(out=out, in_=in_), R, Wr)

        def MEMSET(out, val, Wr):
            k.op(k.dve, lambda: nc.vector.memset(out, val), (), Wr)

        def MM(ob, out, pairs, R, inc=True, first=True, final=True):
            n = len(pairs)
            for i, (l, r) in enumerate(pairs):
                last = i == n - 1
                k.op(k.pe, lambda l=l, r=r, i=i, last=last: nc.tensor.matmul(out, lhsT=l, rhs=r, start=(first and i == 0), stop=(final and last)),
                     R if i == 0 else (), [ob] if i == 0 else (), inc=(last and inc))

        def PTT(out, in0, in1, op, R, Wr):
            k.op(k.pool, lambda: nc.gpsimd.tensor_tensor(out=out, in0=in0, in1=in1, op=op), R, Wr)

        def TR(ob, out, ib, in_, ident, inc=True, extra=()):
            k.op(k.pe, lambda: nc.tensor.transpose(out, in_, ident), [ib] + list(extra), [ob], inc=inc)

        def wload(specs):
            j = k.wj
            k.wj += 1
            if k.dry:
                k.wplan.append(specs)
                return k.wslots[j % NSLOT]
            base = k.whold if k.whold is not None else j
            while k.wissued < min(len(k.wplan), base + NSLOT):
                jj = k.wissued
                slot = k.wslots[jj % NSLOT]
                for (src, kc, ncols, off) in k.wplan[jj]:
                    dst = slot.t[:, off:off + kc * ncols].rearrange("p (k c) -> p k c", k=kc)
                    k.dma(k.pool, dst, src.rearrange("(k p) c -> p k c", p=128), writes=[slot])
                k.wissued += 1
            return k.wslots[j % NSLOT]

        def wv(slot, off, kk, ncols):
            return slot.t[:, off + kk * ncols: off + (kk + 1) * ncols]

        def body():
            k.wj = 0
            k.psi = 0
            k.whold = None
            k.dma(k.sp, VEC.t[:, :], vecs_d[:, :], writes=[VEC])
            k.dma(k.sp, CST.t[:, :], cst_d[:, :], writes=[CST])
            CP(CB.t[:, 0:128], identf, [CST], [CB])
            CP(CB.t[:, 128:256], onesf, [CST], [CB])

            with k.phase():
                stg = [k.buf(f"stg{i}", [128, D], F32) for i in range(2)]
                for r in range(17):
                    st = stg[r % 2]
                    R = 128 if r < 16 else NSMP
                    src = xp[r * 128:(r + 1) * 128, :] if r < 16 else xs[:, :]
                    k.dma(k.sp, st.t[:R, :], src, writes=[st])
                    for half in range(2):
                        pb = ps()
                        for q in range(4):
                            c = half * 4 + q
                            TR(pb, pb.t[:, q * 128:q * 128 + R], st, st.t[:R, c * 128:(c + 1) * 128], identf[:R, :R], inc=(q == 3), extra=[CST])
                        src_v = pb.t[:, :].rearrange("p (a b) -> p a b", a=4)[:, :, :R]
                        dst_v = X.t[:, half * 4:half * 4 + 4, r * 128:r * 128 + R]
                        xw = [X1(m, r * 128) for m in range(half * 4, half * 4 + 4)]
                        if half == 0:
                            ACT(dst_v, src_v, AF.Copy, [pb], xw)
                        else:
                            CP(dst_v, src_v, [pb], xw)

            def rmsnorm(gname, dstB_fn, dst_fn, subs, sqs, rts):
                for ti, (so, do, n) in enumerate(subs):
                    sq = sqs[ti % len(sqs)]
                    rt = rts[ti % len(rts)]
                    ACT(sq.t[:, 0:8, :n], X.t[:, :, so:so + n], AF.Square, XT(so), [sq])
                    pb = ps()
                    MM(pb, pb.t[:, :n], [(onesb, sq.t[:, kk, :n]) for kk in range(8)], [sq, CB])
                    RSTD(rt, n, pb.t[:, :n], pb, 1.0 / D)
                    for kk in range(8):
                        STT(dst_fn(kk, do, n), X.t[:, kk, so:so + n], V(gname, kk), rt.t[:, :n], ALU.mult, ALU.mult, [X1(kk, so), rt, VEC], dstB_fn(do))

            def ffn(i, which):
                wg, wu, wd = W[f"w_ffn{which}_gate"], W[f"w_ffn{which}_up"], W[f"w_ffn{which}_down"]
                with k.phase():
                    XN = k.buf("XN", [128, 8, NT], BF16)
                    H = k.buf("H", [128, 11, NT], BF16)
                    sqs = [k.buf(f"sq{a}", [128, 8, 512], BF16) for a in range(2)]
                    rts = [k.buf(f"rt{a}", [128, 512], F32) for a in range(2)]
                    sgs = [k.buf(f"sg{a}", [128, 512], F32) for a in range(3)]
                    XNb = k.sub(XN, 5, "t")
                    Hb = k.sub(H, 5, "t")
                    sgi = [0]

                    def ffn_a(sl, fi, s, n):
                        pa = ps()
                        pbb = ps()
                        MM(pa, pa.t[:, :n], [(wv(sl, 0, kk, 128), XN.t[:, kk, s:s + n]) for kk in range(8)], [sl, XNb[tix_of(s)]])
                        MM(pbb, pbb.t[:, :n], [(wv(sl, 1024, kk, 128), XN.t[:, kk, s:s + n]) for kk in range(8)], [sl, XNb[tix_of(s)]])
                        sg = sgs[sgi[0] % 3]
                        sgi[0] += 1
                        ACT(sg.t[:, :n], pa.t[:, :n], AF.Silu, [pa], [sg])
                        TT(H.t[:, fi, s:s + n], pbb.t[:, :n], sg.t[:, :n], ALU.mult, [pbb, sg], [Hb[tix_of(s)]])

                    def ffn_w(f):
                        return wload([(wg[i, :, f * 128:(f + 1) * 128], 8, 128, 0), (wu[i, :, f * 128:(f + 1) * 128], 8, 128, 1024)])

                    for half in range(2):
                        fis = list(range(11))
                        if half == 0:
                            k.whold = k.wj
                            sls = [ffn_w(fi) for fi in range(4)]
                            for ti, (s, n) in enumerate(TILES):
                                rmsnorm(f"norm_ffn{which}{i}", lambda do: [XNb[tix_of(do)]], lambda kk, do, n: XN.t[:, kk, do:do + n], [(s, s, n)], [sqs[ti % 2]], [rts[ti % 2]])
                                for fi in range(4):
                                    ffn_a(sls[fi], fi, s, n)
                            k.whold = None
                            fis = list(range(4, 11))
                        for fi in fis:
                            sl = ffn_w(half * 11 + fi)
                            for (s, n) in TILES:
                                ffn_a(sl, fi, s, n)
                        for m in range(8):
                            r0 = half * 1408
                            sl = wload([(wd[i, r0:r0 + 1408, m * 128:(m + 1) * 128], 11, 128, 0)])
                            for (s, n) in TILES:
                                pc = ps()
                                MM(pc, pc.t[:, :n], [(wv(sl, 0, fi, 128), H.t[:, fi, s:s + n]) for fi in range(11)], [sl, Hb[tix_of(s)]])
                                STT(X.t[:, m, s:s + n], pc.t[:, :n], 0.5, X.t[:, m, s:s + n], ALU.mult, ALU.add, [pc, X1(m, s)], [X1(m, s)])

            def ple(i):
                with k.phase():
                    XN = k.buf("XN", [128, 8, NT], BF16)
                    PT = k.buf("PT", [128, 2, NT], BF16)
                    sqs = [k.buf(f"sq{a}", [128, 8, 512], BF16) for a in range(2)]
                    rts = [k.buf(f"rt{a}", [128, 512], F32) for a in range(2)]
                    sgs = [k.buf(f"sg{a}", [128, 512], F32) for a in range(3)]
                    t2s = [k.buf(f"t2{a}", [128, 512], F32) for a in range(2)]
                    stg = [k.buf(f"pstg{a}", [128, PLE], F32) for a in range(2)]
                    PTb = k.sub(PT, 5, "t")
                    for r in range(17):
                        st = stg[r % 2]
                        R = 128 if r < 16 else NSMP
                        src = pp[i, r * 128:(r + 1) * 128, :] if r < 16 else psm[i, :, :]
                        k.dma(k.sp, st.t[:R, :], src, writes=[st])
                        pb = ps()
                        for c in range(2):
                            TR(pb, pb.t[:, c * 128:c * 128 + R], st, st.t[:R, c * 128:(c + 1) * 128], identf[:R, :R], inc=(c == 1), extra=[CST])
                        ACT(PT.t[:, :, r * 128:r * 128 + R], pb.t[:, 0:256].rearrange("p (a b) -> p a b", a=2)[:, :, :R], AF.Copy, [pb], [PTb[tix_of(r * 128)]])
                    XNb = k.sub(XN, 5, "t")
                    ci = [0]

                    def ple_w(m):
                        return wload([(W["w_ple_gate"][i, :, m * 128:(m + 1) * 128], 8, 128, 0), (W["w_ple_proj"][i, :, m * 128:(m + 1) * 128], 2, 128, 1024)])

                    def ple_c(sl, m, s, n):
                        pg = ps()
                        pq = ps()
                        MM(pg, pg.t[:, :n], [(wv(sl, 0, kk, 128), XN.t[:, kk, s:s + n]) for kk in range(8)], [sl, XNb[tix_of(s)]])
                        MM(pq, pq.t[:, :n], [(wv(sl, 1024, kk, 128), PT.t[:, kk, s:s + n]) for kk in range(2)], [sl, PTb[tix_of(s)]])
                        sg = sgs[ci[0] % 3]
                        t2 = t2s[ci[0] % 2]
                        ci[0] += 1
                        ACT(sg.t[:, :n], pg.t[:, :n], AF.Sigmoid, [pg], [sg])
                        TT(t2.t[:, :n], pq.t[:, :n], sg.t[:, :n], ALU.mult, [pq, sg], [t2])
                        TT(X.t[:, m, s:s + n], X.t[:, m, s:s + n], t2.t[:, :n], ALU.add, [X1(m, s), t2], [X1(m, s)])

                    k.whold = k.wj
                    sls = [ple_w(m) for m in range(4)]
                    for ti, (s, n) in enumerate(TILES):
                        rmsnorm(f"norm_ple{i}", lambda do: [XNb[tix_of(do)]], lambda kk, do, n: XN.t[:, kk, do:do + n], [(s, s, n)], [sqs[ti % 2]], [rts[ti % 2]])
                        for m in range(4):
                            ple_c(sls[m], m, s, n)
                    k.whold = None
                    for m in range(4, 8):
                        sl = ple_w(m)
                        for (s, n) in TILES:
                            ple_c(sl, m, s, n)

            def conv_mixer(i, j):
                with k.phase():
                    XN = k.buf("XN", [128, 8, NT], BF16)
                    Vb = k.buf("Vb", [128, 8, NT], BF16)
                    rts = [k.buf(f"rt{a}", [128, 512], F32) for a in range(2)]
                    sgs = [k.buf(f"sg{a}", [128, 512], F32) for a in range(3)]
                    UL = k.buf("ul", [128, 8, 32], F32)
                    USN = k.buf("usn", [128, 8, NSMP], F32)
                    XNb = k.sub(XN, 5, "t")
                    Vbb = k.sub(Vb, 8, "c")
                    with k.phase():
                        sqs = [k.buf(f"sq{a}", [128, 8, 512], BF16) for a in range(2)]
                        rmsnorm(f"norm_mix{i}", lambda do: [XNb[tix_of(do)]], lambda kk, do, n: XN.t[:, kk, do:do + n], [(s, s, n) for s, n in TILES], sqs, rts)
                    c1 = k.phase()
                    c1.__enter__()
                    UE = [k.buf(f"ue{a}", [128, 30 + T], BF16) for a in range(2)]
                    US = [k.buf(f"us{a}", [128, NSMP, CK], F32) for a in range(2)]
                    DG = [k.buf(f"dg{a}", [128, CK, 128], BF16) for a in range(2)]
                    STSS = [k.buf(f"sts{a}", [128, 4, 128], F32) for a in range(2)]
                    tmp = k.buf("ctmp", [128, NSMP, CK], F32)
                    red = k.buf("cred", [128, NSMP], F32)
                    if not k.dry:
                        nc.sync.dma_start(out=ocs[j, :, 0:29, :], in_=stc[j, :, :].rearrange("(b r) c -> b r c", r=30)[:, 1:30, :]).then_inc(dsem_dd, 16)
                        k.ddcnt += 16
                    sgi = 0
                    for c in range(8):
                        sl = wload([(W["cm_w_in"][j, :, c * 128:(c + 1) * 128], 8, 128, 0), (W["cm_w_in"][j, :, D + c * 128:D + (c + 1) * 128], 8, 128, 1024)])
                        ue = UE[c % 2]
                        us = US[c % 2]
                        dg = DG[c % 2]
                        MEMSET(ue.t[:, 0:30], 0.0, [ue])
                        TT(dg.t[:, :, :], identb.unsqueeze(1).to_broadcast([128, CK, 128]), V(f"cdw{j}", c * CK, CK).unsqueeze(2).to_broadcast([128, CK, 128]), ALU.mult, [CB, VEC], [dg])
                        STS = STSS[c % 2]
                        k.dma(k.sp, STS.t[:, 0:3, :], stc[j, 0:384, c * 128:(c + 1) * 128].rearrange("(a p) c -> p a c", p=128), writes=[STS])
                        k.dma(k.sp, STS.t[:96, 3, :], stc[j, 384:480, c * 128:(c + 1) * 128], writes=[STS])
                        pb = ps()
                        for a in range(4):
                            R = 128 if a < 3 else 96
                            TR(pb, pb.t[:, a * 128:a * 128 + R], STS, STS.t[:R, a, :], identf[:R, :R], inc=(a == 3), extra=[CST])
                        CP(us.t[:, :, 0:30], pb.t[:, 0:480].rearrange("p (b r) -> p b r", r=30), [pb], [us])
                        for (s, n) in TILES:
                            pa = ps()
                            pg = ps()
                            MM(pa, pa.t[:, :n], [(wv(sl, 0, kk, 128), XN.t[:, kk, s:s + n]) for kk in range(8)], [sl, XNb[tix_of(s)]])
                            MM(pg, pg.t[:, :n], [(wv(sl, 1024, kk, 128), XN.t[:, kk, s:s + n]) for kk in range(8)], [sl, XNb[tix_of(s)]])
                            sg = sgs[sgi % 3]
                            sgi += 1
                            ACT(sg.t[:, :n], pg.t[:, :n], AF.Sigmoid, [pg, VEC], [sg], bias=V(f"cbin{j}", 8 + c))
                            if s < T:
                                STT(ue.t[:, 30 + s:30 + s + n], pa.t[:, :n], V(f"cbin{j}", c), sg.t[:, :n], ALU.add, ALU.mult, [pa, sg, VEC], [ue])
                                if s + n == T:
                                    STT(UL.t[:, c, :], pa.t[:, n - 32:n], V(f"cbin{j}", c), sg.t[:, n - 32:n], ALU.add, ALU.mult, [pa, sg, VEC], [UL])
                            else:
                                STT(us.t[:, :, 30], pa.t[:, :n], V(f"cbin{j}", c), sg.t[:, :n], ALU.add, ALU.mult, [pa, sg, VEC], [us])
                        for (s, n) in TILES[:4]:
                            pv = ps()
                            MM(pv, pv.t[:, :n], [(dg.t[:, kk, :], ue.t[:, s + kk:s + kk + n]) for kk in range(CK)], [dg, ue])
                            ACT(Vb.t[:, c, s:s + n], pv.t[:, :n], AF.Identity, [pv, VEC], [Vbb[c]], bias=V(f"cdwb{j}", c))
                        TT(tmp.t[:, :, :], us.t[:, :, :], V(f"cdw{j}", c * CK, CK).unsqueeze(1).to_broadcast([128, NSMP, CK]), ALU.mult, [us, VEC], [tmp])
                        k.op(k.dve, lambda: nc.vector.tensor_reduce(out=red.t[:, :], in_=tmp.t[:, :, :], axis=AX.X, op=ALU.add), [tmp], [red])
                        TS(Vb.t[:, c, T:NT], red.t[:, :], V(f"cdwb{j}", c), None, ALU.add, None, [red, VEC], [Vbb[c]])
                        CP(USN.t[:, c, :], us.t[:, :, 30], [us], [USN])
                    c1.__exit__(None, None, None)
                    c2 = k.phase()
                    c2.__enter__()
                    sqs = [k.buf(f"sq{a}", [128, 8, 512], BF16) for a in range(2)]
                    m1s = [k.buf(f"m1{a}", [128, 512], F32) for a in range(2)]
                    m2s = [k.buf(f"m2{a}", [128, 512], F32) for a in range(2)]
                    dts_ = [k.buf(f"dt{a}", [128, 512], F32) for a in range(2)]
                    def ln_stats(ti):
                        s, n = TILES[ti]
                        sq = sqs[ti % 2]
                        ACT(sq.t[:, :, :n], Vb.t[:, :, s:s + n], AF.Square, Vbb, [sq])
                        p1 = ps()
                        p2 = ps()
                        MM(p1, p1.t[:, :n], [(onesb, Vb.t[:, kk, s:s + n]) for kk in range(8)], Vbb + [CB])
                        MM(p2, p2.t[:, :n], [(onesb, sq.t[:, kk, :n]) for kk in range(8)], [sq, CB])
                        return p1, p2

                    def ln_norm(ti, p1, p2):
                        s, n = TILES[ti]
                        rt = rts[ti % 2]
                        m1 = m1s[ti % 2]
                        m2 = m2s[ti % 2]
                        ACT(m1.t[:, :n], p1.t[:, :n], AF.Copy, [p1], [m1], scale=1.0 / D)
                        TT(m2.t[:, :n], m1.t[:, :n], m1.t[:, :n], ALU.mult, [m1], [m2])
                        STT(m2.t[:, :n], p2.t[:, :n], 1.0 / D, m2.t[:, :n], ALU.mult, ALU.subtract, [p2, m2], [m2])
                        RSTD(rt, n, m2.t[:, :n], m2, 1.0)
                        for kk in range(8):
                            dtb = dts_[kk % 2]
                            TT(dtb.t[:, :n], Vb.t[:, kk, s:s + n], m1.t[:, :n], ALU.subtract, [Vbb[kk], m1], [dtb])
                            TT(dtb.t[:, :n], dtb.t[:, :n], rt.t[:, :n], ALU.mult, [dtb, rt], [dtb])
                            ACT(XN.t[:, kk, s:s + n], dtb.t[:, :n], AF.Silu, [dtb, VEC], [XNb[tix_of(s)]], scale=V(f"clng{j}", kk), bias=V(f"clnb{j}", kk))

                    def co_w(m):
                        return wload([(W["cm_w_out"][j, :, m * 128:(m + 1) * 128], 8, 128, 0)])

                    def co_c(sl, m, s, n):
                        pc = ps()
                        MM(pc, pc.t[:, :n], [(wv(sl, 0, kk, 128), XN.t[:, kk, s:s + n]) for kk in range(8)], [sl, XNb[tix_of(s)]])
                        STT(X.t[:, m, s:s + n], pc.t[:, :n], V(f"cbout{j}", m), X.t[:, m, s:s + n], ALU.add, ALU.add, [pc, X1(m, s), VEC], [X1(m, s)])

                    k.whold = k.wj
                    sls = [co_w(m) for m in range(4)]
                    st_next = ln_stats(0)
                    for ti, (s, n) in enumerate(TILES):
                        st_cur = st_next
                        if ti + 1 < len(TILES):
                            st_next = ln_stats(ti + 1)
                        ln_norm(ti, *st_cur)
                        for m in range(4):
                            co_c(sls[m], m, s, n)
                    k.whold = None
                    for m in range(4, 8):
                        sl = co_w(m)
                        for (s, n) in TILES:
                            co_c(sl, m, s, n)
                    c2.__exit__(None, None, None)
                    OST = k.buf("ost", [32, D], F32)
                    for half in range(2):
                        pb = ps()
                        for q in range(4):
                            c = half * 4 + q
                            TR(pb, pb.t[:32, q * 128:(q + 1) * 128], UL, UL.t[:, c, :], identf, inc=(q == 3), extra=[CST])
                        CP(OST.t[:32, half * 512:(half + 1) * 512], pb.t[:32, :], [pb], [OST])
                    k.dma(k.sp, ocp[j, :, :], OST.t[2:32, :], reads=[OST], final=True)
                    OS2 = k.buf("os2", [NSMP, D], F32)
                    for half in range(2):
                        pb = ps()
                        for q in range(4):
                            c = half * 4 + q
                            TR(pb, pb.t[:NSMP, q * 128:(q + 1) * 128], USN, USN.t[:, c, :], identf, inc=(q == 3), extra=[CST])
                        CP(OS2.t[:, half * 512:(half + 1) * 512], pb.t[:NSMP, :], [pb], [OS2])
                    k.dma(k.sp, ocs[j, :, 29, :], OS2.t[:, :], reads=[OS2], final=True)

            def ssd_mixer(i, j):
                win, wout = W["ssd_w_in"], W["ssd_w_out"]
                with k.phase():
                    NL = 512 + NSMP
                    HN = k.buf("HN", [128, 8, NL], BF16)
                    XB = k.buf("XB", [128, 32, NL], BF16)
                    YT = k.buf("YT", [128, 16, NL], BF16)
                    XBb = k.sub(XB, 32, "c")
                    YTb = k.sub(YT, 16, "c")
                    PRE = [k.buf(f"pre{a}", [128, 515], BF16) for a in range(2)]
                    DGS = [k.buf(f"dgs{a}", [128, 4, 128], BF16) for a in range(2)]
                    HIST = k.buf("hist", [128, 32, 3], F32)
                    HISTb = k.sub(HIST, 32, "c")
                    GS = [k.buf(f"gsq{a}", [128, 2, 512], BF16) for a in range(2)]
                    rts = [k.buf(f"rt{a}", [128, 512], F32) for a in range(2)]
                    sgs = rts
                    ABC = k.buf("abc", [128, 32], F32)
                    S = k.buf("S", [128, DIN], F32)
                    SB = k.buf("SB", [128, DIN], BF16)
                    Sb = k.sub(S, 4, "q")
                    SBb = k.sub(SB, 4, "q")
                    MEMSET(HIST.t[:, :, :], 0.0, HISTb)
                    MEMSET(S.t[:, :], 0.0, Sb)
                    MEMSET(SB.t[:, :], 0.0, SBb)
                    ACT(ABC.t[:, :], V(f"alogbc{j}", 0, 32), AF.Exp, [VEC], [ABC])
                    TS(ABC.t[:, :], ABC.t[:, :], -1.0, None, ALU.mult, None, [ABC], [ABC])
                    ci = 0
                    gi = 0
                    for tix in range(4):
                        s0 = tix * 512
                        smp = tix == 3
                        lsubs = [(0, s0, 512)] + ([(512, T, NSMP)] if smp else [])
                        for (l, g, n) in lsubs:
                            pb = ps()
                            for rnd in range(4):
                                sq = GS[gi % 2]
                                gi += 1
                                ACT(sq.t[:, 0:2, :n], X.t[:, 2 * rnd:2 * rnd + 2, g:g + n], AF.Square, [X1(2 * rnd, g), X1(2 * rnd + 1, g)], [sq])
                                MM(pb, pb.t[:, :n], [(onesb, sq.t[:, e, :n]) for e in range(2)], [sq, CB], first=(rnd == 0), final=(rnd == 3))
                            rt = rts[0]
                            RSTD(rt, n, pb.t[:, :n], pb, 1.0 / D)
                            for kk in range(8):
                                STT(HN.t[:, kk, l:l + n], X.t[:, kk, g:g + n], V(f"norm_mix{i}", kk), rt.t[:, :n], ALU.mult, ALU.mult, [X1(kk, g), rt, VEC], [HN])
                        if smp:
                            xph = k.phase()
                            xph.__enter__()
                            STXS = [k.buf(f"stx{a}", [48, 1024], F32) for a in range(2)]
                            STT_ = k.buf("stT", [128, 32, NSMP, 3], F32)
                            NEWP = k.buf("newp", [128, 32, NSMP], F32)
                            if not k.dry:
                                nc.sync.dma_start(out=oxs[j, :, 0:2, :], in_=stsc[j, :, :].rearrange("(b r) c -> b r c", r=3)[:, 1:3, :]).then_inc(dsem_dd, 16)
                                k.ddcnt += 16
                            for g4 in range(4):
                                STX = STXS[g4 % 2]
                                k.dma(k.sp, STX.t[:, :], stsc[j, :, g4 * 1024:(g4 + 1) * 1024], writes=[STX])
                                pb = ps()
                                for q in range(8):
                                    TR(pb, pb.t[:, q * 48:(q + 1) * 48], STX, STX.t[:, q * 128:(q + 1) * 128], identf[:48, :48], inc=(q == 7), extra=[CST])
                                CP(STT_.t[:, g4 * 8:(g4 + 1) * 8, :, :], pb.t[:, 0:384].rearrange("p (c b r) -> p c b r", c=8, b=NSMP), [pb], [STT_])
                            ctm = k.buf("ctm", [128, NSMP, 3], F32)
                            cr = k.buf("cr", [128, NSMP], F32)
                        sls = {}

                        def emit_proj(cc):
                            it, e = divmod(cc, 2)
                            if e == 0:
                                sls[it] = wload([(win[j, :, DIN + it * 256: DIN + (it + 1) * 256], 8, 256, 0)])
                            sl = sls[it]
                            lw = [sl.t[:, kk * 256 + e * 128: kk * 256 + (e + 1) * 128] for kk in range(8)]
                            pa = ps()
                            MM(pa, pa.t[:, :512], [(lw[kk], HN.t[:, kk, 0:512]) for kk in range(8)], [sl, HN])
                            pq = None
                            if smp:
                                pq = ps()
                                MM(pq, pq.t[:, :NSMP], [(lw[kk], HN.t[:, kk, 512:NL]) for kk in range(8)], [sl, HN])
                            return pa, pq

                        stA = {}
                        stB = {}
                        stC = {}

                        def stage_B(cc):
                            nonlocal ci
                            pa, pq = stA.pop(cc)
                            pre = PRE[ci % 2]
                            dgs = DGS[ci % 2]
                            ci += 1
                            TT(dgs.t[:, :, :], identb.unsqueeze(1).to_broadcast([128, 4, 128]), V(f"scw{j}", cc * 4, 4).unsqueeze(2).to_broadcast([128, 4, 128]), ALU.mult, [CB, VEC], [dgs])
                            CP(pre.t[:, 0:3], HIST.t[:, cc, :], [HISTb[cc]], [pre])
                            ACT(pre.t[:, 3:515], pa.t[:, :512], AF.Copy, [pa], [pre])
                            CP(HIST.t[:, cc, :], pa.t[:, 509:512], [pa], [HISTb[cc]])
                            if smp:
                                CP(NEWP.t[:, cc, :], pq.t[:, :NSMP], [pq], [NEWP])
                                TT(ctm.t[:, :, :], STT_.t[:, cc, :, :], V(f"scw{j}", cc * 4, 3).unsqueeze(1).to_broadcast([128, NSMP, 3]), ALU.mult, [STT_, VEC], [ctm])
                                k.op(k.dve, lambda: nc.vector.tensor_reduce(out=cr.t[:, :], in_=ctm.t[:, :, :], axis=AX.X, op=ALU.add), [ctm], [cr])
                                STT(cr.t[:, :], pq.t[:, :NSMP], V(f"scw{j}", cc * 4 + 3), cr.t[:, :], ALU.mult, ALU.add, [pq, cr, VEC], [cr])
                                ACT(XB.t[:, cc, 512:NL], cr.t[:, :], AF.Silu, [cr, VEC], [XBb[cc]], bias=V(f"scb{j}", cc))
                            stB[cc] = (pre, dgs)

                        def stage_C(cc):
                            pre, dgs = stB.pop(cc)
                            pc = ps()
                            MM(pc, pc.t[:, :512], [(dgs.t[:, kk, :], pre.t[:, kk:kk + 512]) for kk in range(4)], [dgs, pre])
                            stC[cc] = pc

                        def stage_D(cc):
                            pc = stC.pop(cc)
                            ACT(XB.t[:, cc, 0:512], pc.t[:, :512], AF.Silu, [pc, VEC], [XBb[cc]], bias=V(f"scb{j}", cc))

                        for it_ in range(-2, 33):
                            if 0 <= it_ + 2 < 32:
                                stA[it_ + 2] = emit_proj(it_ + 2)
                            if 0 <= it_ + 1 < 32:
                                stage_B(it_ + 1)
                            if 0 <= it_ < 32:
                                stage_C(it_)
                            if 0 <= it_ - 1 < 32:
                                stage_D(it_ - 1)
                        if smp:
                            OX2 = k.buf("ox2", [NSMP, 1024], F32)
                            for g4 in range(4):
                                for hf in range(2):
                                    pb = ps()
                                    for q in range(4):
                                        cc = g4 * 8 + hf * 4 + q
                                        TR(pb, pb.t[:NSMP, q * 128:(q + 1) * 128], NEWP, NEWP.t[:, cc, :], identf, inc=(q == 3), extra=[CST])
                                    CP(OX2.t[:, hf * 512:(hf + 1) * 512], pb.t[:NSMP, :], [pb], [OX2])
                                k.dma(k.sp, oxs[j, :, 2, g4 * 1024:(g4 + 1) * 1024], OX2.t[:, :], reads=[OX2], final=True)
                            xph.__exit__(None, None, None)
                        k.mark(f"  L{i} t{tix} xBC+conv end")
                        sld = wload([(win[j, :, DIN + CONVD: DIN + CONVD + 32], 8, 32, 0)])
                        with k.phase():
                            Rg = [k.buf(f"Rg{a}", [128, 4, 128], F32) for a in range(3)]
                            Lg = [k.buf(f"Lg{a}", [128, 4, 128], BF16) for a in range(3)]
                            MTg = [k.buf(f"MT{a}", [128, 4, 128], BF16) for a in range(3)]
                            CBM = k.buf("cbm", [128, 8, 128], BF16)
                            CBMb = k.sub(CBM, 8, "g")
                            XDT = k.buf("xdt", [128, DIN], BF16)
                            XDD = k.buf("xdd", [128, DIN], BF16)
                            XDTb = k.sub(XDT, 2, "h")
                            XDDb = k.sub(XDD, 2, "h")
                            BTM = k.buf("btm", [128, 1024], BF16)
                            YTM = k.buf("ytm", [128, DIN], BF16)
                            YTMb = k.sub(YTM, 4, "q")
                            T1s = [k.buf(f"t1{a}", [128, 512], BF16) for a in range(1)]
                            sms = [{nm: k.buf(nm + str(a), [128, 32], F32) for nm in ("dt", "a", "acs", "ea", "cd", "dd", "dec", "dtd", "e1")} for a in range(2)]

                            def prologue_stage(q, st):
                                sm = sms[q % 2]
                                lo = q * 128
                                if st == 0:
                                    pd = ps()
                                    MM(pd, pd.t[:, 0:32], [(HN.t[:, kk, lo:lo + 128], sld.t[:, kk * 32:(kk + 1) * 32]) for kk in range(8)], [sld, HN])
                                    TT(sm["e1"].t[:, :], pd.t[:, 0:32], V(f"dtbbc{j}", 0, 32), ALU.add, [pd, VEC], [sm["e1"]])
                                elif st == 1:
                                    ACT(sm["e1"].t[:, :], sm["e1"].t[:, :], AF.Exp, [sm["e1"]], [sm["e1"]])
                                    ACT(sm["dt"].t[:, :], sm["e1"].t[:, :], AF.Ln, [sm["e1"]], [sm["dt"]], bias=1.0)
                                elif st == 2:
                                    TT(sm["a"].t[:, :], sm["dt"].t[:, :], ABC.t[:, :], ALU.mult, [sm["dt"], ABC], [sm["a"]])
                                elif st == 3:
                                    pcs = ps()
                                    sm["pcs"] = pcs
                                    MM(pcs, pcs.t[:, 0:32], [(trif, sm["a"].t[:, :])], [CST, sm["a"]], inc=False)
                                    MM(pcs, pcs.t[:, 32:64], [(onesf, sm["a"].t[:, :])], [CST, sm["a"]])
                                elif st == 4:
                                    pcs = sm["pcs"]
                                    ACT(sm["acs"].t[:, :], pcs.t[:, 0:32], AF.Copy, [pcs], [sm["acs"]])
                                    ACT(sm["ea"].t[:, :], pcs.t[:, 0:32], AF.Exp, [pcs], [sm["ea"]])
                                    ACT(sm["cd"].t[:, :], pcs.t[:, 32:64], AF.Exp, [pcs], [sm["cd"]])
                                elif st == 5:
                                    pcs = sm["pcs"]
                                    TT(sm["dd"].t[:, :], pcs.t[:, 32:64], sm["acs"].t[:, :], ALU.subtract, [pcs, sm["acs"]], [sm["dd"]])
                                elif st == 6:
                                    ACT(sm["dec"].t[:, :], sm["dd"].t[:, :], AF.Exp, [sm["dd"]], [sm["dec"]])
                                elif st == 7:
                                    TT(sm["dtd"].t[:, :], sm["dt"].t[:, :], sm["dec"].t[:, :], ALU.mult, [sm["dt"], sm["dec"]], [sm["dtd"]])

                            def prologue(q):
                                for st in range(8):
                                    prologue_stage(q, st)

                            prologue(0)
                            for q in range(4):
                                sm = sms[q % 2]
                                lo = q * 128
                                pts = []
                                for hb in range(2):
                                    pt = ps()
                                    ptb = pt.t[:, :].bitcast(BF16)
                                    for e in range(8):
                                        hh = hb * 8 + e
                                        TR(pt, ptb[:, e * 128:(e + 1) * 128], XBb[hh], XB.t[:, hh, lo:lo + 128], identb, inc=(e == 7), extra=[CB])
                                    pts.append((pt, ptb))
                                ptB = ps()
                                ptBb = ptB.t[:, :].bitcast(BF16)
                                for g in range(8):
                                    TR(ptB, ptBb[:, g * 128:(g + 1) * 128], XBb[16 + g], XB.t[:, 16 + g, lo:lo + 128], identb, inc=(g == 7), extra=[CB])
                                pcbs = []
                                for hf in range(2):
                                    pcb = ps()
                                    for e in range(4):
                                        g = hf * 4 + e
                                        MM(pcb, pcb.t[:, e * 128:(e + 1) * 128], [(XB.t[:, 16 + g, lo:lo + 128], XB.t[:, 24 + g, lo:lo + 128])], [XBb[16 + g], XBb[24 + g]], inc=(e == 3))
                                    pcbs.append(pcb)
                                rg_of = {}

                                def emit_R(g):
                                    rg = Rg[g % 3]
                                    rg_of[g] = rg
                                    PTT(rg.t[:, :, :], sm["a"].t[:, g * 4:(g + 1) * 4].unsqueeze(2).to_broadcast([128, 4, 128]), trif.unsqueeze(1).to_broadcast([128, 4, 128]), ALU.mult, [sm["a"], CST], [rg])

                                for g in range(3):
                                    emit_R(g)
                                for hb in range(2):
                                    pt, ptb = pts[hb]
                                    pv3 = ptb.rearrange("p (h d) -> p h d", d=64)
                                    TT(XDT.t[:, hb * 1024:(hb + 1) * 1024].rearrange("p (h d) -> p h d", d=64), pv3, sm["dt"].t[:, hb * 16:(hb + 1) * 16].unsqueeze(2).to_broadcast([128, 16, 64]), ALU.mult, [pt, sm["dt"]], [XDTb[hb]])
                                    TT(XDD.t[:, hb * 1024:(hb + 1) * 1024].rearrange("p (h d) -> p h d", d=64), pv3, sm["dtd"].t[:, hb * 16:(hb + 1) * 16].unsqueeze(2).to_broadcast([128, 16, 64]), ALU.mult, [pt, sm["dtd"]], [XDDb[hb]])
                                ACT(BTM.t[:, :], ptBb, AF.Copy, [ptB], [BTM])
                                for hf in range(2):
                                    TT(CBM.t[:, hf * 4:(hf + 1) * 4, :], pcbs[hf].t[:, :].rearrange("p (a b) -> p a b", a=4), trif.unsqueeze(1).to_broadcast([128, 4, 128]), ALU.mult, [pcbs[hf], CST], CBMb[hf * 4:(hf + 1) * 4])
                                psegs = {}

                                def emit_seg(g):
                                    rg = rg_of[g]
                                    pseg = ps()
                                    MM(pseg, pseg.t[:, :], [(ustrf, rg.t[:, :, :].rearrange("p a b -> p (a b)"))], [CST, rg])
                                    if g + 3 < 8:
                                        emit_R(g + 3)
                                    lg = Lg[g % 3]
                                    ACT(lg.t[:, :, :].rearrange("p a b -> p (a b)"), pseg.t[:, :], AF.Exp, [pseg], [lg])
                                    mt = MTg[g % 3]
                                    TT(mt.t[:, :, :], lg.t[:, :, :], CBM.t[:, g, :].unsqueeze(1).to_broadcast([128, 4, 128]), ALU.mult, [lg, CBMb[g]], [mt])
                                    return mt

                                mts = {0: emit_seg(0), 1: emit_seg(1)}
                                pyd = pyo = None
                                for g in range(8):
                                    b4, e2_ = divmod(g, 2)
                                    if q + 1 < 4:
                                        prologue_stage(q + 1, g)
                                    if g + 2 < 8:
                                        mts[g + 2] = emit_seg(g + 2)
                                    if e2_ == 0:
                                        pyd = ps()
                                        pyo = ps()
                                    mt = mts[g]
                                    for r4 in range(4):
                                        h = g * 4 + r4
                                        e = e2_ * 4 + r4
                                        MM(pyd, pyd.t[:, e * 64:(e + 1) * 64], [(mt.t[:, r4, :], XDT.t[:, h * 64:(h + 1) * 64])], [mt, XDTb[h // 16]], inc=(r4 == 3))
                                    MM(pyo, pyo.t[:, e2_ * 256:(e2_ + 1) * 256], [(XB.t[:, 24 + g, lo:lo + 128], SB.t[:, g * 256:(g + 1) * 256])], [XBb[24 + g], SBb[b4]])
                                    if e2_ == 1:
                                        T1 = T1s[0]
                                        TT(T1.t[:, :].rearrange("p (h d) -> p h d", d=64), pyo.t[:, :].rearrange("p (h d) -> p h d", d=64), sm["ea"].t[:, b4 * 8:(b4 + 1) * 8].unsqueeze(2).to_broadcast([128, 8, 64]), ALU.mult, [pyo, sm["ea"]], [T1])
                                        TT(YTM.t[:, b4 * 512:(b4 + 1) * 512], pyd.t[:, :], T1.t[:, :], ALU.add, [pyd, T1], [YTMb[b4]])
                                for b4 in range(4):
                                    pst = ps()
                                    for e in range(2):
                                        g = b4 * 2 + e
                                        MM(pst, pst.t[:, e * 256:(e + 1) * 256], [(BTM.t[:, g * 128:(g + 1) * 128], XDD.t[:, g * 256:(g + 1) * 256])], [BTM, XDDb[g // 4]], inc=(e == 1))
                                    sv = S.t[:, b4 * 512:(b4 + 1) * 512]
                                    TT(sv.rearrange("p (h d) -> p h d", d=64), sv.rearrange("p (h d) -> p h d", d=64), sm["cd"].t[:, b4 * 8:(b4 + 1) * 8].unsqueeze(2).to_broadcast([128, 8, 64]), ALU.mult, [Sb[b4], sm["cd"]], [Sb[b4]])
                                    TT(sv, pst.t[:, :], sv, ALU.add, [pst, Sb[b4]], [Sb[b4]])
                                    ACT(SB.t[:, b4 * 512:(b4 + 1) * 512], sv, AF.Copy, [Sb[b4]], [SBb[b4]])
                                for hb in range(2):
                                    pt = ps()
                                    ptb = pt.t[:, :].bitcast(BF16)
                                    for e in range(8):
                                        hh = hb * 8 + e
                                        TR(pt, ptb[:, e * 128:(e + 1) * 128], YTMb[hh // 4], YTM.t[:, hh * 128:(hh + 1) * 128], identb, inc=(e == 7), extra=[CB])
                                    ACT(YT.t[:, hb * 8:(hb + 1) * 8, lo:lo + 128], ptb.rearrange("p (a b) -> p a b", a=8), AF.Copy, [pt], YTb[hb * 8:(hb + 1) * 8])
                                k.mark(f"    L{i} t{tix} chunk{q} end")
                        if smp:
                            with k.phase():
                                SO = k.buf("so", [128, 16, 128], F32)
                                for g4 in range(4):
                                    pb = ps()
                                    for q in range(4):
                                        blk = g4 * 4 + q
                                        TR(pb, pb.t[:, q * 128:(q + 1) * 128], Sb[g4], S.t[:, blk * 128:(blk + 1) * 128], identf, inc=(q == 3), extra=[CST])
                                    CP(SO.t[:, g4 * 4:(g4 + 1) * 4, :], pb.t[:, :].rearrange("p (a b) -> p a b", a=4), [pb], [SO])
                                k.dma(k.sp, osp[j, :, :].rearrange("(a p) n -> p a n", p=128), SO.t[:, :, :], reads=[SO], final=True)
                                OXS = [k.buf(f"ox{a}", [3, 1024], F32) for a in range(2)]
                                for g4 in range(4):
                                    OX = OXS[g4 % 2]
                                    for hf in range(2):
                                        pb = ps()
                                        for q in range(4):
                                            cc = g4 * 8 + hf * 4 + q
                                            TR(pb, pb.t[:3, q * 128:(q + 1) * 128], HISTb[cc], HIST.t[:, cc, :], identf, inc=(q == 3), extra=[CST])
                                        CP(OX.t[:, hf * 512:(hf + 1) * 512], pb.t[:3, :], [pb], [OX])
                                    k.dma(k.sp, oxp[j, :, g4 * 1024:(g4 + 1) * 1024], OX.t[:, :], reads=[OX], final=True)
                            with k.phase():
                                DBC = k.buf("dbc", [128, 16, 32], F32)
                                XDS = k.buf("xds", [128, 16, NSMP], F32)
                                YS = k.buf("ysm", [128, 16, NSMP], F32)
                                with k.phase():
                                    SEL = k.buf("sel", [32, 2048], F32)
                                    k.dma(k.sp, SEL.t[:, :], sel_d[:, :], writes=[SEL])
                                    DTF = k.buf("dtf", [32, 32], F32)
                                    e2 = k.buf("e2", [32, NSMP], F32)
                                    a32 = k.buf("a32", [32, 1], F32)
                                    ACT(a32.t[:, :], V(f"alog32{j}")[:32, :], AF.Exp, [VEC], [a32])
                                    TS(a32.t[:, :], a32.t[:, :], -1.0, None, ALU.mult, None, [a32], [a32])
                                    pd = ps()
                                    MM(pd, pd.t[:32, 0:NSMP], [(sld.t[:, kk * 32:(kk + 1) * 32], HN.t[:, kk, 512:NL]) for kk in range(8)], [sld, HN])
                                    ACT(e2.t[:, :], pd.t[:32, 0:NSMP], AF.Exp, [pd, VEC], [e2], bias=V(f"dtb32{j}")[:32, :])
                                    ACT(DTF.t[:, 0:16], e2.t[:, :], AF.Ln, [e2], [DTF], bias=1.0)
                                    ACT(DTF.t[:, 16:32], DTF.t[:, 0:16], AF.Exp, [DTF, a32], [DTF], scale=a32.t[:, :])
                                    pbq = ps()
                                    for hh in range(16):
                                        MM(pbq, pbq.t[:, hh * 32:(hh + 1) * 32], [(SEL.t[:, hh * 128:(hh + 1) * 128], DTF.t[:, :])], [SEL, DTF], inc=(hh == 15))
                                    CP(DBC.t[:, :, :], pbq.t[:, :].rearrange("p (a b) -> p a b", a=16), [pbq], [DBC])
                                TT(XDS.t[:, :, :], XB.t[:, 0:16, 512:NL], DBC.t[:, :, 0:16], ALU.mult, XBb[0:16] + [DBC], [XDS])
                                DB = k.buf("dgb", [128, 8, 128], BF16)
                                DC = k.buf("dgc", [128, 8, 128], BF16)
                                CS = k.buf("cbs", [128, 8, 128], BF16)
                                SS = [k.buf(f"ss{a}", [128, 16, 128], F32) for a in range(2)]
                                ssb = [k.sub(SS[a], 16, "h") for a in range(2)]
                                TA = k.buf("ta", [128, 16, 128], BF16)

                                def s_load(b):
                                    k.dma(k.sp, SS[b % 2].t[:, :, :], stss[j, b, :, :].rearrange("(a p) n -> p a n", p=128), writes=ssb[b % 2])

                                def s_prep(b):
                                    PTT(DB.t[:, :, :], identb.unsqueeze(1).to_broadcast([128, 8, 128]), XB.t[:, 16:24, 512 + b:512 + b + 1].to_broadcast([128, 8, 128]), ALU.mult, [CB] + XBb[16:24], [DB])
                                    PTT(DC.t[:, :, :], identb.unsqueeze(1).to_broadcast([128, 8, 128]), XB.t[:, 24:32, 512 + b:512 + b + 1].to_broadcast([128, 8, 128]), ALU.mult, [CB] + XBb[24:32], [DC])
                                    pbc = [ps() for _ in range(4)]
                                    for a in range(2):
                                        MM(pbc[a], pbc[a].t[:, :], [(onesb, DB.t[:, a * 4:(a + 1) * 4, :].rearrange("p a b -> p (a b)"))], [CB, DB])
                                    for a in range(2):
                                        MM(pbc[2 + a], pbc[2 + a].t[:, :], [(onesb, DC.t[:, a * 4:(a + 1) * 4, :].rearrange("p a b -> p (a b)"))], [CB, DC])
                                    return pbc

                                s_load(0)
                                pbc_next = s_prep(0)
                                for b in range(NSMP):
                                    ss = SS[b % 2]
                                    if b + 1 < NSMP:
                                        s_load(b + 1)
                                    pbc = pbc_next
                                    if b + 1 < NSMP:
                                        pbc_next = s_prep(b + 1)
                                    for hh in range(16):
                                        ACT(ss.t[:, hh, :], ss.t[:, hh, :], AF.Copy, [ssb[b % 2][hh], DBC], [ssb[b % 2][hh]], scale=DBC.t[:, hh, 16 + b:17 + b])
                                    for hh in range(16):
                                        g = hh // 2
                                        STT(ss.t[:, hh, :], pbc[g // 4].t[:, (g % 4) * 128:(g % 4 + 1) * 128], XDS.t[:, hh, b:b + 1], ss.t[:, hh, :], ALU.mult, ALU.add, [pbc[g // 4], XDS, ssb[b % 2][hh]], [ssb[b % 2][hh]])
                                    k.dma(k.sp, oss[j, b, :, :].rearrange("(a p) n -> p a n", p=128), ss.t[:, :, :], reads=ssb[b % 2], final=True)
                                    for a in range(2):
                                        ACT(CS.t[:, a * 4:(a + 1) * 4, :], pbc[2 + a].t[:, :].rearrange("p (g n) -> p g n", n=128), AF.Copy, [pbc[2 + a]], [CS])
                                    PTT(TA.t[:, :, :].rearrange("p (g e) n -> p g e n", e=2),
                                        ss.t[:, :, :].rearrange("p (g e) n -> p g e n", e=2),
                                        CS.t[:, :, :].unsqueeze(2).to_broadcast([128, 8, 2, 128]),
                                        ALU.mult, ssb[b % 2] + [CS], [TA])
                                    k.op(k.dve, lambda b=b: nc.vector.tensor_reduce(out=YS.t[:, :, b], in_=TA.t[:, :, :], axis=AX.X, op=ALU.add), [TA], [YS])
                                CP(YT.t[:, :, 512:NL], YS.t[:, :, :], [YS], YTb)
                        k.mark(f"  L{i} t{tix} core/sample end")
                        zi = 0
                        for it in range(8):
                            sl = wload([(win[j, :, it * 256:(it + 1) * 256], 8, 256, 0)])
                            for e in range(2):
                                zc = it * 2 + e
                                for (l, g, n) in lsubs:
                                    pz = ps()
                                    MM(pz, pz.t[:, :n], [(sl.t[:, kk * 256 + e * 128: kk * 256 + (e + 1) * 128], HN.t[:, kk, l:l + n]) for kk in range(8)], [sl, HN])
                                    gq = GS[zi % 2]
                                    zi += 1
                                    ACT(gq.t[:, 0, :n], pz.t[:, :n], AF.Silu, [pz], [gq])
                                    TS(gq.t[:, 1, :n], XB.t[:, zc, l:l + n], V(f"sD{j}", zc), None, ALU.mult, None, [XBb[zc], VEC, gq], [gq])
                                    TT(YT.t[:, zc, l:l + n], YT.t[:, zc, l:l + n], gq.t[:, 1, :n], ALU.add, [YTb[zc], gq], [YTb[zc]])
                                    TT(YT.t[:, zc, l:l + n], YT.t[:, zc, l:l + n], gq.t[:, 0, :n], ALU.mult, [YTb[zc], gq], [YTb[zc]])
                        k.mark(f"  L{i} t{tix} z end")
                        gjobs = [(g8, l, g, n) for g8 in range(8) for (l, g, n) in lsubs]

                        def gn_sq(idx):
                            g8, l, g, n = gjobs[idx]
                            gq = GS[idx % 2]
                            ACT(gq.t[:, 0:2, :n], YT.t[:, 2 * g8:2 * g8 + 2, l:l + n], AF.Square, YTb[2 * g8:2 * g8 + 2], [gq])
                            pn = ps()
                            MM(pn, pn.t[:, :n], [(onesb, gq.t[:, e, :n]) for e in range(2)], [gq, CB])
                            return pn

                        pn_next = gn_sq(0)
                        for idx, (g8, l, g, n) in enumerate(gjobs):
                            pn = pn_next
                            if idx + 1 < len(gjobs):
                                pn_next = gn_sq(idx + 1)
                            rt = rts[idx % 2]
                            RSTD(rt, n, pn.t[:, :n], pn, 1.0 / 256)
                            for e in range(2):
                                STT(YT.t[:, 2 * g8 + e, l:l + n], YT.t[:, 2 * g8 + e, l:l + n], V(f"snorm{j}", 2 * g8 + e), rt.t[:, :n], ALU.mult, ALU.mult, [YTb[2 * g8 + e], rt, VEC], [YTb[2 * g8 + e]])
                        for m in range(8):
                            sl = wload([(wout[j, :, m * 128:(m + 1) * 128], 16, 128, 0)])
                            for (l, g, n) in lsubs:
                                po = ps()
                                MM(po, po.t[:, :n], [(wv(sl, 0, kk, 128), YT.t[:, kk, l:l + n]) for kk in range(16)], [sl] + YTb)
                                TT(X.t[:, m, g:g + n], po.t[:, :n], X.t[:, m, g:g + n], ALU.add, [po, X1(m, g)], [X1(m, g)])

            k.mark("start")
            for i in range(DEPTH):
                ffn(i, 1)
                k.mark(f"L{i} ffn1 end")
                if i % 2 == 0:
                    conv_mixer(i, i // 2)
                else:
                    ssd_mixer(i, i // 2)
                k.mark(f"L{i} mixer end")
                ffn(i, 2)
                k.mark(f"L{i} ffn2 end")
                ple(i)
                k.mark(f"L{i} ple end")

            with k.phase():
                YN = k.buf("YN", [128, 8, NT], F32)
                sqs = [k.buf(f"sq{a}", [128, 8, 512], BF16) for a in range(2)]
                rts = [k.buf(f"rt{a}", [128, 512], F32) for a in range(2)]
                ost = [k.buf(f"yo{a}", [128, D], F32) for a in range(2)]
                YNb = k.sub(YN, 5, "t")
                rmsnorm("final_norm", lambda do: [YNb[tix_of(do)]], lambda kk, do, n: YN.t[:, kk, do:do + n], [(s, s, n) for s, n in TILES], sqs, rts)
                for r in range(17):
                    R = 128 if r < 16 else NSMP
                    o = ost[r % 2]
                    for half in range(2):
                        pb = ps()
                        for q in range(4):
                            c = half * 4 + q
                            TR(pb, pb.t[:R, q * 128:(q + 1) * 128], YNb[tix_of(r * 128)], YN.t[:, c, r * 128:r * 128 + R], identf, inc=(q == 3), extra=[CST])
                        if half == 0:
                            ACT(o.t[:R, 0:512], pb.t[:R, :], AF.Copy, [pb], [o])
                        else:
                            CP(o.t[:R, 512:1024], pb.t[:R, :], [pb], [o])
                    dst = yp[r * 128:(r + 1) * 128, :] if r < 16 else ysd[:, :]
                    k.dma(k.sp, dst, o.t[:R, :], reads=[o], final=True)
                if not k.dry:
                    for dep in k.final.values():
                        k._wait(k.sp, dep)
                    if k.ddcnt:
                        nc.sync.wait_ge(dsem_dd, k.ddcnt)

        k.dry = True
        k.wplan = []
        k.ddcnt = 0
        body()
        k.dry = False
        k.wissued = 0
        k.ddcnt = 0
        body()
        build.marks = getattr(k, "marks", [])
    return nc


_NC_CACHE = {}


def make_in_maps(inp):
    inp = {k_: np.asarray(v) for k_, v in inp.items()}
    vecs = np.ascontiguousarray(np.concatenate([a for (_, _, a) in vec_entries(inp)], axis=1), dtype=np.float32)
    cst = const_array()
    sel = sel_array()
    wnames = ["w_ffn1_gate", "w_ffn1_up", "w_ffn1_down", "w_ffn2_gate", "w_ffn2_up", "w_ffn2_down", "w_ple_gate", "w_ple_proj",
              "cm_w_in", "cm_w_out", "ssd_w_in", "ssd_w_out"]
    shared = {nm: np.ascontiguousarray(inp[nm], dtype=np.float32) for nm in wnames}
    shared.update(vecs=vecs, cst=cst, sel=sel)
    in_maps = []
    for c in range(NCORES):
        b0 = c * NSMP
        m = dict(shared)
        m["xp"] = np.ascontiguousarray(inp["x_prompt"][c], dtype=np.float32)
        m["xs"] = np.ascontiguousarray(inp["x_sample"][b0:b0 + NSMP, 0], dtype=np.float32)
        m["stc"] = np.ascontiguousarray(inp["state_conv"][:, b0:b0 + NSMP].reshape(2, NSMP * 30, D), dtype=np.float32)
        m["stsc"] = np.ascontiguousarray(inp["state_ssd_conv"][:, b0:b0 + NSMP].reshape(2, NSMP * 3, CONVD), dtype=np.float32)
        m["stss"] = np.ascontiguousarray(inp["state_ssd"][:, b0:b0 + NSMP].reshape(2, NSMP, NH * 64, 128), dtype=np.float32)
        m["pp"] = np.ascontiguousarray(inp["p_prompt"][:, c], dtype=np.float32)
        m["psm"] = np.ascontiguousarray(inp["p_sample"][:, b0:b0 + NSMP, 0], dtype=np.float32)
        in_maps.append(m)
    return in_maps


def kernel(**inp):
    if "nc" not in _NC_CACHE:
        _NC_CACHE["nc"] = build()
    nc = _NC_CACHE["nc"]
    in_maps = make_in_maps(inp)
    res = run_bass_kernel_spmd(nc, in_maps, core_ids=list(range(NCORES)))
    R = res.results
    y_prompt = np.stack([R[c]["yp"] for c in range(NCORES)], 0).astype(np.float32)
    y_sample = np.concatenate([R[c]["ys"] for c in range(NCORES)], 0).reshape(NCORES * NSMP, 1, D).astype(np.float32)
    conv_p = np.stack([R[c]["ocp"] for c in range(NCORES)], 1).astype(np.float32)
    xbc_p = np.stack([R[c]["oxp"] for c in range(NCORES)], 1).astype(np.float32)
    ssm_p = np.stack([R[c]["osp"].reshape(2, NH, 64, 128) for c in range(NCORES)], 1).astype(np.float32)
    conv_s = np.concatenate([R[c]["ocs"] for c in range(NCORES)], 1).astype(np.float32)
    xbc_s = np.concatenate([R[c]["oxs"] for c in range(NCORES)], 1).astype(np.float32)
    ssm_s = np.concatenate([R[c]["oss"].reshape(2, NSMP, NH, 64, 128) for c in range(NCORES)], 1).astype(np.float32)
    return (y_prompt, y_sample, conv_p, xbc_p, ssm_p, conv_s, xbc_s, ssm_s)
```

```python
import numpy as np
from contextlib import ExitStack, contextmanager
import concourse.bass as bass
import concourse.mybir as mybir
from concourse.bass_utils import run_bass_kernel_spmd

F32 = mybir.dt.float32
BF16 = mybir.dt.bfloat16
AF = mybir.ActivationFunctionType
ALU = mybir.AluOpType
AX = mybir.AxisListType
P = 128
D = 1024
DFF = 2816
T = 2048
NSMP = 16
NT = T + NSMP
DEPTH = 4
DIN = 2048
CONVD = 4096
NH = 32
PLE = 256
CK = 31
EPS = 1e-6
NSLOT = 5
SLOTE = 2048
NCORES = 8
TILES = [(0, 512), (512, 512), (1024, 512), (1536, 512), (2048, 16)]


def fm(v):
    v = np.asarray(v, np.float32).reshape(-1)
    return np.ascontiguousarray(v.reshape(-1, 128).T)


def vec_entries(inp):
    E = []

    def add(name, n, fn):
        if inp is None:
            E.append((name, n, None))
        else:
            a = np.zeros((128, n), np.float32)
            b = fn()
            a[: b.shape[0], :] = b
            E.append((name, n, a))

    for i in range(DEPTH):
        for nm in ("norm_ffn1", "norm_mix", "norm_ffn2", "norm_ple"):
            add(f"{nm}{i}", 8, lambda nm=nm, i=i: fm(inp[nm][i]))
    add("final_norm", 8, lambda: fm(inp["final_norm"]))
    for j in range(2):
        add(f"cbin{j}", 16, lambda j=j: fm(inp["cm_b_in"][j]))
        add(f"cdw{j}", 8 * CK, lambda j=j: np.asarray(inp["cm_dw"][j], np.float32).T.reshape(8, 128, CK).transpose(1, 0, 2).reshape(128, 8 * CK))
        add(f"cdwb{j}", 8, lambda j=j: fm(inp["cm_dw_b"][j]))
        add(f"clng{j}", 8, lambda j=j: fm(inp["cm_ln_g"][j]))
        add(f"clnb{j}", 8, lambda j=j: fm(inp["cm_ln_b"][j]))
        add(f"cbout{j}", 8, lambda j=j: fm(inp["cm_b_out"][j]))
    for j in range(2):
        add(f"scw{j}", 32 * 4, lambda j=j: np.asarray(inp["ssd_conv_w"][j], np.float32).T.reshape(32, 128, 4).transpose(1, 0, 2).reshape(128, 128))
        add(f"scb{j}", 32, lambda j=j: fm(inp["ssd_conv_b"][j]))
        add(f"snorm{j}", 16, lambda j=j: fm(inp["ssd_norm"][j]))
        add(f"sD{j}", 16, lambda j=j: np.repeat(np.asarray(inp["ssd_D"][j], np.float32).reshape(16, 2, 1), 64, axis=2).transpose(1, 2, 0).reshape(128, 16))
        add(f"dtb32{j}", 1, lambda j=j: np.asarray(inp["ssd_dt_bias"][j], np.float32).reshape(32, 1))
        add(f"alog32{j}", 1, lambda j=j: np.asarray(inp["ssd_A_log"][j], np.float32).reshape(32, 1))
        add(f"dtbbc{j}", 32, lambda j=j: np.tile(np.asarray(inp["ssd_dt_bias"][j], np.float32).reshape(1, 32), (128, 1)))
        add(f"alogbc{j}", 32, lambda j=j: np.tile(np.asarray(inp["ssd_A_log"][j], np.float32).reshape(1, 32), (128, 1)))
    return E


def vec_offsets():
    off = {}
    o = 0
    for name, n, _ in vec_entries(None):
        off[name] = o
        o += n
    return off, o


def const_array():
    c = np.zeros((128, 512), np.float32)
    i = np.arange(128)
    c[:, 0:128] = np.eye(128, dtype=np.float32)
    c[:, 128:256] = (i[:, None] <= i[None, :]).astype(np.float32)
    c[:, 256:384] = (i[:, None] > i[None, :]).astype(np.float32)
    c[:, 384:512] = 1.0
    return c


def sel_array():
    s = np.zeros((32, 16, 128), np.float32)
    for hh in range(16):
        for m in range(128):
            s[2 * hh + m // 64, hh, m] = 1.0
    return s.reshape(32, 2048)


class Buf:
    def __init__(self, t, name):
        self.t = t
        self.name = name
        self.w = None
        self.r = {}
        self.dsem = None


class Eng:
    def __init__(self, raw, sem, name):
        self.raw = raw
        self.sem = sem
        self.name = name
        self.cnt = 0
        self.seen = {}


class K:
    def __init__(self, nc, es):
        self.nc = nc
        self.es = es
        self.dry = False
        self.alloc = es
        self.phase_bufs = None
        self.sem_free = []
        self.sem_tot = {}
        self.pending = {}
        self.final = {}
        self.nsem = 0
        self.uid = 0

    def init_engines(self):
        nc, es = self.nc, self.es

        def mk(raw, name):
            return Eng(raw, es.enter_context(nc.semaphore("sem_" + name)), name)

        self.pe = mk(nc.tensor, "pe")
        self.act = mk(nc.scalar, "act")
        self.dve = mk(nc.vector, "dve")
        self.pool = mk(nc.gpsimd, "pool")
        self.sp = mk(nc.sync, "sp")
        self.engs = [self.pe, self.act, self.dve, self.pool, self.sp]
        for e in self.engs:
            e.cnt = 0
            e.seen = {}

    def buf(self, name, shape, dt):
        self.uid += 1
        t = self.alloc.enter_context(self.nc.sbuf_tensor(f"{name}_{self.uid}", list(shape), dt))
        b = Buf(t, name)
        if self.phase_bufs is not None:
            self.phase_bufs.append(b)
            b.in_phase = True
        return b

    def sub(self, b, n, tag=""):
        out = []
        for i in range(n):
            s = Buf(b.t, f"{b.name}{tag}{i}")
            s.in_phase = getattr(b, "in_phase", False)
            if self.phase_bufs is not None and s.in_phase:
                self.phase_bufs.append(s)
            out.append(s)
        return out

    def _getsem(self, b):
        if b.dsem is None:
            if self.sem_free and getattr(b, "in_phase", False):
                b.dsem = self.sem_free.pop()
            else:
                self.nsem += 1
                sem = self.es.enter_context(self.nc.semaphore(f"dsem{self.nsem}"))
                b.dsem = (f"dsem{self.nsem}", sem)
                self.sem_tot[b.dsem[0]] = 0
        return b.dsem

    def _wait(self, e, dep):
        if dep is None:
            return
        key, sem, val = dep
        if e.seen.get(key, 0) >= val:
            return
        e.raw.wait_ge(sem, val)
        e.seen[key] = val

    def _deps(self, e, reads, writes, isdma=False):
        for b in reads:
            self._wait(e, b.w)
        same_ok = (not isdma) and e.name == "pe"
        for b in writes:
            if b.w is not None and not (same_ok and b.w[0] == e.name):
                self._wait(e, b.w)
            for k, (sem, val) in b.r.items():
                if not (same_ok and k == e.name):
                    self._wait(e, (k, sem, val))

    def op(self, e, fn, reads=(), writes=(), inc=True):
        if self.dry:
            return None
        self._deps(e, reads, writes)
        ins = fn()
        self.opidx = getattr(self, "opidx", 0) + 1
        for b in reads:
            b.last_read = self.opidx
        for b in writes:
            b.last_write = self.opidx
        if e.name == "pe":
            self.npe = getattr(self, "npe", 0) + 1
        if inc:
            e.cnt += 1
            ins.then_inc(e.sem, 1)
            c = e.cnt
        else:
            c = e.cnt + 1
        for b in reads:
            b.r[e.name] = (e.sem, c)
        for b in writes:
            b.w = (e.name, e.sem, c)
            b.r = {}
        return ins

    def dma(self, q, out, in_, reads=(), writes=(), final=False, **kw):
        if self.dry:
            return
        self._deps(q, reads, writes, isdma=True)
        ins = q.raw.dma_start(out=out, in_=in_, **kw)
        b = (list(writes) + list(reads))[0]
        key, sem = self._getsem(b)
        self.sem_tot[key] += 16
        tot = self.sem_tot[key]
        ins.then_inc(sem, 16)
        for w in writes:
            w.w = (key, sem, tot)
            w.r = {}
        for r in reads:
            r.r[key] = (sem, tot)
        self.pending[key] = (key, sem, tot)
        if final:
            self.final[key] = (key, sem, tot)

    def mark(self, label):
        if not self.dry:
            self.marks = getattr(self, "marks", [])
            self.marks.append((label, getattr(self, "npe", 0)))

    def barrier(self):
        if self.dry:
            return
        engs = [self.pe, self.act, self.dve, self.sp, self.pool]
        for e in engs:
            for e2 in engs:
                if e2 is not e and e2.cnt > 0:
                    self._wait(e, (e2.name, e2.sem, e2.cnt))
            for dep in self.pending.values():
                self._wait(e, dep)
        self.pending = {}

    @contextmanager
    def phase(self):
        old_alloc, old_bufs = self.alloc, self.phase_bufs
        with ExitStack() as st:
            self.alloc = st
            self.phase_bufs = []
            yield
            self.barrier()
            for b in self.phase_bufs:
                if b.dsem is not None:
                    self.sem_free.append(b.dsem)
        self.alloc, self.phase_bufs = old_alloc, old_bufs


def build():
    nc = bass.Bass("TRN2", target_bir_lowering=False)
    voff, NV = vec_offsets()

    def din(name, shape):
        return nc.dram_tensor(name, list(shape), F32, kind="ExternalInput").ap()

    def dout(name, shape):
        return nc.dram_tensor(name, list(shape), F32, kind="ExternalOutput").ap()

    xp = din("xp", [T, D])
    xs = din("xs", [NSMP, D])
    stc = din("stc", [2, NSMP * 30, D])
    stsc = din("stsc", [2, NSMP * 3, CONVD])
    stss = din("stss", [2, NSMP, NH * 64, 128])
    pp = din("pp", [DEPTH, T, PLE])
    psm = din("psm", [DEPTH, NSMP, PLE])
    W = {}
    for nm, shp in [("w_ffn1_gate", [DEPTH, D, DFF]), ("w_ffn1_up", [DEPTH, D, DFF]), ("w_ffn1_down", [DEPTH, DFF, D]),
                    ("w_ffn2_gate", [DEPTH, D, DFF]), ("w_ffn2_up", [DEPTH, D, DFF]), ("w_ffn2_down", [DEPTH, DFF, D]),
                    ("w_ple_gate", [DEPTH, D, D]), ("w_ple_proj", [DEPTH, PLE, D]),
                    ("cm_w_in", [2, D, 2 * D]), ("cm_w_out", [2, D, D]),
                    ("ssd_w_in", [2, D, 6176]), ("ssd_w_out", [2, DIN, D])]:
        W[nm] = din(nm, shp)
    vecs_d = din("vecs", [128, NV])
    cst_d = din("cst", [128, 512])
    sel_d = din("sel", [32, 2048])
    yp = dout("yp", [T, D])
    ysd = dout("ys", [NSMP, D])
    ocp = dout("ocp", [2, 30, D])
    oxp = dout("oxp", [2, 3, CONVD])
    osp = dout("osp", [2, NH * 64, 128])
    ocs = dout("ocs", [2, NSMP, 30, D])
    oxs = dout("oxs", [2, NSMP, 3, CONVD])
    oss = dout("oss", [2, NSMP, NH * 64, 128])

    with ExitStack() as es:
        k = K(nc, es)
        X = k.buf("X", [128, 8, NT], F32)
        VEC = k.buf("VEC", [128, NV], F32)
        CST = k.buf("CST", [128, 512], F32)
        CB = k.buf("CB", [128, 256], BF16)
        k.wslots = [k.buf(f"wslot{i}", [128, SLOTE], BF16) for i in range(NSLOT)]
        PS = [Buf(es.enter_context(nc.psum_tensor(f"psum{i}", [128, 512], F32)), f"psum{i}") for i in range(8)]
        dsem_dd = es.enter_context(nc.semaphore("dsem_dd"))
        k.init_engines()
        k.psi = 0

        identf = CST.t[:, 0:128]
        trif = CST.t[:, 128:256]
        ustrf = CST.t[:, 256:384]
        onesf = CST.t[:, 384:512]
        identb = CB.t[:, 0:128]
        onesb = CB.t[:, 128:256]

        Xb = [k.sub(X, 5, f"c{m}t") for m in range(8)]

        def tix_of(col):
            return min(col // 512, 4)

        def XT(col):
            return [Xb[m][tix_of(col)] for m in range(8)]

        def X1(m, col):
            return Xb[m][tix_of(col)]

        def V(name, c=0, n=1):
            o = voff[name] + c
            return VEC.t[:, o:o + n]

        def ps():
            if k.dry:
                return PS[0]
            free = [b for b in PS if getattr(b, "last_read", 0) >= getattr(b, "last_write", 0)]
            if free:
                b = min(free, key=lambda b: getattr(b, "last_read", 0))
            else:
                b = min(PS, key=lambda b: getattr(b, "last_write", 0))
            k.opidx = getattr(k, "opidx", 0) + 1
            b.last_write = k.opidx
            return b

        def ACT(out, in_, func, R, Wr, **kw):
            k.op(k.act, lambda: nc.scalar.activation(out=out, in_=in_, func=func, **kw), R, Wr)

        def TT(out, in0, in1, op, R, Wr):
            k.op(k.dve, lambda: nc.vector.tensor_tensor(out=out, in0=in0, in1=in1, op=op), R, Wr)

        def STT(out, in0, scalar, in1, op0, op1, R, Wr):
            k.op(k.dve, lambda: nc.vector.scalar_tensor_tensor(out=out, in0=in0, scalar=scalar, in1=in1, op0=op0, op1=op1), R, Wr)

        def TS(out, in0, s1, s2, op0, op1, R, Wr):
            if op1 is None:
                k.op(k.dve, lambda: nc.vector.tensor_scalar(out=out, in0=in0, scalar1=s1, scalar2=None, op0=op0), R, Wr)
            else:
                k.op(k.dve, lambda: nc.vector.tensor_scalar(out=out, in0=in0, scalar1=s1, scalar2=s2, op0=op0, op1=op1), R, Wr)

        def CP(out, in_, R, Wr):
            k.op(k.dve, lambda: nc.vector.tensor_copy(out=out, in_=in_), R, Wr)

        def RSTD(rt, n, src_ap, srcB, scale):
            ACT(rt.t[:, :n], src_ap, AF.Ln, [srcB], [rt], scale=scale, bias=EPS)
            ACT(rt.t[:, :n], rt.t[:, :n], AF.Exp, [rt], [rt], scale=-0.5)

        def RCP(out, in_, R, Wr):
            k.op(k.dve, lambda: nc.vector.reciprocal(out=out, in_=in_), R, Wr)

        def MEMSET(out, val, Wr):
            k.op(k.dve, lambda: nc.vector.memset(out, val), (), Wr)

        def MM(ob, out, pairs, R, inc=True, first=True, final=True):
            n = len(pairs)
            for i, (l, r) in enumerate(pairs):
                last = i == n - 1
                k.op(k.pe, lambda l=l, r=r, i=i, last=last: nc.tensor.matmul(out, lhsT=l, rhs=r, start=(first and i == 0), stop=(final and last)),
                     R if i == 0 else (), [ob] if i == 0 else (), inc=(last and inc))

        def PTT(out, in0, in1, op, R, Wr):
            k.op(k.pool, lambda: nc.gpsimd.tensor_tensor(out=out, in0=in0, in1=in1, op=op), R, Wr)

        def TR(ob, out, ib, in_, ident, inc=True, extra=()):
            k.op(k.pe, lambda: nc.tensor.transpose(out, in_, ident), [ib] + list(extra), [ob], inc=inc)

        def wload(specs):
            j = k.wj
            k.wj += 1
            if k.dry:
                k.wplan.append(specs)
                return k.wslots[j % NSLOT]
            base = k.whold if k.whold is not None else j
            while k.wissued < min(len(k.wplan), base + NSLOT):
                jj = k.wissued
                slot = k.wslots[jj % NSLOT]
                for (src, kc, ncols, off) in k.wplan[jj]:
                    dst = slot.t[:, off:off + kc * ncols].rearrange("p (k c) -> p k c", k=kc)
                    k.dma(k.pool, dst, src.rearrange("(k p) c -> p k c", p=128), writes=[slot])
                k.wissued += 1
            return k.wslots[j % NSLOT]

        def wv(slot, off, kk, ncols):
            return slot.t[:, off + kk * ncols: off + (kk + 1) * ncols]

        def body():
            k.wj = 0
            k.psi = 0
            k.whold = None
            k.dma(k.sp, VEC.t[:, :], vecs_d[:, :], writes=[VEC])
            k.dma(k.sp, CST.t[:, :], cst_d[:, :], writes=[CST])
            CP(CB.t[:, 0:128], identf, [CST], [CB])
            CP(CB.t[:, 128:256], onesf, [CST], [CB])

            with k.phase():
                stg = [k.buf(f"stg{i}", [128, D], F32) for i in range(2)]
                for r in range(17):
                    st = stg[r % 2]
                    R = 128 if r < 16 else NSMP
                    src = xp[r * 128:(r + 1) * 128, :] if r < 16 else xs[:, :]
                    k.dma(k.sp, st.t[:R, :], src, writes=[st])
                    for half in range(2):
                        pb = ps()
                        for q in range(4):
                            c = half * 4 + q
                            TR(pb, pb.t[:, q * 128:q * 128 + R], st, st.t[:R, c * 128:(c + 1) * 128], identf[:R, :R], inc=(q == 3), extra=[CST])
                        src_v = pb.t[:, :].rearrange("p (a b) -> p a b", a=4)[:, :, :R]
                        dst_v = X.t[:, half * 4:half * 4 + 4, r * 128:r * 128 + R]
                        xw = [X1(m, r * 128) for m in range(half * 4, half * 4 + 4)]
                        if half == 0:
                            ACT(dst_v, src_v, AF.Copy, [pb], xw)
                        else:
                            CP(dst_v, src_v, [pb], xw)

            def rmsnorm(gname, dstB_fn, dst_fn, subs, sqs, rts):
                for ti, (so, do, n) in enumerate(subs):
                    sq = sqs[ti % len(sqs)]
                    rt = rts[ti % len(rts)]
                    ACT(sq.t[:, 0:8, :n], X.t[:, :, so:so + n], AF.Square, XT(so), [sq])
                    pb = ps()
                    MM(pb, pb.t[:, :n], [(onesb, sq.t[:, kk, :n]) for kk in range(8)], [sq, CB])
                    RSTD(rt, n, pb.t[:, :n], pb, 1.0 / D)
                    for kk in range(8):
                        STT(dst_fn(kk, do, n), X.t[:, kk, so:so + n], V(gname, kk), rt.t[:, :n], ALU.mult, ALU.mult, [X1(kk, so), rt, VEC], dstB_fn(do))

            def ffn(i, which):
                wg, wu, wd = W[f"w_ffn{which}_gate"], W[f"w_ffn{which}_up"], W[f"w_ffn{which}_down"]
                with k.phase():
                    XN = k.buf("XN", [128, 8, NT], BF16)
                    H = k.buf("H", [128, 11, NT], BF16)
                    sqs = [k.buf(f"sq{a}", [128, 8, 512], BF16) for a in range(2)]
                    rts = [k.buf(f"rt{a}", [128, 512], F32) for a in range(2)]
                    sgs = [k.buf(f"sg{a}", [128, 512], F32) for a in range(3)]
                    XNb = k.sub(XN, 5, "t")
                    Hb = k.sub(H, 5, "t")
                    rmsnorm(f"norm_ffn{which}{i}", lambda do: [XNb[tix_of(do)]], lambda kk, do, n: XN.t[:, kk, do:do + n], [(s, s, n) for s, n in TILES], sqs, rts)
                    sgi = [0]

                    def ffn_a(sl, fi, s, n):
                        pa = ps()
                        pbb = ps()
                        MM(pa, pa.t[:, :n], [(wv(sl, 0, kk, 128), XN.t[:, kk, s:s + n]) for kk in range(8)], [sl, XNb[tix_of(s)]])
                        MM(pbb, pbb.t[:, :n], [(wv(sl, 1024, kk, 128), XN.t[:, kk, s:s + n]) for kk in range(8)], [sl, XNb[tix_of(s)]])
                        sg = sgs[sgi[0] % 3]
                        sgi[0] += 1
                        ACT(sg.t[:, :n], pa.t[:, :n], AF.Silu, [pa], [sg])
                        TT(H.t[:, fi, s:s + n], pbb.t[:, :n], sg.t[:, :n], ALU.mult, [pbb, sg], [Hb[tix_of(s)]])

                    def ffn_w(f):
                        return wload([(wg[i, :, f * 128:(f + 1) * 128], 8, 128, 0), (wu[i, :, f * 128:(f + 1) * 128], 8, 128, 1024)])

                    for half in range(2):
                        fis = list(range(11))
                        if half == 0:
                            k.whold = k.wj
                            sls = [ffn_w(fi) for fi in range(4)]
                            for (s, n) in TILES:
                                for fi in range(4):
                                    ffn_a(sls[fi], fi, s, n)
                            k.whold = None
                            fis = list(range(4, 11))
                        for fi in fis:
                            sl = ffn_w(half * 11 + fi)
                            for (s, n) in TILES:
                                ffn_a(sl, fi, s, n)
                        for m in range(8):
                            r0 = half * 1408
                            sl = wload([(wd[i, r0:r0 + 1408, m * 128:(m + 1) * 128], 11, 128, 0)])
                            for (s, n) in TILES:
                                pc = ps()
                                MM(pc, pc.t[:, :n], [(wv(sl, 0, fi, 128), H.t[:, fi, s:s + n]) for fi in range(11)], [sl, Hb[tix_of(s)]])
                                STT(X.t[:, m, s:s + n], pc.t[:, :n], 0.5, X.t[:, m, s:s + n], ALU.mult, ALU.add, [pc, X1(m, s)], [X1(m, s)])

            def ple(i):
                with k.phase():
                    XN = k.buf("XN", [128, 8, NT], BF16)
                    PT = k.buf("PT", [128, 2, NT], BF16)
                    sqs = [k.buf(f"sq{a}", [128, 8, 512], BF16) for a in range(2)]
                    rts = [k.buf(f"rt{a}", [128, 512], F32) for a in range(2)]
                    sgs = [k.buf(f"sg{a}", [128, 512], F32) for a in range(3)]
                    t2s = [k.buf(f"t2{a}", [128, 512], F32) for a in range(2)]
                    stg = [k.buf(f"pstg{a}", [128, PLE], F32) for a in range(2)]
                    PTb = k.sub(PT, 5, "t")
                    for r in range(17):
                        st = stg[r % 2]
                        R = 128 if r < 16 else NSMP
                        src = pp[i, r * 128:(r + 1) * 128, :] if r < 16 else psm[i, :, :]
                        k.dma(k.sp, st.t[:R, :], src, writes=[st])
                        pb = ps()
                        for c in range(2):
                            TR(pb, pb.t[:, c * 128:c * 128 + R], st, st.t[:R, c * 128:(c + 1) * 128], identf[:R, :R], inc=(c == 1), extra=[CST])
                        ACT(PT.t[:, :, r * 128:r * 128 + R], pb.t[:, 0:256].rearrange("p (a b) -> p a b", a=2)[:, :, :R], AF.Copy, [pb], [PTb[tix_of(r * 128)]])
                    XNb = k.sub(XN, 5, "t")
                    rmsnorm(f"norm_ple{i}", lambda do: [XNb[tix_of(do)]], lambda kk, do, n: XN.t[:, kk, do:do + n], [(s, s, n) for s, n in TILES], sqs, rts)
                    ci = [0]

                    def ple_w(m):
                        return wload([(W["w_ple_gate"][i, :, m * 128:(m + 1) * 128], 8, 128, 0), (W["w_ple_proj"][i, :, m * 128:(m + 1) * 128], 2, 128, 1024)])

                    def ple_c(sl, m, s, n):
                        pg = ps()
                        pq = ps()
                        MM(pg, pg.t[:, :n], [(wv(sl, 0, kk, 128), XN.t[:, kk, s:s + n]) for kk in range(8)], [sl, XNb[tix_of(s)]])
                        MM(pq, pq.t[:, :n], [(wv(sl, 1024, kk, 128), PT.t[:, kk, s:s + n]) for kk in range(2)], [sl, PTb[tix_of(s)]])
                        sg = sgs[ci[0] % 3]
                        t2 = t2s[ci[0] % 2]
                        ci[0] += 1
                        ACT(sg.t[:, :n], pg.t[:, :n], AF.Sigmoid, [pg], [sg])
                        TT(t2.t[:, :n], pq.t[:, :n], sg.t[:, :n], ALU.mult, [pq, sg], [t2])
                        TT(X.t[:, m, s:s + n], X.t[:, m, s:s + n], t2.t[:, :n], ALU.add, [X1(m, s), t2], [X1(m, s)])

                    k.whold = k.wj
                    sls = [ple_w(m) for m in range(4)]
                    for (s, n) in TILES:
                        for m in range(4):
                            ple_c(sls[m], m, s, n)
                    k.whold = None
                    for m in range(4, 8):
                        sl = ple_w(m)
                        for (s, n) in TILES:
                            ple_c(sl, m, s, n)

            def conv_mixer(i, j):
                with k.phase():
                    XN = k.buf("XN", [128, 8, NT], BF16)
                    Vb = k.buf("Vb", [128, 8, NT], BF16)
                    rts = [k.buf(f"rt{a}", [128, 512], F32) for a in range(2)]
                    sgs = [k.buf(f"sg{a}", [128, 512], F32) for a in range(3)]
                    UL = k.buf("ul", [128, 8, 32], F32)
                    USN = k.buf("usn", [128, 8, NSMP], F32)
                    XNb = k.sub(XN, 5, "t")
                    Vbb = k.sub(Vb, 8, "c")
                    with k.phase():
                        sqs = [k.buf(f"sq{a}", [128, 8, 512], BF16) for a in range(2)]
                        rmsnorm(f"norm_mix{i}", lambda do: [XNb[tix_of(do)]], lambda kk, do, n: XN.t[:, kk, do:do + n], [(s, s, n) for s, n in TILES], sqs, rts)
                    c1 = k.phase()
                    c1.__enter__()
                    UE = [k.buf(f"ue{a}", [128, 30 + T], BF16) for a in range(2)]
                    US = [k.buf(f"us{a}", [128, NSMP, CK], F32) for a in range(2)]
                    DG = [k.buf(f"dg{a}", [128, CK, 128], BF16) for a in range(2)]
                    STSS = [k.buf(f"sts{a}", [128, 4, 128], F32) for a in range(2)]
                    tmp = k.buf("ctmp", [128, NSMP, CK], F32)
                    red = k.buf("cred", [128, NSMP], F32)
                    if not k.dry:
                        nc.sync.dma_start(out=ocs[j, :, 0:29, :], in_=stc[j, :, :].rearrange("(b r) c -> b r c", r=30)[:, 1:30, :]).then_inc(dsem_dd, 16)
                        k.ddcnt += 16
                    sgi = 0
                    for c in range(8):
                        sl = wload([(W["cm_w_in"][j, :, c * 128:(c + 1) * 128], 8, 128, 0), (W["cm_w_in"][j, :, D + c * 128:D + (c + 1) * 128], 8, 128, 1024)])
                        ue = UE[c % 2]
                        us = US[c % 2]
                        dg = DG[c % 2]
                        MEMSET(ue.t[:, 0:30], 0.0, [ue])
                        TT(dg.t[:, :, :], identb.unsqueeze(1).to_broadcast([128, CK, 128]), V(f"cdw{j}", c * CK, CK).unsqueeze(2).to_broadcast([128, CK, 128]), ALU.mult, [CB, VEC], [dg])
                        STS = STSS[c % 2]
                        k.dma(k.sp, STS.t[:, 0:3, :], stc[j, 0:384, c * 128:(c + 1) * 128].rearrange("(a p) c -> p a c", p=128), writes=[STS])
                        k.dma(k.sp, STS.t[:96, 3, :], stc[j, 384:480, c * 128:(c + 1) * 128], writes=[STS])
                        pb = ps()
                        for a in range(4):
                            R = 128 if a < 3 else 96
                            TR(pb, pb.t[:, a * 128:a * 128 + R], STS, STS.t[:R, a, :], identf[:R, :R], inc=(a == 3), extra=[CST])
                        CP(us.t[:, :, 0:30], pb.t[:, 0:480].rearrange("p (b r) -> p b r", r=30), [pb], [us])
                        for (s, n) in TILES:
                            pa = ps()
                            pg = ps()
                            MM(pa, pa.t[:, :n], [(wv(sl, 0, kk, 128), XN.t[:, kk, s:s + n]) for kk in range(8)], [sl, XNb[tix_of(s)]])
                            MM(pg, pg.t[:, :n], [(wv(sl, 1024, kk, 128), XN.t[:, kk, s:s + n]) for kk in range(8)], [sl, XNb[tix_of(s)]])
                            sg = sgs[sgi % 3]
                            sgi += 1
                            ACT(sg.t[:, :n], pg.t[:, :n], AF.Sigmoid, [pg, VEC], [sg], bias=V(f"cbin{j}", 8 + c))
                            if s < T:
                                STT(ue.t[:, 30 + s:30 + s + n], pa.t[:, :n], V(f"cbin{j}", c), sg.t[:, :n], ALU.add, ALU.mult, [pa, sg, VEC], [ue])
                                if s + n == T:
                                    STT(UL.t[:, c, :], pa.t[:, n - 32:n], V(f"cbin{j}", c), sg.t[:, n - 32:n], ALU.add, ALU.mult, [pa, sg, VEC], [UL])
                            else:
                                STT(us.t[:, :, 30], pa.t[:, :n], V(f"cbin{j}", c), sg.t[:, :n], ALU.add, ALU.mult, [pa, sg, VEC], [us])
                        for (s, n) in TILES[:4]:
                            pv = ps()
                            MM(pv, pv.t[:, :n], [(dg.t[:, kk, :], ue.t[:, s + kk:s + kk + n]) for kk in range(CK)], [dg, ue])
                            ACT(Vb.t[:, c, s:s + n], pv.t[:, :n], AF.Identity, [pv, VEC], [Vbb[c]], bias=V(f"cdwb{j}", c))
                        TT(tmp.t[:, :, :], us.t[:, :, :], V(f"cdw{j}", c * CK, CK).unsqueeze(1).to_broadcast([128, NSMP, CK]), ALU.mult, [us, VEC], [tmp])
                        k.op(k.dve, lambda: nc.vector.tensor_reduce(out=red.t[:, :], in_=tmp.t[:, :, :], axis=AX.X, op=ALU.add), [tmp], [red])
                        TS(Vb.t[:, c, T:NT], red.t[:, :], V(f"cdwb{j}", c), None, ALU.add, None, [red, VEC], [Vbb[c]])
                        CP(USN.t[:, c, :], us.t[:, :, 30], [us], [USN])
                    c1.__exit__(None, None, None)
                    c2 = k.phase()
                    c2.__enter__()
                    sqs = [k.buf(f"sq{a}", [128, 8, 512], BF16) for a in range(2)]
                    m1s = [k.buf(f"m1{a}", [128, 512], F32) for a in range(2)]
                    m2s = [k.buf(f"m2{a}", [128, 512], F32) for a in range(2)]
                    dts_ = [k.buf(f"dt{a}", [128, 512], F32) for a in range(2)]
                    for ti, (s, n) in enumerate(TILES):
                        sq = sqs[ti % 2]
                        rt = rts[ti % 2]
                        m1 = m1s[ti % 2]
                        m2 = m2s[ti % 2]
                        ACT(sq.t[:, :, :n], Vb.t[:, :, s:s + n], AF.Square, Vbb, [sq])
                        p1 = ps()
                        p2 = ps()
                        MM(p1, p1.t[:, :n], [(onesb, Vb.t[:, kk, s:s + n]) for kk in range(8)], Vbb + [CB])
                        MM(p2, p2.t[:, :n], [(onesb, sq.t[:, kk, :n]) for kk in range(8)], [sq, CB])
                        ACT(m1.t[:, :n], p1.t[:, :n], AF.Copy, [p1], [m1], scale=1.0 / D)
                        TT(m2.t[:, :n], m1.t[:, :n], m1.t[:, :n], ALU.mult, [m1], [m2])
                        STT(m2.t[:, :n], p2.t[:, :n], 1.0 / D, m2.t[:, :n], ALU.mult, ALU.subtract, [p2, m2], [m2])
                        RSTD(rt, n, m2.t[:, :n], m2, 1.0)
                        for kk in range(8):
                            dtb = dts_[kk % 2]
                            TT(dtb.t[:, :n], Vb.t[:, kk, s:s + n], m1.t[:, :n], ALU.subtract, [Vbb[kk], m1], [dtb])
                            TT(dtb.t[:, :n], dtb.t[:, :n], rt.t[:, :n], ALU.mult, [dtb, rt], [dtb])
                            ACT(XN.t[:, kk, s:s + n], dtb.t[:, :n], AF.Silu, [dtb, VEC], [XNb[tix_of(s)]], scale=V(f"clng{j}", kk), bias=V(f"clnb{j}", kk))
                    def co_w(m):
                        return wload([(W["cm_w_out"][j, :, m * 128:(m + 1) * 128], 8, 128, 0)])

                    def co_c(sl, m, s, n):
                        pc = ps()
                        MM(pc, pc.t[:, :n], [(wv(sl, 0, kk, 128), XN.t[:, kk, s:s + n]) for kk in range(8)], [sl, XNb[tix_of(s)]])
                        STT(X.t[:, m, s:s + n], pc.t[:, :n], V(f"cbout{j}", m), X.t[:, m, s:s + n], ALU.add, ALU.add, [pc, X1(m, s), VEC], [X1(m, s)])

                    k.whold = k.wj
                    sls = [co_w(m) for m in range(4)]
                    for (s, n) in TILES:
                        for m in range(4):
                            co_c(sls[m], m, s, n)
                    k.whold = None
                    for m in range(4, 8):
                        sl = co_w(m)
                        for (s, n) in TILES:
                            co_c(sl, m, s, n)
                    c2.__exit__(None, None, None)
                    OST = k.buf("ost", [32, D], F32)
                    for half in range(2):
                        pb = ps()
                        for q in range(4):
                            c = half * 4 + q
                            TR(pb, pb.t[:32, q * 128:(q + 1) * 128], UL, UL.t[:, c, :], identf, inc=(q == 3), extra=[CST])
                        CP(OST.t[:32, half * 512:(half + 1) * 512], pb.t[:32, :], [pb], [OST])
                    k.dma(k.sp, ocp[j, :, :], OST.t[2:32, :], reads=[OST], final=True)
                    OS2 = k.buf("os2", [NSMP, D], F32)
                    for half in range(2):
                        pb = ps()
                        for q in range(4):
                            c = half * 4 + q
                            TR(pb, pb.t[:NSMP, q * 128:(q + 1) * 128], USN, USN.t[:, c, :], identf, inc=(q == 3), extra=[CST])
                        CP(OS2.t[:, half * 512:(half + 1) * 512], pb.t[:NSMP, :], [pb], [OS2])
                    k.dma(k.sp, ocs[j, :, 29, :], OS2.t[:, :], reads=[OS2], final=True)

            def ssd_mixer(i, j):
                win, wout = W["ssd_w_in"], W["ssd_w_out"]
                with k.phase():
                    NL = 512 + NSMP
                    HN = k.buf("HN", [128, 8, NL], BF16)
                    XB = k.buf("XB", [128, 32, NL], BF16)
                    YT = k.buf("YT", [128, 16, NL], BF16)
                    XBb = k.sub(XB, 32, "c")
                    YTb = k.sub(YT, 16, "c")
                    PRE = [k.buf(f"pre{a}", [128, 515], BF16) for a in range(2)]
                    DGS = [k.buf(f"dgs{a}", [128, 4, 128], BF16) for a in range(2)]
                    HIST = k.buf("hist", [128, 32, 3], F32)
                    HISTb = k.sub(HIST, 32, "c")
                    GS = [k.buf(f"gsq{a}", [128, 2, 512], BF16) for a in range(2)]
                    rts = [k.buf(f"rt{a}", [128, 512], F32) for a in range(2)]
                    sgs = rts
                    ABC = k.buf("abc", [128, 32], F32)
                    S = k.buf("S", [128, DIN], F32)
                    SB = k.buf("SB", [128, DIN], BF16)
                    Sb = k.sub(S, 4, "q")
                    SBb = k.sub(SB, 4, "q")
                    MEMSET(HIST.t[:, :, :], 0.0, HISTb)
                    MEMSET(S.t[:, :], 0.0, Sb)
                    MEMSET(SB.t[:, :], 0.0, SBb)
                    ACT(ABC.t[:, :], V(f"alogbc{j}", 0, 32), AF.Exp, [VEC], [ABC])
                    TS(ABC.t[:, :], ABC.t[:, :], -1.0, None, ALU.mult, None, [ABC], [ABC])
                    ci = 0
                    gi = 0
                    for tix in range(4):
                        s0 = tix * 512
                        smp = tix == 3
                        lsubs = [(0, s0, 512)] + ([(512, T, NSMP)] if smp else [])
                        for (l, g, n) in lsubs:
                            pb = ps()
                            for rnd in range(4):
                                sq = GS[gi % 2]
                                gi += 1
                                ACT(sq.t[:, 0:2, :n], X.t[:, 2 * rnd:2 * rnd + 2, g:g + n], AF.Square, [X1(2 * rnd, g), X1(2 * rnd + 1, g)], [sq])
                                MM(pb, pb.t[:, :n], [(onesb, sq.t[:, e, :n]) for e in range(2)], [sq, CB], first=(rnd == 0), final=(rnd == 3))
                            rt = rts[0]
                            RSTD(rt, n, pb.t[:, :n], pb, 1.0 / D)
                            for kk in range(8):
                                STT(HN.t[:, kk, l:l + n], X.t[:, kk, g:g + n], V(f"norm_mix{i}", kk), rt.t[:, :n], ALU.mult, ALU.mult, [X1(kk, g), rt, VEC], [HN])
                        if smp:
                            xph = k.phase()
                            xph.__enter__()
                            STXS = [k.buf(f"stx{a}", [48, 1024], F32) for a in range(2)]
                            STT_ = k.buf("stT", [128, 32, NSMP, 3], F32)
                            NEWP = k.buf("newp", [128, 32, NSMP], F32)
                            if not k.dry:
                                nc.sync.dma_start(out=oxs[j, :, 0:2, :], in_=stsc[j, :, :].rearrange("(b r) c -> b r c", r=3)[:, 1:3, :]).then_inc(dsem_dd, 16)
                                k.ddcnt += 16
                            for g4 in range(4):
                                STX = STXS[g4 % 2]
                                k.dma(k.sp, STX.t[:, :], stsc[j, :, g4 * 1024:(g4 + 1) * 1024], writes=[STX])
                                pb = ps()
                                for q in range(8):
                                    TR(pb, pb.t[:, q * 48:(q + 1) * 48], STX, STX.t[:, q * 128:(q + 1) * 128], identf[:48, :48], inc=(q == 7), extra=[CST])
                                CP(STT_.t[:, g4 * 8:(g4 + 1) * 8, :, :], pb.t[:, 0:384].rearrange("p (c b r) -> p c b r", c=8, b=NSMP), [pb], [STT_])
                            ctm = k.buf("ctm", [128, NSMP, 3], F32)
                            cr = k.buf("cr", [128, NSMP], F32)
                        sls = {}

                        def emit_proj(cc):
                            it, e = divmod(cc, 2)
                            if e == 0:
                                sls[it] = wload([(win[j, :, DIN + it * 256: DIN + (it + 1) * 256], 8, 256, 0)])
                            sl = sls[it]
                            lw = [sl.t[:, kk * 256 + e * 128: kk * 256 + (e + 1) * 128] for kk in range(8)]
                            pa = ps()
                            MM(pa, pa.t[:, :512], [(lw[kk], HN.t[:, kk, 0:512]) for kk in range(8)], [sl, HN])
                            pq = None
                            if smp:
                                pq = ps()
                                MM(pq, pq.t[:, :NSMP], [(lw[kk], HN.t[:, kk, 512:NL]) for kk in range(8)], [sl, HN])
                            return pa, pq

                        stA = {}
                        stB = {}
                        stC = {}

                        def stage_B(cc):
                            nonlocal ci
                            pa, pq = stA.pop(cc)
                            pre = PRE[ci % 2]
                            dgs = DGS[ci % 2]
                            ci += 1
                            TT(dgs.t[:, :, :], identb.unsqueeze(1).to_broadcast([128, 4, 128]), V(f"scw{j}", cc * 4, 4).unsqueeze(2).to_broadcast([128, 4, 128]), ALU.mult, [CB, VEC], [dgs])
                            CP(pre.t[:, 0:3], HIST.t[:, cc, :], [HISTb[cc]], [pre])
                            ACT(pre.t[:, 3:515], pa.t[:, :512], AF.Copy, [pa], [pre])
                            CP(HIST.t[:, cc, :], pa.t[:, 509:512], [pa], [HISTb[cc]])
                            if smp:
                                CP(NEWP.t[:, cc, :], pq.t[:, :NSMP], [pq], [NEWP])
                                TT(ctm.t[:, :, :], STT_.t[:, cc, :, :], V(f"scw{j}", cc * 4, 3).unsqueeze(1).to_broadcast([128, NSMP, 3]), ALU.mult, [STT_, VEC], [ctm])
                                k.op(k.dve, lambda: nc.vector.tensor_reduce(out=cr.t[:, :], in_=ctm.t[:, :, :], axis=AX.X, op=ALU.add), [ctm], [cr])
                                STT(cr.t[:, :], pq.t[:, :NSMP], V(f"scw{j}", cc * 4 + 3), cr.t[:, :], ALU.mult, ALU.add, [pq, cr, VEC], [cr])
                                ACT(XB.t[:, cc, 512:NL], cr.t[:, :], AF.Silu, [cr, VEC], [XBb[cc]], bias=V(f"scb{j}", cc))
                            stB[cc] = (pre, dgs)

                        def stage_C(cc):
                            pre, dgs = stB.pop(cc)
                            pc = ps()
                            MM(pc, pc.t[:, :512], [(dgs.t[:, kk, :], pre.t[:, kk:kk + 512]) for kk in range(4)], [dgs, pre])
                            stC[cc] = pc

                        def stage_D(cc):
                            pc = stC.pop(cc)
                            ACT(XB.t[:, cc, 0:512], pc.t[:, :512], AF.Silu, [pc, VEC], [XBb[cc]], bias=V(f"scb{j}", cc))

                        for it_ in range(-2, 33):
                            if 0 <= it_ + 2 < 32:
                                stA[it_ + 2] = emit_proj(it_ + 2)
                            if 0 <= it_ + 1 < 32:
                                stage_B(it_ + 1)
                            if 0 <= it_ < 32:
                                stage_C(it_)
                            if 0 <= it_ - 1 < 32:
                                stage_D(it_ - 1)
                        if smp:
                            OX2 = k.buf("ox2", [NSMP, 1024], F32)
                            for g4 in range(4):
                                for hf in range(2):
                                    pb = ps()
                                    for q in range(4):
                                        cc = g4 * 8 + hf * 4 + q
                                        TR(pb, pb.t[:NSMP, q * 128:(q + 1) * 128], NEWP, NEWP.t[:, cc, :], identf, inc=(q == 3), extra=[CST])
                                    CP(OX2.t[:, hf * 512:(hf + 1) * 512], pb.t[:NSMP, :], [pb], [OX2])
                                k.dma(k.sp, oxs[j, :, 2, g4 * 1024:(g4 + 1) * 1024], OX2.t[:, :], reads=[OX2], final=True)
                            xph.__exit__(None, None, None)
                        k.mark(f"  L{i} t{tix} xBC+conv end")
                        sld = wload([(win[j, :, DIN + CONVD: DIN + CONVD + 32], 8, 32, 0)])
                        with k.phase():
                            Rg = [k.buf(f"Rg{a}", [128, 4, 128], F32) for a in range(3)]
                            Lg = [k.buf(f"Lg{a}", [128, 4, 128], BF16) for a in range(3)]
                            MTg = [k.buf(f"MT{a}", [128, 4, 128], BF16) for a in range(3)]
                            CBM = k.buf("cbm", [128, 8, 128], BF16)
                            CBMb = k.sub(CBM, 8, "g")
                            XDT = k.buf("xdt", [128, DIN], BF16)
                            XDD = k.buf("xdd", [128, DIN], BF16)
                            XDTb = k.sub(XDT, 2, "h")
                            XDDb = k.sub(XDD, 2, "h")
                            BTM = k.buf("btm", [128, 1024], BF16)
                            YTM = k.buf("ytm", [128, DIN], BF16)
                            YTMb = k.sub(YTM, 4, "q")
                            T1s = [k.buf(f"t1{a}", [128, 512], BF16) for a in range(1)]
                            sms = [{nm: k.buf(nm + str(a), [128, 32], F32) for nm in ("dt", "a", "acs", "ea", "cd", "dd", "dec", "dtd", "e1")} for a in range(2)]

                            def prologue_stage(q, st):
                                sm = sms[q % 2]
                                lo = q * 128
                                if st == 0:
                                    pd = ps()
                                    MM(pd, pd.t[:, 0:32], [(HN.t[:, kk, lo:lo + 128], sld.t[:, kk * 32:(kk + 1) * 32]) for kk in range(8)], [sld, HN])
                                    TT(sm["e1"].t[:, :], pd.t[:, 0:32], V(f"dtbbc{j}", 0, 32), ALU.add, [pd, VEC], [sm["e1"]])
                                elif st == 1:
                                    ACT(sm["e1"].t[:, :], sm["e1"].t[:, :], AF.Exp, [sm["e1"]], [sm["e1"]])
                                    ACT(sm["dt"].t[:, :], sm["e1"].t[:, :], AF.Ln, [sm["e1"]], [sm["dt"]], bias=1.0)
                                elif st == 2:
                                    TT(sm["a"].t[:, :], sm["dt"].t[:, :], ABC.t[:, :], ALU.mult, [sm["dt"], ABC], [sm["a"]])
                                elif st == 3:
                                    pcs = ps()
                                    sm["pcs"] = pcs
                                    MM(pcs, pcs.t[:, 0:32], [(trif, sm["a"].t[:, :])], [CST, sm["a"]], inc=False)
                                    MM(pcs, pcs.t[:, 32:64], [(onesf, sm["a"].t[:, :])], [CST, sm["a"]])
                                elif st == 4:
                                    pcs = sm["pcs"]
                                    ACT(sm["acs"].t[:, :], pcs.t[:, 0:32], AF.Copy, [pcs], [sm["acs"]])
                                    ACT(sm["ea"].t[:, :], pcs.t[:, 0:32], AF.Exp, [pcs], [sm["ea"]])
                                    ACT(sm["cd"].t[:, :], pcs.t[:, 32:64], AF.Exp, [pcs], [sm["cd"]])
                                elif st == 5:
                                    pcs = sm["pcs"]
                                    TT(sm["dd"].t[:, :], pcs.t[:, 32:64], sm["acs"].t[:, :], ALU.subtract, [pcs, sm["acs"]], [sm["dd"]])
                                elif st == 6:
                                    ACT(sm["dec"].t[:, :], sm["dd"].t[:, :], AF.Exp, [sm["dd"]], [sm["dec"]])
                                elif st == 7:
                                    TT(sm["dtd"].t[:, :], sm["dt"].t[:, :], sm["dec"].t[:, :], ALU.mult, [sm["dt"], sm["dec"]], [sm["dtd"]])

                            def prologue(q):
                                for st in range(8):
                                    prologue_stage(q, st)

                            prologue(0)
                            for q in range(4):
                                sm = sms[q % 2]
                                lo = q * 128
                                pts = []
                                for hb in range(2):
                                    pt = ps()
                                    ptb = pt.t[:, :].bitcast(BF16)
                                    for e in range(8):
                                        hh = hb * 8 + e
                                        TR(pt, ptb[:, e * 128:(e + 1) * 128], XBb[hh], XB.t[:, hh, lo:lo + 128], identb, inc=(e == 7), extra=[CB])
                                    pts.append((pt, ptb))
                                ptB = ps()
                                ptBb = ptB.t[:, :].bitcast(BF16)
                                for g in range(8):
                                    TR(ptB, ptBb[:, g * 128:(g + 1) * 128], XBb[16 + g], XB.t[:, 16 + g, lo:lo + 128], identb, inc=(g == 7), extra=[CB])
                                pcbs = []
                                for hf in range(2):
                                    pcb = ps()
                                    for e in range(4):
                                        g = hf * 4 + e
                                        MM(pcb, pcb.t[:, e * 128:(e + 1) * 128], [(XB.t[:, 16 + g, lo:lo + 128], XB.t[:, 24 + g, lo:lo + 128])], [XBb[16 + g], XBb[24 + g]], inc=(e == 3))
                                    pcbs.append(pcb)
                                rg_of = {}

                                def emit_R(g):
                                    rg = Rg[g % 3]
                                    rg_of[g] = rg
                                    PTT(rg.t[:, :, :], sm["a"].t[:, g * 4:(g + 1) * 4].unsqueeze(2).to_broadcast([128, 4, 128]), trif.unsqueeze(1).to_broadcast([128, 4, 128]), ALU.mult, [sm["a"], CST], [rg])

                                for g in range(3):
                                    emit_R(g)
                                for hb in range(2):
                                    pt, ptb = pts[hb]
                                    pv3 = ptb.rearrange("p (h d) -> p h d", d=64)
                                    TT(XDT.t[:, hb * 1024:(hb + 1) * 1024].rearrange("p (h d) -> p h d", d=64), pv3, sm["dt"].t[:, hb * 16:(hb + 1) * 16].unsqueeze(2).to_broadcast([128, 16, 64]), ALU.mult, [pt, sm["dt"]], [XDTb[hb]])
                                    TT(XDD.t[:, hb * 1024:(hb + 1) * 1024].rearrange("p (h d) -> p h d", d=64), pv3, sm["dtd"].t[:, hb * 16:(hb + 1) * 16].unsqueeze(2).to_broadcast([128, 16, 64]), ALU.mult, [pt, sm["dtd"]], [XDDb[hb]])
                                ACT(BTM.t[:, :], ptBb, AF.Copy, [ptB], [BTM])
                                for hf in range(2):
                                    TT(CBM.t[:, hf * 4:(hf + 1) * 4, :], pcbs[hf].t[:, :].rearrange("p (a b) -> p a b", a=4), trif.unsqueeze(1).to_broadcast([128, 4, 128]), ALU.mult, [pcbs[hf], CST], CBMb[hf * 4:(hf + 1) * 4])
                                psegs = {}

                                def emit_seg(g):
                                    rg = rg_of[g]
                                    pseg = ps()
                                    MM(pseg, pseg.t[:, :], [(ustrf, rg.t[:, :, :].rearrange("p a b -> p (a b)"))], [CST, rg])
                                    if g + 3 < 8:
                                        emit_R(g + 3)
                                    lg = Lg[g % 3]
                                    ACT(lg.t[:, :, :].rearrange("p a b -> p (a b)"), pseg.t[:, :], AF.Exp, [pseg], [lg])
                                    mt = MTg[g % 3]
                                    TT(mt.t[:, :, :], lg.t[:, :, :], CBM.t[:, g, :].unsqueeze(1).to_broadcast([128, 4, 128]), ALU.mult, [lg, CBMb[g]], [mt])
                                    return mt

                                mts = {0: emit_seg(0), 1: emit_seg(1)}
                                pyd = pyo = None
                                for g in range(8):
                                    b4, e2_ = divmod(g, 2)
                                    if q + 1 < 4:
                                        prologue_stage(q + 1, g)
                                    if g + 2 < 8:
                                        mts[g + 2] = emit_seg(g + 2)
                                    if e2_ == 0:
                                        pyd = ps()
                                        pyo = ps()
                                    mt = mts[g]
                                    for r4 in range(4):
                                        h = g * 4 + r4
                                        e = e2_ * 4 + r4
                                        MM(pyd, pyd.t[:, e * 64:(e + 1) * 64], [(mt.t[:, r4, :], XDT.t[:, h * 64:(h + 1) * 64])], [mt, XDTb[h // 16]], inc=(r4 == 3))
                                    MM(pyo, pyo.t[:, e2_ * 256:(e2_ + 1) * 256], [(XB.t[:, 24 + g, lo:lo + 128], SB.t[:, g * 256:(g + 1) * 256])], [XBb[24 + g], SBb[b4]])
                                    if e2_ == 1:
                                        T1 = T1s[0]
                                        TT(T1.t[:, :].rearrange("p (h d) -> p h d", d=64), pyo.t[:, :].rearrange("p (h d) -> p h d", d=64), sm["ea"].t[:, b4 * 8:(b4 + 1) * 8].unsqueeze(2).to_broadcast([128, 8, 64]), ALU.mult, [pyo, sm["ea"]], [T1])
                                        TT(YTM.t[:, b4 * 512:(b4 + 1) * 512], pyd.t[:, :], T1.t[:, :], ALU.add, [pyd, T1], [YTMb[b4]])
                                for b4 in range(4):
                                    pst = ps()
                                    for e in range(2):
                                        g = b4 * 2 + e
                                        MM(pst, pst.t[:, e * 256:(e + 1) * 256], [(BTM.t[:, g * 128:(g + 1) * 128], XDD.t[:, g * 256:(g + 1) * 256])], [BTM, XDDb[g // 4]], inc=(e == 1))
                                    sv = S.t[:, b4 * 512:(b4 + 1) * 512]
                                    TT(sv.rearrange("p (h d) -> p h d", d=64), sv.rearrange("p (h d) -> p h d", d=64), sm["cd"].t[:, b4 * 8:(b4 + 1) * 8].unsqueeze(2).to_broadcast([128, 8, 64]), ALU.mult, [Sb[b4], sm["cd"]], [Sb[b4]])
                                    TT(sv, pst.t[:, :], sv, ALU.add, [pst, Sb[b4]], [Sb[b4]])
                                    ACT(SB.t[:, b4 * 512:(b4 + 1) * 512], sv, AF.Copy, [Sb[b4]], [SBb[b4]])
                                for hb in range(2):
                                    pt = ps()
                                    ptb = pt.t[:, :].bitcast(BF16)
                                    for e in range(8):
                                        hh = hb * 8 + e
                                        TR(pt, ptb[:, e * 128:(e + 1) * 128], YTMb[hh // 4], YTM.t[:, hh * 128:(hh + 1) * 128], identb, inc=(e == 7), extra=[CB])
                                    ACT(YT.t[:, hb * 8:(hb + 1) * 8, lo:lo + 128], ptb.rearrange("p (a b) -> p a b", a=8), AF.Copy, [pt], YTb[hb * 8:(hb + 1) * 8])
                                k.mark(f"    L{i} t{tix} chunk{q} end")
                        if smp:
                            with k.phase():
                                SO = k.buf("so", [128, 16, 128], F32)
                                for g4 in range(4):
                                    pb = ps()
                                    for q in range(4):
                                        blk = g4 * 4 + q
                                        TR(pb, pb.t[:, q * 128:(q + 1) * 128], Sb[g4], S.t[:, blk * 128:(blk + 1) * 128], identf, inc=(q == 3), extra=[CST])
                                    CP(SO.t[:, g4 * 4:(g4 + 1) * 4, :], pb.t[:, :].rearrange("p (a b) -> p a b", a=4), [pb], [SO])
                                k.dma(k.sp, osp[j, :, :].rearrange("(a p) n -> p a n", p=128), SO.t[:, :, :], reads=[SO], final=True)
                                OXS = [k.buf(f"ox{a}", [3, 1024], F32) for a in range(2)]
                                for g4 in range(4):
                                    OX = OXS[g4 % 2]
                                    for hf in range(2):
                                        pb = ps()
                                        for q in range(4):
                                            cc = g4 * 8 + hf * 4 + q
                                            TR(pb, pb.t[:3, q * 128:(q + 1) * 128], HISTb[cc], HIST.t[:, cc, :], identf, inc=(q == 3), extra=[CST])
                                        CP(OX.t[:, hf * 512:(hf + 1) * 512], pb.t[:3, :], [pb], [OX])
                                    k.dma(k.sp, oxp[j, :, g4 * 1024:(g4 + 1) * 1024], OX.t[:, :], reads=[OX], final=True)
                            with k.phase():
                                DBC = k.buf("dbc", [128, 16, 32], F32)
                                XDS = k.buf("xds", [128, 16, NSMP], F32)
                                YS = k.buf("ysm", [128, 16, NSMP], F32)
                                with k.phase():
                                    SEL = k.buf("sel", [32, 2048], F32)
                                    k.dma(k.sp, SEL.t[:, :], sel_d[:, :], writes=[SEL])
                                    DTF = k.buf("dtf", [32, 32], F32)
                                    e2 = k.buf("e2", [32, NSMP], F32)
                                    a32 = k.buf("a32", [32, 1], F32)
                                    ACT(a32.t[:, :], V(f"alog32{j}")[:32, :], AF.Exp, [VEC], [a32])
                                    TS(a32.t[:, :], a32.t[:, :], -1.0, None, ALU.mult, None, [a32], [a32])
                                    pd = ps()
                                    MM(pd, pd.t[:32, 0:NSMP], [(sld.t[:, kk * 32:(kk + 1) * 32], HN.t[:, kk, 512:NL]) for kk in range(8)], [sld, HN])
                                    ACT(e2.t[:, :], pd.t[:32, 0:NSMP], AF.Exp, [pd, VEC], [e2], bias=V(f"dtb32{j}")[:32, :])
                                    ACT(DTF.t[:, 0:16], e2.t[:, :], AF.Ln, [e2], [DTF], bias=1.0)
                                    ACT(DTF.t[:, 16:32], DTF.t[:, 0:16], AF.Exp, [DTF, a32], [DTF], scale=a32.t[:, :])
                                    pbq = ps()
                                    for hh in range(16):
                                        MM(pbq, pbq.t[:, hh * 32:(hh + 1) * 32], [(SEL.t[:, hh * 128:(hh + 1) * 128], DTF.t[:, :])], [SEL, DTF], inc=(hh == 15))
                                    CP(DBC.t[:, :, :], pbq.t[:, :].rearrange("p (a b) -> p a b", a=16), [pbq], [DBC])
                                TT(XDS.t[:, :, :], XB.t[:, 0:16, 512:NL], DBC.t[:, :, 0:16], ALU.mult, XBb[0:16] + [DBC], [XDS])
                                DB = k.buf("dgb", [128, 8, 128], BF16)
                                DC = k.buf("dgc", [128, 8, 128], BF16)
                                CS = k.buf("cbs", [128, 8, 128], BF16)
                                SS = [k.buf(f"ss{a}", [128, 16, 128], F32) for a in range(2)]
                                ssb = [k.sub(SS[a], 16, "h") for a in range(2)]
                                TA = k.buf("ta", [128, 16, 128], BF16)

                                def s_load(b):
                                    k.dma(k.sp, SS[b % 2].t[:, :, :], stss[j, b, :, :].rearrange("(a p) n -> p a n", p=128), writes=ssb[b % 2])

                                def s_prep(b):
                                    PTT(DB.t[:, :, :], identb.unsqueeze(1).to_broadcast([128, 8, 128]), XB.t[:, 16:24, 512 + b:512 + b + 1].to_broadcast([128, 8, 128]), ALU.mult, [CB] + XBb[16:24], [DB])
                                    PTT(DC.t[:, :, :], identb.unsqueeze(1).to_broadcast([128, 8, 128]), XB.t[:, 24:32, 512 + b:512 + b + 1].to_broadcast([128, 8, 128]), ALU.mult, [CB] + XBb[24:32], [DC])
                                    pbc = [ps() for _ in range(4)]
                                    for a in range(2):
                                        MM(pbc[a], pbc[a].t[:, :], [(onesb, DB.t[:, a * 4:(a + 1) * 4, :].rearrange("p a b -> p (a b)"))], [CB, DB])
                                    for a in range(2):
                                        MM(pbc[2 + a], pbc[2 + a].t[:, :], [(onesb, DC.t[:, a * 4:(a + 1) * 4, :].rearrange("p a b -> p (a b)"))], [CB, DC])
                                    return pbc

                                s_load(0)
                                pbc_next = s_prep(0)
                                for b in range(NSMP):
                                    ss = SS[b % 2]
                                    if b + 1 < NSMP:
                                        s_load(b + 1)
                                    pbc = pbc_next
                                    if b + 1 < NSMP:
                                        pbc_next = s_prep(b + 1)
                                    for hh in range(16):
                                        ACT(ss.t[:, hh, :], ss.t[:, hh, :], AF.Copy, [ssb[b % 2][hh], DBC], [ssb[b % 2][hh]], scale=DBC.t[:, hh, 16 + b:17 + b])
                                    for hh in range(16):
                                        g = hh // 2
                                        STT(ss.t[:, hh, :], pbc[g // 4].t[:, (g % 4) * 128:(g % 4 + 1) * 128], XDS.t[:, hh, b:b + 1], ss.t[:, hh, :], ALU.mult, ALU.add, [pbc[g // 4], XDS, ssb[b % 2][hh]], [ssb[b % 2][hh]])
                                    k.dma(k.sp, oss[j, b, :, :].rearrange("(a p) n -> p a n", p=128), ss.t[:, :, :], reads=ssb[b % 2], final=True)
                                    for a in range(2):
                                        ACT(CS.t[:, a * 4:(a + 1) * 4, :], pbc[2 + a].t[:, :].rearrange("p (g n) -> p g n", n=128), AF.Copy, [pbc[2 + a]], [CS])
                                    PTT(TA.t[:, :, :].rearrange("p (g e) n -> p g e n", e=2),
                                        ss.t[:, :, :].rearrange("p (g e) n -> p g e n", e=2),
                                        CS.t[:, :, :].unsqueeze(2).to_broadcast([128, 8, 2, 128]),
                                        ALU.mult, ssb[b % 2] + [CS], [TA])
                                    k.op(k.dve, lambda b=b: nc.vector.tensor_reduce(out=YS.t[:, :, b], in_=TA.t[:, :, :], axis=AX.X, op=ALU.add), [TA], [YS])
                                CP(YT.t[:, :, 512:NL], YS.t[:, :, :], [YS], YTb)
                        k.mark(f"  L{i} t{tix} core/sample end")
                        zi = 0
                        for it in range(8):
                            sl = wload([(win[j, :, it * 256:(it + 1) * 256], 8, 256, 0)])
                            for e in range(2):
                                zc = it * 2 + e
                                for (l, g, n) in lsubs:
                                    pz = ps()
                                    MM(pz, pz.t[:, :n], [(sl.t[:, kk * 256 + e * 128: kk * 256 + (e + 1) * 128], HN.t[:, kk, l:l + n]) for kk in range(8)], [sl, HN])
                                    gq = GS[zi % 2]
                                    zi += 1
                                    ACT(gq.t[:, 0, :n], pz.t[:, :n], AF.Silu, [pz], [gq])
                                    TS(gq.t[:, 1, :n], XB.t[:, zc, l:l + n], V(f"sD{j}", zc), None, ALU.mult, None, [XBb[zc], VEC, gq], [gq])
                                    TT(YT.t[:, zc, l:l + n], YT.t[:, zc, l:l + n], gq.t[:, 1, :n], ALU.add, [YTb[zc], gq], [YTb[zc]])
                                    TT(YT.t[:, zc, l:l + n], YT.t[:, zc, l:l + n], gq.t[:, 0, :n], ALU.mult, [YTb[zc], gq], [YTb[zc]])
                        k.mark(f"  L{i} t{tix} z end")
                        gjobs = [(g8, l, g, n) for g8 in range(8) for (l, g, n) in lsubs]

                        def gn_sq(idx):
                            g8, l, g, n = gjobs[idx]
                            gq = GS[idx % 2]
                            ACT(gq.t[:, 0:2, :n], YT.t[:, 2 * g8:2 * g8 + 2, l:l + n], AF.Square, YTb[2 * g8:2 * g8 + 2], [gq])
                            pn = ps()
                            MM(pn, pn.t[:, :n], [(onesb, gq.t[:, e, :n]) for e in range(2)], [gq, CB])
                            return pn

                        pn_next = gn_sq(0)
                        for idx, (g8, l, g, n) in enumerate(gjobs):
                            pn = pn_next
                            if idx + 1 < len(gjobs):
                                pn_next = gn_sq(idx + 1)
                            rt = rts[idx % 2]
                            RSTD(rt, n, pn.t[:, :n], pn, 1.0 / 256)
                            for e in range(2):
                                STT(YT.t[:, 2 * g8 + e, l:l + n], YT.t[:, 2 * g8 + e, l:l + n], V(f"snorm{j}", 2 * g8 + e), rt.t[:, :n], ALU.mult, ALU.mult, [YTb[2 * g8 + e], rt, VEC], [YTb[2 * g8 + e]])
                        for m in range(8):
                            sl = wload([(wout[j, :, m * 128:(m + 1) * 128], 16, 128, 0)])
                            for (l, g, n) in lsubs:
                                po = ps()
                                MM(po, po.t[:, :n], [(wv(sl, 0, kk, 128), YT.t[:, kk, l:l + n]) for kk in range(16)], [sl] + YTb)
                                TT(X.t[:, m, g:g + n], po.t[:, :n], X.t[:, m, g:g + n], ALU.add, [po, X1(m, g)], [X1(m, g)])

            k.mark("start")
            for i in range(DEPTH):
                ffn(i, 1)
                k.mark(f"L{i} ffn1 end")
                if i % 2 == 0:
                    conv_mixer(i, i // 2)
                else:
                    ssd_mixer(i, i // 2)
                k.mark(f"L{i} mixer end")
                ffn(i, 2)
                k.mark(f"L{i} ffn2 end")
                ple(i)
                k.mark(f"L{i} ple end")

            with k.phase():
                YN = k.buf("YN", [128, 8, NT], F32)
                sqs = [k.buf(f"sq{a}", [128, 8, 512], BF16) for a in range(2)]
                rts = [k.buf(f"rt{a}", [128, 512], F32) for a in range(2)]
                ost = [k.buf(f"yo{a}", [128, D], F32) for a in range(2)]
                YNb = k.sub(YN, 5, "t")
                rmsnorm("final_norm", lambda do: [YNb[tix_of(do)]], lambda kk, do, n: YN.t[:, kk, do:do + n], [(s, s, n) for s, n in TILES], sqs, rts)
                for r in range(17):
                    R = 128 if r < 16 else NSMP
                    o = ost[r % 2]
                    for half in range(2):
                        pb = ps()
                        for q in range(4):
                            c = half * 4 + q
                            TR(pb, pb.t[:R, q * 128:(q + 1) * 128], YNb[tix_of(r * 128)], YN.t[:, c, r * 128:r * 128 + R], identf, inc=(q == 3), extra=[CST])
                        if half == 0:
                            ACT(o.t[:R, 0:512], pb.t[:R, :], AF.Copy, [pb], [o])
                        else:
                            CP(o.t[:R, 512:1024], pb.t[:R, :], [pb], [o])
                    dst = yp[r * 128:(r + 1) * 128, :] if r < 16 else ysd[:, :]
                    k.dma(k.sp, dst, o.t[:R, :], reads=[o], final=True)
                if not k.dry:
                    for dep in k.final.values():
                        k._wait(k.sp, dep)
                    if k.ddcnt:
                        nc.sync.wait_ge(dsem_dd, k.ddcnt)

        k.dry = True
        k.wplan = []
        k.ddcnt = 0
        body()
        k.dry = False
        k.wissued = 0
        k.ddcnt = 0
        body()
        build.marks = getattr(k, "marks", [])
    return nc


_NC_CACHE = {}


def make_in_maps(inp):
    inp = {k_: np.asarray(v) for k_, v in inp.items()}
    vecs = np.ascontiguousarray(np.concatenate([a for (_, _, a) in vec_entries(inp)], axis=1), dtype=np.float32)
    cst = const_array()
    sel = sel_array()
    wnames = ["w_ffn1_gate", "w_ffn1_up", "w_ffn1_down", "w_ffn2_gate", "w_ffn2_up", "w_ffn2_down", "w_ple_gate", "w_ple_proj",
              "cm_w_in", "cm_w_out", "ssd_w_in", "ssd_w_out"]
    shared = {nm: np.ascontiguousarray(inp[nm], dtype=np.float32) for nm in wnames}
    shared.update(vecs=vecs, cst=cst, sel=sel)
    in_maps = []
    for c in range(NCORES):
        b0 = c * NSMP
        m = dict(shared)
        m["xp"] = np.ascontiguousarray(inp["x_prompt"][c], dtype=np.float32)
        m["xs"] = np.ascontiguousarray(inp["x_sample"][b0:b0 + NSMP, 0], dtype=np.float32)
        m["stc"] = np.ascontiguousarray(inp["state_conv"][:, b0:b0 + NSMP].reshape(2, NSMP * 30, D), dtype=np.float32)
        m["stsc"] = np.ascontiguousarray(inp["state_ssd_conv"][:, b0:b0 + NSMP].reshape(2, NSMP * 3, CONVD), dtype=np.float32)
        m["stss"] = np.ascontiguousarray(inp["state_ssd"][:, b0:b0 + NSMP].reshape(2, NSMP, NH * 64, 128), dtype=np.float32)
        m["pp"] = np.ascontiguousarray(inp["p_prompt"][:, c], dtype=np.float32)
        m["psm"] = np.ascontiguousarray(inp["p_sample"][:, b0:b0 + NSMP, 0], dtype=np.float32)
        in_maps.append(m)
    return in_maps


def kernel(**inp):
    if "nc" not in _NC_CACHE:
        _NC_CACHE["nc"] = build()
    nc = _NC_CACHE["nc"]
    in_maps = make_in_maps(inp)
    res = run_bass_kernel_spmd(nc, in_maps, core_ids=list(range(NCORES)))
    R = res.results
    y_prompt = np.stack([R[c]["yp"] for c in range(NCORES)], 0).astype(np.float32)
    y_sample = np.concatenate([R[c]["ys"] for c in range(NCORES)], 0).reshape(NCORES * NSMP, 1, D).astype(np.float32)
    conv_p = np.stack([R[c]["ocp"] for c in range(NCORES)], 1).astype(np.float32)
    xbc_p = np.stack([R[c]["oxp"] for c in range(NCORES)], 1).astype(np.float32)
    ssm_p = np.stack([R[c]["osp"].reshape(2, NH, 64, 128) for c in range(NCORES)], 1).astype(np.float32)
    conv_s = np.concatenate([R[c]["ocs"] for c in range(NCORES)], 1).astype(np.float32)
    xbc_s = np.concatenate([R[c]["oxs"] for c in range(NCORES)], 1).astype(np.float32)
    ssm_s = np.concatenate([R[c]["oss"].reshape(2, NSMP, NH, 64, 128) for c in range(NCORES)], 1).astype(np.float32)
    return (y_prompt, y_sample, conv_p, xbc_p, ssm_p, conv_s, xbc_s, ssm_s)
```

```python
import numpy as np
from contextlib import ExitStack, contextmanager
import concourse.bass as bass
import concourse.mybir as mybir
from concourse.bass_utils import run_bass_kernel_spmd

F32 = mybir.dt.float32
BF16 = mybir.dt.bfloat16
AF = mybir.ActivationFunctionType
ALU = mybir.AluOpType
AX = mybir.AxisListType
P = 128
D = 1024
DFF = 2816
T = 2048
NSMP = 16
NT = T + NSMP
DEPTH = 4
DIN = 2048
CONVD = 4096
NH = 32
PLE = 256
CK = 31
EPS = 1e-6
NSLOT = 5
SLOTE = 2048
NCORES = 8
TILES = [(0, 512), (512, 512), (1024, 512), (1536, 512), (2048, 16)]


def fm(v):
    v = np.asarray(v, np.float32).reshape(-1)
    return np.ascontiguousarray(v.reshape(-1, 128).T)


def vec_entries(inp):
    E = []

    def add(name, n, fn):
        if inp is None:
            E.append((name, n, None))
        else:
            a = np.zeros((128, n), np.float32)
            b = fn()
            a[: b.shape[0], :] = b
            E.append((name, n, a))

    for i in range(DEPTH):
        for nm in ("norm_ffn1", "norm_mix", "norm_ffn2", "norm_ple"):
            add(f"{nm}{i}", 8, lambda nm=nm, i=i: fm(inp[nm][i]))
    add("final_norm", 8, lambda: fm(inp["final_norm"]))
    for j in range(2):
        add(f"cbin{j}", 16, lambda j=j: fm(inp["cm_b_in"][j]))
        add(f"cdw{j}", 8 * CK, lambda j=j: np.asarray(inp["cm_dw"][j], np.float32).T.reshape(8, 128, CK).transpose(1, 0, 2).reshape(128, 8 * CK))
        add(f"cdwb{j}", 8, lambda j=j: fm(inp["cm_dw_b"][j]))
        add(f"clng{j}", 8, lambda j=j: fm(inp["cm_ln_g"][j]))
        add(f"clnb{j}", 8, lambda j=j: fm(inp["cm_ln_b"][j]))
        add(f"cbout{j}", 8, lambda j=j: fm(inp["cm_b_out"][j]))
    for j in range(2):
        add(f"scw{j}", 32 * 4, lambda j=j: np.asarray(inp["ssd_conv_w"][j], np.float32).T.reshape(32, 128, 4).transpose(1, 0, 2).reshape(128, 128))
        add(f"scb{j}", 32, lambda j=j: fm(inp["ssd_conv_b"][j]))
        add(f"snorm{j}", 16, lambda j=j: fm(inp["ssd_norm"][j]))
        add(f"sD{j}", 16, lambda j=j: np.repeat(np.asarray(inp["ssd_D"][j], np.float32).reshape(16, 2, 1), 64, axis=2).transpose(1, 2, 0).reshape(128, 16))
        add(f"dtb32{j}", 1, lambda j=j: np.asarray(inp["ssd_dt_bias"][j], np.float32).reshape(32, 1))
        add(f"alog32{j}", 1, lambda j=j: np.asarray(inp["ssd_A_log"][j], np.float32).reshape(32, 1))
        add(f"dtbbc{j}", 32, lambda j=j: np.tile(np.asarray(inp["ssd_dt_bias"][j], np.float32).reshape(1, 32), (128, 1)))
        add(f"alogbc{j}", 32, lambda j=j: np.tile(np.asarray(inp["ssd_A_log"][j], np.float32).reshape(1, 32), (128, 1)))
    return E


def vec_offsets():
    off = {}
    o = 0
    for name, n, _ in vec_entries(None):
        off[name] = o
        o += n
    return off, o


def const_array():
    c = np.zeros((128, 512), np.float32)
    i = np.arange(128)
    c[:, 0:128] = np.eye(128, dtype=np.float32)
    c[:, 128:256] = (i[:, None] <= i[None, :]).astype(np.float32)
    c[:, 256:384] = (i[:, None] > i[None, :]).astype(np.float32)
    c[:, 384:512] = 1.0
    return c


def sel_array():
    s = np.zeros((32, 16, 128), np.float32)
    for hh in range(16):
        for m in range(128):
            s[2 * hh + m // 64, hh, m] = 1.0
    return s.reshape(32, 2048)


class Buf:
    def __init__(self, t, name):
        self.t = t
        self.name = name
        self.w = None
        self.r = {}
        self.dsem = None


class Eng:
    def __init__(self, raw, sem, name):
        self.raw = raw
        self.sem = sem
        self.name = name
        self.cnt = 0
        self.seen = {}


class K:
    def __init__(self, nc, es):
        self.nc = nc
        self.es = es
        self.dry = False
        self.alloc = es
        self.phase_bufs = None
        self.sem_free = []
        self.sem_tot = {}
        self.pending = {}
        self.final = {}
        self.nsem = 0
        self.uid = 0

    def init_engines(self):
        nc, es = self.nc, self.es

        def mk(raw, name):
            return Eng(raw, es.enter_context(nc.semaphore("sem_" + name)), name)

        self.pe = mk(nc.tensor, "pe")
        self.act = mk(nc.scalar, "act")
        self.dve = mk(nc.vector, "dve")
        self.pool = mk(nc.gpsimd, "pool")
        self.sp = mk(nc.sync, "sp")
        self.engs = [self.pe, self.act, self.dve, self.pool, self.sp]
        for e in self.engs:
            e.cnt = 0
            e.seen = {}

    def buf(self, name, shape, dt):
        self.uid += 1
        t = self.alloc.enter_context(self.nc.sbuf_tensor(f"{name}_{self.uid}", list(shape), dt))
        b = Buf(t, name)
        if self.phase_bufs is not None:
            self.phase_bufs.append(b)
            b.in_phase = True
        return b

    def sub(self, b, n, tag=""):
        out = []
        for i in range(n):
            s = Buf(b.t, f"{b.name}{tag}{i}")
            s.in_phase = getattr(b, "in_phase", False)
            if self.phase_bufs is not None and s.in_phase:
                self.phase_bufs.append(s)
            out.append(s)
        return out

    def _getsem(self, b):
        if b.dsem is None:
            if self.sem_free and getattr(b, "in_phase", False):
                b.dsem = self.sem_free.pop()
            else:
                self.nsem += 1
                sem = self.es.enter_context(self.nc.semaphore(f"dsem{self.nsem}"))
                b.dsem = (f"dsem{self.nsem}", sem)
                self.sem_tot[b.dsem[0]] = 0
        return b.dsem

    def _wait(self, e, dep):
        if dep is None:
            return
        key, sem, val = dep
        if e.seen.get(key, 0) >= val:
            return
        e.raw.wait_ge(sem, val)
        e.seen[key] = val

    def _deps(self, e, reads, writes, isdma=False):
        for b in reads:
            self._wait(e, b.w)
        same_ok = (not isdma) and e.name == "pe"
        for b in writes:
            if b.w is not None and not (same_ok and b.w[0] == e.name):
                self._wait(e, b.w)
            for k, (sem, val) in b.r.items():
                if not (same_ok and k == e.name):
                    self._wait(e, (k, sem, val))

    def op(self, e, fn, reads=(), writes=(), inc=True):
        if self.dry:
            return None
        self._deps(e, reads, writes)
        ins = fn()
        self.opidx = getattr(self, "opidx", 0) + 1
        for b in reads:
            b.last_read = self.opidx
        for b in writes:
            b.last_write = self.opidx
        if e.name == "pe":
            self.npe = getattr(self, "npe", 0) + 1
        if inc:
            e.cnt += 1
            ins.then_inc(e.sem, 1)
            c = e.cnt
        else:
            c = e.cnt + 1
        for b in reads:
            b.r[e.name] = (e.sem, c)
        for b in writes:
            b.w = (e.name, e.sem, c)
            b.r = {}
        return ins

    def dma(self, q, out, in_, reads=(), writes=(), final=False, **kw):
        if self.dry:
            return
        self._deps(q, reads, writes, isdma=True)
        ins = q.raw.dma_start(out=out, in_=in_, **kw)
        b = (list(writes) + list(reads))[0]
        key, sem = self._getsem(b)
        self.sem_tot[key] += 16
        tot = self.sem_tot[key]
        ins.then_inc(sem, 16)
        for w in writes:
            w.w = (key, sem, tot)
            w.r = {}
        for r in reads:
            r.r[key] = (sem, tot)
        self.pending[key] = (key, sem, tot)
        if final:
            self.final[key] = (key, sem, tot)

    def mark(self, label):
        if not self.dry:
            self.marks = getattr(self, "marks", [])
            self.marks.append((label, getattr(self, "npe", 0)))

    def barrier(self):
        if self.dry:
            return
        engs = [self.pe, self.act, self.dve, self.sp, self.pool]
        for e in engs:
            for e2 in engs:
                if e2 is not e and e2.cnt > 0:
                    self._wait(e, (e2.name, e2.sem, e2.cnt))
            for dep in self.pending.values():
                self._wait(e, dep)
        self.pending = {}

    @contextmanager
    def phase(self):
        old_alloc, old_bufs = self.alloc, self.phase_bufs
        with ExitStack() as st:
            self.alloc = st
            self.phase_bufs = []
            yield
            self.barrier()
            for b in self.phase_bufs:
                if b.dsem is not None:
                    self.sem_free.append(b.dsem)
        self.alloc, self.phase_bufs = old_alloc, old_bufs


def build():
    nc = bass.Bass("TRN2", target_bir_lowering=False)
    voff, NV = vec_offsets()

    def din(name, shape):
        return nc.dram_tensor(name, list(shape), F32, kind="ExternalInput").ap()

    def dout(name, shape):
        return nc.dram_tensor(name, list(shape), F32, kind="ExternalOutput").ap()

    xp = din("xp", [T, D])
    xs = din("xs", [NSMP, D])
    stc = din("stc", [2, NSMP * 30, D])
    stsc = din("stsc", [2, NSMP * 3, CONVD])
    stss = din("stss", [2, NSMP, NH * 64, 128])
    pp = din("pp", [DEPTH, T, PLE])
    psm = din("psm", [DEPTH, NSMP, PLE])
    W = {}
    for nm, shp in [("w_ffn1_gate", [DEPTH, D, DFF]), ("w_ffn1_up", [DEPTH, D, DFF]), ("w_ffn1_down", [DEPTH, DFF, D]),
                    ("w_ffn2_gate", [DEPTH, D, DFF]), ("w_ffn2_up", [DEPTH, D, DFF]), ("w_ffn2_down", [DEPTH, DFF, D]),
                    ("w_ple_gate", [DEPTH, D, D]), ("w_ple_proj", [DEPTH, PLE, D]),
                    ("cm_w_in", [2, D, 2 * D]), ("cm_w_out", [2, D, D]),
                    ("ssd_w_in", [2, D, 6176]), ("ssd_w_out", [2, DIN, D])]:
        W[nm] = din(nm, shp)
    vecs_d = din("vecs", [128, NV])
    cst_d = din("cst", [128, 512])
    sel_d = din("sel", [32, 2048])
    yp = dout("yp", [T, D])
    ysd = dout("ys", [NSMP, D])
    ocp = dout("ocp", [2, 30, D])
    oxp = dout("oxp", [2, 3, CONVD])
    osp = dout("osp", [2, NH * 64, 128])
    ocs = dout("ocs", [2, NSMP, 30, D])
    oxs = dout("oxs", [2, NSMP, 3, CONVD])
    oss = dout("oss", [2, NSMP, NH * 64, 128])

    with ExitStack() as es:
        k = K(nc, es)
        X = k.buf("X", [128, 8, NT], F32)
        VEC = k.buf("VEC", [128, NV], F32)
        CST = k.buf("CST", [128, 512], F32)
        CB = k.buf("CB", [128, 256], BF16)
        k.wslots = [k.buf(f"wslot{i}", [128, SLOTE], BF16) for i in range(NSLOT)]
        PS = [Buf(es.enter_context(nc.psum_tensor(f"psum{i}", [128, 512], F32)), f"psum{i}") for i in range(8)]
        dsem_dd = es.enter_context(nc.semaphore("dsem_dd"))
        k.init_engines()
        k.psi = 0

        identf = CST.t[:, 0:128]
        trif = CST.t[:, 128:256]
        ustrf = CST.t[:, 256:384]
        onesf = CST.t[:, 384:512]
        identb = CB.t[:, 0:128]
        onesb = CB.t[:, 128:256]

        Xb = [k.sub(X, 5, f"c{m}t") for m in range(8)]

        def tix_of(col):
            return min(col // 512, 4)

        def XT(col):
            return [Xb[m][tix_of(col)] for m in range(8)]

        def X1(m, col):
            return Xb[m][tix_of(col)]

        def V(name, c=0, n=1):
            o = voff[name] + c
            return VEC.t[:, o:o + n]

        def ps():
            if k.dry:
                return PS[0]
            free = [b for b in PS if getattr(b, "last_read", 0) >= getattr(b, "last_write", 0)]
            if free:
                b = min(free, key=lambda b: getattr(b, "last_read", 0))
            else:
                b = min(PS, key=lambda b: getattr(b, "last_write", 0))
            k.opidx = getattr(k, "opidx", 0) + 1
            b.last_write = k.opidx
            return b

        def ACT(out, in_, func, R, Wr, **kw):
            k.op(k.act, lambda: nc.scalar.activation(out=out, in_=in_, func=func, **kw), R, Wr)

        def TT(out, in0, in1, op, R, Wr):
            k.op(k.dve, lambda: nc.vector.tensor_tensor(out=out, in0=in0, in1=in1, op=op), R, Wr)

        def STT(out, in0, scalar, in1, op0, op1, R, Wr):
            k.op(k.dve, lambda: nc.vector.scalar_tensor_tensor(out=out, in0=in0, scalar=scalar, in1=in1, op0=op0, op1=op1), R, Wr)

        def TS(out, in0, s1, s2, op0, op1, R, Wr):
            if op1 is None:
                k.op(k.dve, lambda: nc.vector.tensor_scalar(out=out, in0=in0, scalar1=s1, scalar2=None, op0=op0), R, Wr)
            else:
                k.op(k.dve, lambda: nc.vector.tensor_scalar(out=out, in0=in0, scalar1=s1, scalar2=s2, op0=op0, op1=op1), R, Wr)

        def CP(out, in_, R, Wr):
            k.op(k.dve, lambda: nc.vector.tensor_copy(out=out, in_=in_), R, Wr)

        def RSTD(rt, n, src_ap, srcB, scale):
            ACT(rt.t[:, :n], src_ap, AF.Ln, [srcB], [rt], scale=scale, bias=EPS)
            ACT(rt.t[:, :n], rt.t[:, :n], AF.Exp, [rt], [rt], scale=-0.5)

        def RCP(out, in_, R, Wr):
            k.op(k.dve, lambda: nc.vector.reciprocal(out=out, in_=in_), R, Wr)

        def MEMSET(out, val, Wr):
            k.op(k.dve, lambda: nc.vector.memset(out, val), (), Wr)

        def MM(ob, out, pairs, R, inc=True, first=True, final=True):
            n = len(pairs)
            for i, (l, r) in enumerate(pairs):
                last = i == n - 1
                k.op(k.pe, lambda l=l, r=r, i=i, last=last: nc.tensor.matmul(out, lhsT=l, rhs=r, start=(first and i == 0), stop=(final and last)),
                     R if i == 0 else (), [ob] if i == 0 else (), inc=(last and inc))

        def PTT(out, in0, in1, op, R, Wr):
            k.op(k.pool, lambda: nc.gpsimd.tensor_tensor(out=out, in0=in0, in1=in1, op=op), R, Wr)

        def TR(ob, out, ib, in_, ident, inc=True, extra=()):
            k.op(k.pe, lambda: nc.tensor.transpose(out, in_, ident), [ib] + list(extra), [ob], inc=inc)

        def wload(specs):
            j = k.wj
            k.wj += 1
            if k.dry:
                k.wplan.append(specs)
                return k.wslots[j % NSLOT]
            base = k.whold if k.whold is not None else j
            while k.wissued < min(len(k.wplan), base + NSLOT):
                jj = k.wissued
                slot = k.wslots[jj % NSLOT]
                for (src, kc, ncols, off) in k.wplan[jj]:
                    dst = slot.t[:, off:off + kc * ncols].rearrange("p (k c) -> p k c", k=kc)
                    k.dma(k.pool, dst, src.rearrange("(k p) c -> p k c", p=128), writes=[slot])
                k.wissued += 1
            return k.wslots[j % NSLOT]

        def wv(slot, off, kk, ncols):
            return slot.t[:, off + kk * ncols: off + (kk + 1) * ncols]

        def body():
            k.wj = 0
            k.psi = 0
            k.whold = None
            k.dma(k.sp, VEC.t[:, :], vecs_d[:, :], writes=[VEC])
            k.dma(k.sp, CST.t[:, :], cst_d[:, :], writes=[CST])
            CP(CB.t[:, 0:128], identf, [CST], [CB])
            CP(CB.t[:, 128:256], onesf, [CST], [CB])

            with k.phase():
                stg = [k.buf(f"stg{i}", [128, D], F32) for i in range(2)]
                for r in range(17):
                    st = stg[r % 2]
                    R = 128 if r < 16 else NSMP
                    src = xp[r * 128:(r + 1) * 128, :] if r < 16 else xs[:, :]
                    k.dma(k.sp, st.t[:R, :], src, writes=[st])
                    for half in range(2):
                        pb = ps()
                        for q in range(4):
                            c = half * 4 + q
                            TR(pb, pb.t[:, q * 128:q * 128 + R], st, st.t[:R, c * 128:(c + 1) * 128], identf[:R, :R], inc=(q == 3), extra=[CST])
                        src_v = pb.t[:, :].rearrange("p (a b) -> p a b", a=4)[:, :, :R]
                        dst_v = X.t[:, half * 4:half * 4 + 4, r * 128:r * 128 + R]
                        xw = [X1(m, r * 128) for m in range(half * 4, half * 4 + 4)]
                        if half == 0:
                            ACT(dst_v, src_v, AF.Copy, [pb], xw)
                        else:
                            CP(dst_v, src_v, [pb], xw)

            def rmsnorm(gname, dstB_fn, dst_fn, subs, sqs, rts):
                for ti, (so, do, n) in enumerate(subs):
                    sq = sqs[ti % len(sqs)]
                    rt = rts[ti % len(rts)]
                    ACT(sq.t[:, 0:8, :n], X.t[:, :, so:so + n], AF.Square, XT(so), [sq])
                    pb = ps()
                    MM(pb, pb.t[:, :n], [(onesb, sq.t[:, kk, :n]) for kk in range(8)], [sq, CB])
                    RSTD(rt, n, pb.t[:, :n], pb, 1.0 / D)
                    for kk in range(8):
                        STT(dst_fn(kk, do, n), X.t[:, kk, so:so + n], V(gname, kk), rt.t[:, :n], ALU.mult, ALU.mult, [X1(kk, so), rt, VEC], dstB_fn(do))

            def ffn(i, which):
                wg, wu, wd = W[f"w_ffn{which}_gate"], W[f"w_ffn{which}_up"], W[f"w_ffn{which}_down"]
                with k.phase():
                    XN = k.buf("XN", [128, 8, NT], BF16)
                    H = k.buf("H", [128, 11, NT], BF16)
                    sqs = [k.buf(f"sq{a}", [128, 8, 512], BF16) for a in range(2)]
                    rts = [k.buf(f"rt{a}", [128, 512], F32) for a in range(2)]
                    sgs = [k.buf(f"sg{a}", [128, 512], F32) for a in range(3)]
                    XNb = k.sub(XN, 5, "t")
                    Hb = k.sub(H, 5, "t")
                    rmsnorm(f"norm_ffn{which}{i}", lambda do: [XNb[tix_of(do)]], lambda kk, do, n: XN.t[:, kk, do:do + n], [(s, s, n) for s, n in TILES], sqs, rts)
                    sgi = [0]

                    def ffn_a(sl, fi, s, n):
                        pa = ps()
                        pbb = ps()
                        MM(pa, pa.t[:, :n], [(wv(sl, 0, kk, 128), XN.t[:, kk, s:s + n]) for kk in range(8)], [sl, XNb[tix_of(s)]])
                        MM(pbb, pbb.t[:, :n], [(wv(sl, 1024, kk, 128), XN.t[:, kk, s:s + n]) for kk in range(8)], [sl, XNb[tix_of(s)]])
                        sg = sgs[sgi[0] % 3]
                        sgi[0] += 1
                        ACT(sg.t[:, :n], pa.t[:, :n], AF.Silu, [pa], [sg])
                        TT(H.t[:, fi, s:s + n], pbb.t[:, :n], sg.t[:, :n], ALU.mult, [pbb, sg], [Hb[tix_of(s)]])

                    def ffn_w(f):
                        return wload([(wg[i, :, f * 128:(f + 1) * 128], 8, 128, 0), (wu[i, :, f * 128:(f + 1) * 128], 8, 128, 1024)])

                    for half in range(2):
                        fis = list(range(11))
                        if half == 0:
                            k.whold = k.wj
                            sls = [ffn_w(fi) for fi in range(4)]
                            for (s, n) in TILES:
                                for fi in range(4):
                                    ffn_a(sls[fi], fi, s, n)
                            k.whold = None
                            fis = list(range(4, 11))
                        for fi in fis:
                            sl = ffn_w(half * 11 + fi)
                            for (s, n) in TILES:
                                ffn_a(sl, fi, s, n)
                        for m in range(8):
                            r0 = half * 1408
                            sl = wload([(wd[i, r0:r0 + 1408, m * 128:(m + 1) * 128], 11, 128, 0)])
                            for (s, n) in TILES:
                                pc = ps()
                                MM(pc, pc.t[:, :n], [(wv(sl, 0, fi, 128), H.t[:, fi, s:s + n]) for fi in range(11)], [sl, Hb[tix_of(s)]])
                                STT(X.t[:, m, s:s + n], pc.t[:, :n], 0.5, X.t[:, m, s:s + n], ALU.mult, ALU.add, [pc, X1(m, s)], [X1(m, s)])

            def ple(i):
                with k.phase():
                    XN = k.buf("XN", [128, 8, NT], BF16)
                    PT = k.buf("PT", [128, 2, NT], BF16)
                    sqs = [k.buf(f"sq{a}", [128, 8, 512], BF16) for a in range(2)]
                    rts = [k.buf(f"rt{a}", [128, 512], F32) for a in range(2)]
                    sgs = [k.buf(f"sg{a}", [128, 512], F32) for a in range(3)]
                    t2s = [k.buf(f"t2{a}", [128, 512], F32) for a in range(2)]
                    stg = [k.buf(f"pstg{a}", [128, PLE], F32) for a in range(2)]
                    PTb = k.sub(PT, 5, "t")
                    for r in range(17):
                        st = stg[r % 2]
                        R = 128 if r < 16 else NSMP
                        src = pp[i, r * 128:(r + 1) * 128, :] if r < 16 else psm[i, :, :]
                        k.dma(k.sp, st.t[:R, :], src, writes=[st])
                        pb = ps()
                        for c in range(2):
                            TR(pb, pb.t[:, c * 128:c * 128 + R], st, st.t[:R, c * 128:(c + 1) * 128], identf[:R, :R], inc=(c == 1), extra=[CST])
                        ACT(PT.t[:, :, r * 128:r * 128 + R], pb.t[:, 0:256].rearrange("p (a b) -> p a b", a=2)[:, :, :R], AF.Copy, [pb], [PTb[tix_of(r * 128)]])
                    XNb = k.sub(XN, 5, "t")
                    rmsnorm(f"norm_ple{i}", lambda do: [XNb[tix_of(do)]], lambda kk, do, n: XN.t[:, kk, do:do + n], [(s, s, n) for s, n in TILES], sqs, rts)
                    ci = [0]

                    def ple_w(m):
                        return wload([(W["w_ple_gate"][i, :, m * 128:(m + 1) * 128], 8, 128, 0), (W["w_ple_proj"][i, :, m * 128:(m + 1) * 128], 2, 128, 1024)])

                    def ple_c(sl, m, s, n):
                        pg = ps()
                        pq = ps()
                        MM(pg, pg.t[:, :n], [(wv(sl, 0, kk, 128), XN.t[:, kk, s:s + n]) for kk in range(8)], [sl, XNb[tix_of(s)]])
                        MM(pq, pq.t[:, :n], [(wv(sl, 1024, kk, 128), PT.t[:, kk, s:s + n]) for kk in range(2)], [sl, PTb[tix_of(s)]])
                        sg = sgs[ci[0] % 3]
                        t2 = t2s[ci[0] % 2]
                        ci[0] += 1
                        ACT(sg.t[:, :n], pg.t[:, :n], AF.Sigmoid, [pg], [sg])
                        TT(t2.t[:, :n], pq.t[:, :n], sg.t[:, :n], ALU.mult, [pq, sg], [t2])
                        TT(X.t[:, m, s:s + n], X.t[:, m, s:s + n], t2.t[:, :n], ALU.add, [X1(m, s), t2], [X1(m, s)])

                    k.whold = k.wj
                    sls = [ple_w(m) for m in range(4)]
                    for (s, n) in TILES:
                        for m in range(4):
                            ple_c(sls[m], m, s, n)
                    k.whold = None
                    for m in range(4, 8):
                        sl = ple_w(m)
                        for (s, n) in TILES:
                            ple_c(sl, m, s, n)

            def conv_mixer(i, j):
                with k.phase():
                    XN = k.buf("XN", [128, 8, NT], BF16)
                    Vb = k.buf("Vb", [128, 8, NT], BF16)
                    rts = [k.buf(f"rt{a}", [128, 512], F32) for a in range(2)]
                    sgs = [k.buf(f"sg{a}", [128, 512], F32) for a in range(3)]
                    UL = k.buf("ul", [128, 8, 32], F32)
                    USN = k.buf("usn", [128, 8, NSMP], F32)
                    XNb = k.sub(XN, 5, "t")
                    Vbb = k.sub(Vb, 8, "c")
                    with k.phase():
                        sqs = [k.buf(f"sq{a}", [128, 8, 512], BF16) for a in range(2)]
                        rmsnorm(f"norm_mix{i}", lambda do: [XNb[tix_of(do)]], lambda kk, do, n: XN.t[:, kk, do:do + n], [(s, s, n) for s, n in TILES], sqs, rts)
                    c1 = k.phase()
                    c1.__enter__()
                    UE = [k.buf(f"ue{a}", [128, 30 + T], BF16) for a in range(2)]
                    US = [k.buf(f"us{a}", [128, NSMP, CK], F32) for a in range(2)]
                    DG = [k.buf(f"dg{a}", [128, CK, 128], BF16) for a in range(2)]
                    STSS = [k.buf(f"sts{a}", [128, 4, 128], F32) for a in range(2)]
                    tmp = k.buf("ctmp", [128, NSMP, CK], F32)
                    red = k.buf("cred", [128, NSMP], F32)
                    if not k.dry:
                        nc.sync.dma_start(out=ocs[j, :, 0:29, :], in_=stc[j, :, :].rearrange("(b r) c -> b r c", r=30)[:, 1:30, :]).then_inc(dsem_dd, 16)
                        k.ddcnt += 16
                    sgi = 0
                    for c in range(8):
                        sl = wload([(W["cm_w_in"][j, :, c * 128:(c + 1) * 128], 8, 128, 0), (W["cm_w_in"][j, :, D + c * 128:D + (c + 1) * 128], 8, 128, 1024)])
                        ue = UE[c % 2]
                        us = US[c % 2]
                        dg = DG[c % 2]
                        MEMSET(ue.t[:, 0:30], 0.0, [ue])
                        TT(dg.t[:, :, :], identb.unsqueeze(1).to_broadcast([128, CK, 128]), V(f"cdw{j}", c * CK, CK).unsqueeze(2).to_broadcast([128, CK, 128]), ALU.mult, [CB, VEC], [dg])
                        STS = STSS[c % 2]
                        k.dma(k.sp, STS.t[:, 0:3, :], stc[j, 0:384, c * 128:(c + 1) * 128].rearrange("(a p) c -> p a c", p=128), writes=[STS])
                        k.dma(k.sp, STS.t[:96, 3, :], stc[j, 384:480, c * 128:(c + 1) * 128], writes=[STS])
                        pb = ps()
                        for a in range(4):
                            R = 128 if a < 3 else 96
                            TR(pb, pb.t[:, a * 128:a * 128 + R], STS, STS.t[:R, a, :], identf[:R, :R], inc=(a == 3), extra=[CST])
                        CP(us.t[:, :, 0:30], pb.t[:, 0:480].rearrange("p (b r) -> p b r", r=30), [pb], [us])
                        for (s, n) in TILES:
                            pa = ps()
                            pg = ps()
                            MM(pa, pa.t[:, :n], [(wv(sl, 0, kk, 128), XN.t[:, kk, s:s + n]) for kk in range(8)], [sl, XNb[tix_of(s)]])
                            MM(pg, pg.t[:, :n], [(wv(sl, 1024, kk, 128), XN.t[:, kk, s:s + n]) for kk in range(8)], [sl, XNb[tix_of(s)]])
                            sg = sgs[sgi % 3]
                            sgi += 1
                            ACT(sg.t[:, :n], pg.t[:, :n], AF.Sigmoid, [pg, VEC], [sg], bias=V(f"cbin{j}", 8 + c))
                            if s < T:
                                STT(ue.t[:, 30 + s:30 + s + n], pa.t[:, :n], V(f"cbin{j}", c), sg.t[:, :n], ALU.add, ALU.mult, [pa, sg, VEC], [ue])
                                if s + n == T:
                                    STT(UL.t[:, c, :], pa.t[:, n - 32:n], V(f"cbin{j}", c), sg.t[:, n - 32:n], ALU.add, ALU.mult, [pa, sg, VEC], [UL])
                            else:
                                STT(us.t[:, :, 30], pa.t[:, :n], V(f"cbin{j}", c), sg.t[:, :n], ALU.add, ALU.mult, [pa, sg, VEC], [us])
                        for (s, n) in TILES[:4]:
                            pv = ps()
                            MM(pv, pv.t[:, :n], [(dg.t[:, kk, :], ue.t[:, s + kk:s + kk + n]) for kk in range(CK)], [dg, ue])
                            ACT(Vb.t[:, c, s:s + n], pv.t[:, :n], AF.Identity, [pv, VEC], [Vbb[c]], bias=V(f"cdwb{j}", c))
                        TT(tmp.t[:, :, :], us.t[:, :, :], V(f"cdw{j}", c * CK, CK).unsqueeze(1).to_broadcast([128, NSMP, CK]), ALU.mult, [us, VEC], [tmp])
                        k.op(k.dve, lambda: nc.vector.tensor_reduce(out=red.t[:, :], in_=tmp.t[:, :, :], axis=AX.X, op=ALU.add), [tmp], [red])
                        TS(Vb.t[:, c, T:NT], red.t[:, :], V(f"cdwb{j}", c), None, ALU.add, None, [red, VEC], [Vbb[c]])
                        CP(USN.t[:, c, :], us.t[:, :, 30], [us], [USN])
                    c1.__exit__(None, None, None)
                    c2 = k.phase()
                    c2.__enter__()
                    sqs = [k.buf(f"sq{a}", [128, 8, 512], BF16) for a in range(2)]
                    m1s = [k.buf(f"m1{a}", [128, 512], F32) for a in range(2)]
                    m2s = [k.buf(f"m2{a}", [128, 512], F32) for a in range(2)]
                    dts_ = [k.buf(f"dt{a}", [128, 512], F32) for a in range(2)]
                    for ti, (s, n) in enumerate(TILES):
                        sq = sqs[ti % 2]
                        rt = rts[ti % 2]
                        m1 = m1s[ti % 2]
                        m2 = m2s[ti % 2]
                        ACT(sq.t[:, :, :n], Vb.t[:, :, s:s + n], AF.Square, Vbb, [sq])
                        p1 = ps()
                        p2 = ps()
                        MM(p1, p1.t[:, :n], [(onesb, Vb.t[:, kk, s:s + n]) for kk in range(8)], Vbb + [CB])
                        MM(p2, p2.t[:, :n], [(onesb, sq.t[:, kk, :n]) for kk in range(8)], [sq, CB])
                        ACT(m1.t[:, :n], p1.t[:, :n], AF.Copy, [p1], [m1], scale=1.0 / D)
                        TT(m2.t[:, :n], m1.t[:, :n], m1.t[:, :n], ALU.mult, [m1], [m2])
                        STT(m2.t[:, :n], p2.t[:, :n], 1.0 / D, m2.t[:, :n], ALU.mult, ALU.subtract, [p2, m2], [m2])
                        RSTD(rt, n, m2.t[:, :n], m2, 1.0)
                        for kk in range(8):
                            dtb = dts_[kk % 2]
                            TT(dtb.t[:, :n], Vb.t[:, kk, s:s + n], m1.t[:, :n], ALU.subtract, [Vbb[kk], m1], [dtb])
                            TT(dtb.t[:, :n], dtb.t[:, :n], rt.t[:, :n], ALU.mult, [dtb, rt], [dtb])
                            ACT(XN.t[:, kk, s:s + n], dtb.t[:, :n], AF.Silu, [dtb, VEC], [XNb[tix_of(s)]], scale=V(f"clng{j}", kk), bias=V(f"clnb{j}", kk))
                    def co_w(m):
                        return wload([(W["cm_w_out"][j, :, m * 128:(m + 1) * 128], 8, 128, 0)])

                    def co_c(sl, m, s, n):
                        pc = ps()
                        MM(pc, pc.t[:, :n], [(wv(sl, 0, kk, 128), XN.t[:, kk, s:s + n]) for kk in range(8)], [sl, XNb[tix_of(s)]])
                        STT(X.t[:, m, s:s + n], pc.t[:, :n], V(f"cbout{j}", m), X.t[:, m, s:s + n], ALU.add, ALU.add, [pc, X1(m, s), VEC], [X1(m, s)])

                    k.whold = k.wj
                    sls = [co_w(m) for m in range(4)]
                    for (s, n) in TILES:
                        for m in range(4):
                            co_c(sls[m], m, s, n)
                    k.whold = None
                    for m in range(4, 8):
                        sl = co_w(m)
                        for (s, n) in TILES:
                            co_c(sl, m, s, n)
                    c2.__exit__(None, None, None)
                    OST = k.buf("ost", [32, D], F32)
                    for half in range(2):
                        pb = ps()
                        for q in range(4):
                            c = half * 4 + q
                            TR(pb, pb.t[:32, q * 128:(q + 1) * 128], UL, UL.t[:, c, :], identf, inc=(q == 3), extra=[CST])
                        CP(OST.t[:32, half * 512:(half + 1) * 512], pb.t[:32, :], [pb], [OST])
                    k.dma(k.sp, ocp[j, :, :], OST.t[2:32, :], reads=[OST], final=True)
                    OS2 = k.buf("os2", [NSMP, D], F32)
                    for half in range(2):
                        pb = ps()
                        for q in range(4):
                            c = half * 4 + q
                            TR(pb, pb.t[:NSMP, q * 128:(q + 1) * 128], USN, USN.t[:, c, :], identf, inc=(q == 3), extra=[CST])
                        CP(OS2.t[:, half * 512:(half + 1) * 512], pb.t[:NSMP, :], [pb], [OS2])
                    k.dma(k.sp, ocs[j, :, 29, :], OS2.t[:, :], reads=[OS2], final=True)

            def ssd_mixer(i, j):
                win, wout = W["ssd_w_in"], W["ssd_w_out"]
                with k.phase():
                    NL = 512 + NSMP
                    HN = k.buf("HN", [128, 8, NL], BF16)
                    XB = k.buf("XB", [128, 32, NL], BF16)
                    YT = k.buf("YT", [128, 16, NL], BF16)
                    XBb = k.sub(XB, 32, "c")
                    YTb = k.sub(YT, 16, "c")
                    PRE = [k.buf(f"pre{a}", [128, 515], BF16) for a in range(2)]
                    DGS = [k.buf(f"dgs{a}", [128, 4, 128], BF16) for a in range(2)]
                    HIST = k.buf("hist", [128, 32, 3], F32)
                    HISTb = k.sub(HIST, 32, "c")
                    GS = [k.buf(f"gsq{a}", [128, 2, 512], BF16) for a in range(2)]
                    rts = [k.buf(f"rt{a}", [128, 512], F32) for a in range(2)]
                    sgs = rts
                    ABC = k.buf("abc", [128, 32], F32)
                    S = k.buf("S", [128, DIN], F32)
                    SB = k.buf("SB", [128, DIN], BF16)
                    Sb = k.sub(S, 4, "q")
                    SBb = k.sub(SB, 4, "q")
                    MEMSET(HIST.t[:, :, :], 0.0, HISTb)
                    MEMSET(S.t[:, :], 0.0, Sb)
                    MEMSET(SB.t[:, :], 0.0, SBb)
                    ACT(ABC.t[:, :], V(f"alogbc{j}", 0, 32), AF.Exp, [VEC], [ABC])
                    TS(ABC.t[:, :], ABC.t[:, :], -1.0, None, ALU.mult, None, [ABC], [ABC])
                    ci = 0
                    gi = 0
                    for tix in range(4):
                        s0 = tix * 512
                        smp = tix == 3
                        lsubs = [(0, s0, 512)] + ([(512, T, NSMP)] if smp else [])
                        for (l, g, n) in lsubs:
                            pb = ps()
                            for rnd in range(4):
                                sq = GS[gi % 2]
                                gi += 1
                                ACT(sq.t[:, 0:2, :n], X.t[:, 2 * rnd:2 * rnd + 2, g:g + n], AF.Square, [X1(2 * rnd, g), X1(2 * rnd + 1, g)], [sq])
                                MM(pb, pb.t[:, :n], [(onesb, sq.t[:, e, :n]) for e in range(2)], [sq, CB], first=(rnd == 0), final=(rnd == 3))
                            rt = rts[0]
                            RSTD(rt, n, pb.t[:, :n], pb, 1.0 / D)
                            for kk in range(8):
                                STT(HN.t[:, kk, l:l + n], X.t[:, kk, g:g + n], V(f"norm_mix{i}", kk), rt.t[:, :n], ALU.mult, ALU.mult, [X1(kk, g), rt, VEC], [HN])
                        if smp:
                            xph = k.phase()
                            xph.__enter__()
                            STXS = [k.buf(f"stx{a}", [48, 1024], F32) for a in range(2)]
                            STT_ = k.buf("stT", [128, 32, NSMP, 3], F32)
                            NEWP = k.buf("newp", [128, 32, NSMP], F32)
                            if not k.dry:
                                nc.sync.dma_start(out=oxs[j, :, 0:2, :], in_=stsc[j, :, :].rearrange("(b r) c -> b r c", r=3)[:, 1:3, :]).then_inc(dsem_dd, 16)
                                k.ddcnt += 16
                            for g4 in range(4):
                                STX = STXS[g4 % 2]
                                k.dma(k.sp, STX.t[:, :], stsc[j, :, g4 * 1024:(g4 + 1) * 1024], writes=[STX])
                                pb = ps()
                                for q in range(8):
                                    TR(pb, pb.t[:, q * 48:(q + 1) * 48], STX, STX.t[:, q * 128:(q + 1) * 128], identf[:48, :48], inc=(q == 7), extra=[CST])
                                CP(STT_.t[:, g4 * 8:(g4 + 1) * 8, :, :], pb.t[:, 0:384].rearrange("p (c b r) -> p c b r", c=8, b=NSMP), [pb], [STT_])
                            ctm = k.buf("ctm", [128, NSMP, 3], F32)
                            cr = k.buf("cr", [128, NSMP], F32)
                        sls = {}

                        def emit_proj(cc):
                            it, e = divmod(cc, 2)
                            if e == 0:
                                sls[it] = wload([(win[j, :, DIN + it * 256: DIN + (it + 1) * 256], 8, 256, 0)])
                            sl = sls[it]
                            lw = [sl.t[:, kk * 256 + e * 128: kk * 256 + (e + 1) * 128] for kk in range(8)]
                            pa = ps()
                            MM(pa, pa.t[:, :512], [(lw[kk], HN.t[:, kk, 0:512]) for kk in range(8)], [sl, HN])
                            pq = None
                            if smp:
                                pq = ps()
                                MM(pq, pq.t[:, :NSMP], [(lw[kk], HN.t[:, kk, 512:NL]) for kk in range(8)], [sl, HN])
                            return pa, pq

                        stA = {}
                        stB = {}
                        stC = {}

                        def stage_B(cc):
                            nonlocal ci
                            pa, pq = stA.pop(cc)
                            pre = PRE[ci % 2]
                            dgs = DGS[ci % 2]
                            ci += 1
                            TT(dgs.t[:, :, :], identb.unsqueeze(1).to_broadcast([128, 4, 128]), V(f"scw{j}", cc * 4, 4).unsqueeze(2).to_broadcast([128, 4, 128]), ALU.mult, [CB, VEC], [dgs])
                            CP(pre.t[:, 0:3], HIST.t[:, cc, :], [HISTb[cc]], [pre])
                            ACT(pre.t[:, 3:515], pa.t[:, :512], AF.Copy, [pa], [pre])
                            CP(HIST.t[:, cc, :], pa.t[:, 509:512], [pa], [HISTb[cc]])
                            if smp:
                                CP(NEWP.t[:, cc, :], pq.t[:, :NSMP], [pq], [NEWP])
                                TT(ctm.t[:, :, :], STT_.t[:, cc, :, :], V(f"scw{j}", cc * 4, 3).unsqueeze(1).to_broadcast([128, NSMP, 3]), ALU.mult, [STT_, VEC], [ctm])
                                k.op(k.dve, lambda: nc.vector.tensor_reduce(out=cr.t[:, :], in_=ctm.t[:, :, :], axis=AX.X, op=ALU.add), [ctm], [cr])
                                STT(cr.t[:, :], pq.t[:, :NSMP], V(f"scw{j}", cc * 4 + 3), cr.t[:, :], ALU.mult, ALU.add, [pq, cr, VEC], [cr])
                                ACT(XB.t[:, cc, 512:NL], cr.t[:, :], AF.Silu, [cr, VEC], [XBb[cc]], bias=V(f"scb{j}", cc))
                            stB[cc] = (pre, dgs)

                        def stage_C(cc):
                            pre, dgs = stB.pop(cc)
                            pc = ps()
                            MM(pc, pc.t[:, :512], [(dgs.t[:, kk, :], pre.t[:, kk:kk + 512]) for kk in range(4)], [dgs, pre])
                            stC[cc] = pc

                        def stage_D(cc):
                            pc = stC.pop(cc)
                            ACT(XB.t[:, cc, 0:512], pc.t[:, :512], AF.Silu, [pc, VEC], [XBb[cc]], bias=V(f"scb{j}", cc))

                        for it_ in range(-2, 33):
                            if 0 <= it_ + 2 < 32:
                                stA[it_ + 2] = emit_proj(it_ + 2)
                            if 0 <= it_ + 1 < 32:
                                stage_B(it_ + 1)
                            if 0 <= it_ < 32:
                                stage_C(it_)
                            if 0 <= it_ - 1 < 32:
                                stage_D(it_ - 1)
                        if smp:
                            OX2 = k.buf("ox2", [NSMP, 1024], F32)
                            for g4 in range(4):
                                for hf in range(2):
                                    pb = ps()
                                    for q in range(4):
                                        cc = g4 * 8 + hf * 4 + q
                                        TR(pb, pb.t[:NSMP, q * 128:(q + 1) * 128], NEWP, NEWP.t[:, cc, :], identf, inc=(q == 3), extra=[CST])
                                    CP(OX2.t[:, hf * 512:(hf + 1) * 512], pb.t[:NSMP, :], [pb], [OX2])
                                k.dma(k.sp, oxs[j, :, 2, g4 * 1024:(g4 + 1) * 1024], OX2.t[:, :], reads=[OX2], final=True)
                            xph.__exit__(None, None, None)
                        k.mark(f"  L{i} t{tix} xBC+conv end")
                        sld = wload([(win[j, :, DIN + CONVD: DIN + CONVD + 32], 8, 32, 0)])
                        with k.phase():
                            Rg = [k.buf(f"Rg{a}", [128, 4, 128], F32) for a in range(3)]
                            Lg = [k.buf(f"Lg{a}", [128, 4, 128], BF16) for a in range(3)]
                            MTg = [k.buf(f"MT{a}", [128, 4, 128], BF16) for a in range(3)]
                            CBM = k.buf("cbm", [128, 8, 128], BF16)
                            CBMb = k.sub(CBM, 8, "g")
                            XDT = k.buf("xdt", [128, DIN], BF16)
                            XDD = k.buf("xdd", [128, DIN], BF16)
                            XDTb = k.sub(XDT, 2, "h")
                            XDDb = k.sub(XDD, 2, "h")
                            BTM = k.buf("btm", [128, 1024], BF16)
                            YTM = k.buf("ytm", [128, DIN], BF16)
                            YTMb = k.sub(YTM, 4, "q")
                            T1s = [k.buf(f"t1{a}", [128, 512], BF16) for a in range(1)]
                            sms = [{nm: k.buf(nm + str(a), [128, 32], F32) for nm in ("dt", "a", "acs", "ea", "cd", "dd", "dec", "dtd", "e1")} for a in range(2)]

                            def prologue_stage(q, st):
                                sm = sms[q % 2]
                                lo = q * 128
                                if st == 0:
                                    pd = ps()
                                    MM(pd, pd.t[:, 0:32], [(HN.t[:, kk, lo:lo + 128], sld.t[:, kk * 32:(kk + 1) * 32]) for kk in range(8)], [sld, HN])
                                    TT(sm["e1"].t[:, :], pd.t[:, 0:32], V(f"dtbbc{j}", 0, 32), ALU.add, [pd, VEC], [sm["e1"]])
                                elif st == 1:
                                    ACT(sm["e1"].t[:, :], sm["e1"].t[:, :], AF.Exp, [sm["e1"]], [sm["e1"]])
                                    ACT(sm["dt"].t[:, :], sm["e1"].t[:, :], AF.Ln, [sm["e1"]], [sm["dt"]], bias=1.0)
                                elif st == 2:
                                    TT(sm["a"].t[:, :], sm["dt"].t[:, :], ABC.t[:, :], ALU.mult, [sm["dt"], ABC], [sm["a"]])
                                elif st == 3:
                                    pcs = ps()
                                    sm["pcs"] = pcs
                                    MM(pcs, pcs.t[:, 0:32], [(trif, sm["a"].t[:, :])], [CST, sm["a"]], inc=False)
                                    MM(pcs, pcs.t[:, 32:64], [(onesf, sm["a"].t[:, :])], [CST, sm["a"]])
                                elif st == 4:
                                    pcs = sm["pcs"]
                                    ACT(sm["acs"].t[:, :], pcs.t[:, 0:32], AF.Copy, [pcs], [sm["acs"]])
                                    ACT(sm["ea"].t[:, :], pcs.t[:, 0:32], AF.Exp, [pcs], [sm["ea"]])
                                    ACT(sm["cd"].t[:, :], pcs.t[:, 32:64], AF.Exp, [pcs], [sm["cd"]])
                                elif st == 5:
                                    pcs = sm["pcs"]
                                    TT(sm["dd"].t[:, :], pcs.t[:, 32:64], sm["acs"].t[:, :], ALU.subtract, [pcs, sm["acs"]], [sm["dd"]])
                                elif st == 6:
                                    ACT(sm["dec"].t[:, :], sm["dd"].t[:, :], AF.Exp, [sm["dd"]], [sm["dec"]])
                                elif st == 7:
                                    TT(sm["dtd"].t[:, :], sm["dt"].t[:, :], sm["dec"].t[:, :], ALU.mult, [sm["dt"], sm["dec"]], [sm["dtd"]])

                            def prologue(q):
                                for st in range(8):
                                    prologue_stage(q, st)

                            prologue(0)
                            for q in range(4):
                                sm = sms[q % 2]
                                lo = q * 128
                                pts = []
                                for hb in range(2):
                                    pt = ps()
                                    ptb = pt.t[:, :].bitcast(BF16)
                                    for e in range(8):
                                        hh = hb * 8 + e
                                        TR(pt, ptb[:, e * 128:(e + 1) * 128], XBb[hh], XB.t[:, hh, lo:lo + 128], identb, inc=(e == 7), extra=[CB])
                                    pts.append((pt, ptb))
                                ptB = ps()
                                ptBb = ptB.t[:, :].bitcast(BF16)
                                for g in range(8):
                                    TR(ptB, ptBb[:, g * 128:(g + 1) * 128], XBb[16 + g], XB.t[:, 16 + g, lo:lo + 128], identb, inc=(g == 7), extra=[CB])
                                pcbs = []
                                for hf in range(2):
                                    pcb = ps()
                                    for e in range(4):
                                        g = hf * 4 + e
                                        MM(pcb, pcb.t[:, e * 128:(e + 1) * 128], [(XB.t[:, 16 + g, lo:lo + 128], XB.t[:, 24 + g, lo:lo + 128])], [XBb[16 + g], XBb[24 + g]], inc=(e == 3))
                                    pcbs.append(pcb)
                                rg_of = {}

                                def emit_R(g):
                                    rg = Rg[g % 3]
                                    rg_of[g] = rg
                                    PTT(rg.t[:, :, :], sm["a"].t[:, g * 4:(g + 1) * 4].unsqueeze(2).to_broadcast([128, 4, 128]), trif.unsqueeze(1).to_broadcast([128, 4, 128]), ALU.mult, [sm["a"], CST], [rg])

                                for g in range(3):
                                    emit_R(g)
                                for hb in range(2):
                                    pt, ptb = pts[hb]
                                    pv3 = ptb.rearrange("p (h d) -> p h d", d=64)
                                    TT(XDT.t[:, hb * 1024:(hb + 1) * 1024].rearrange("p (h d) -> p h d", d=64), pv3, sm["dt"].t[:, hb * 16:(hb + 1) * 16].unsqueeze(2).to_broadcast([128, 16, 64]), ALU.mult, [pt, sm["dt"]], [XDTb[hb]])
                                    TT(XDD.t[:, hb * 1024:(hb + 1) * 1024].rearrange("p (h d) -> p h d", d=64), pv3, sm["dtd"].t[:, hb * 16:(hb + 1) * 16].unsqueeze(2).to_broadcast([128, 16, 64]), ALU.mult, [pt, sm["dtd"]], [XDDb[hb]])
                                ACT(BTM.t[:, :], ptBb, AF.Copy, [ptB], [BTM])
                                for hf in range(2):
                                    TT(CBM.t[:, hf * 4:(hf + 1) * 4, :], pcbs[hf].t[:, :].rearrange("p (a b) -> p a b", a=4), trif.unsqueeze(1).to_broadcast([128, 4, 128]), ALU.mult, [pcbs[hf], CST], CBMb[hf * 4:(hf + 1) * 4])
                                psegs = {}

                                def emit_seg(g):
                                    rg = rg_of[g]
                                    pseg = ps()
                                    MM(pseg, pseg.t[:, :], [(ustrf, rg.t[:, :, :].rearrange("p a b -> p (a b)"))], [CST, rg])
                                    if g + 3 < 8:
                                        emit_R(g + 3)
                                    lg = Lg[g % 3]
                                    ACT(lg.t[:, :, :].rearrange("p a b -> p (a b)"), pseg.t[:, :], AF.Exp, [pseg], [lg])
                                    mt = MTg[g % 3]
                                    TT(mt.t[:, :, :], lg.t[:, :, :], CBM.t[:, g, :].unsqueeze(1).to_broadcast([128, 4, 128]), ALU.mult, [lg, CBMb[g]], [mt])
                                    return mt

                                mts = {0: emit_seg(0), 1: emit_seg(1)}
                                pyd = pyo = None
                                for g in range(8):
                                    b4, e2_ = divmod(g, 2)
                                    if q + 1 < 4:
                                        prologue_stage(q + 1, g)
                                    if g + 2 < 8:
                                        mts[g + 2] = emit_seg(g + 2)
                                    if e2_ == 0:
                                        pyd = ps()
                                        pyo = ps()
                                    mt = mts[g]
                                    for r4 in range(4):
                                        h = g * 4 + r4
                                        e = e2_ * 4 + r4
                                        MM(pyd, pyd.t[:, e * 64:(e + 1) * 64], [(mt.t[:, r4, :], XDT.t[:, h * 64:(h + 1) * 64])], [mt, XDTb[h // 16]], inc=(r4 == 3))
                                    MM(pyo, pyo.t[:, e2_ * 256:(e2_ + 1) * 256], [(XB.t[:, 24 + g, lo:lo + 128], SB.t[:, g * 256:(g + 1) * 256])], [XBb[24 + g], SBb[b4]])
                                    if e2_ == 1:
                                        T1 = T1s[0]
                                        TT(T1.t[:, :].rearrange("p (h d) -> p h d", d=64), pyo.t[:, :].rearrange("p (h d) -> p h d", d=64), sm["ea"].t[:, b4 * 8:(b4 + 1) * 8].unsqueeze(2).to_broadcast([128, 8, 64]), ALU.mult, [pyo, sm["ea"]], [T1])
                                        TT(YTM.t[:, b4 * 512:(b4 + 1) * 512], pyd.t[:, :], T1.t[:, :], ALU.add, [pyd, T1], [YTMb[b4]])
                                for b4 in range(4):
                                    pst = ps()
                                    for e in range(2):
                                        g = b4 * 2 + e
                                        MM(pst, pst.t[:, e * 256:(e + 1) * 256], [(BTM.t[:, g * 128:(g + 1) * 128], XDD.t[:, g * 256:(g + 1) * 256])], [BTM, XDDb[g // 4]], inc=(e == 1))
                                    sv = S.t[:, b4 * 512:(b4 + 1) * 512]
                                    TT(sv.rearrange("p (h d) -> p h d", d=64), sv.rearrange("p (h d) -> p h d", d=64), sm["cd"].t[:, b4 * 8:(b4 + 1) * 8].unsqueeze(2).to_broadcast([128, 8, 64]), ALU.mult, [Sb[b4], sm["cd"]], [Sb[b4]])
                                    TT(sv, pst.t[:, :], sv, ALU.add, [pst, Sb[b4]], [Sb[b4]])
                                    ACT(SB.t[:, b4 * 512:(b4 + 1) * 512], sv, AF.Copy, [Sb[b4]], [SBb[b4]])
                                for hb in range(2):
                                    pt = ps()
                                    ptb = pt.t[:, :].bitcast(BF16)
                                    for e in range(8):
                                        hh = hb * 8 + e
                                        TR(pt, ptb[:, e * 128:(e + 1) * 128], YTMb[hh // 4], YTM.t[:, hh * 128:(hh + 1) * 128], identb, inc=(e == 7), extra=[CB])
                                    ACT(YT.t[:, hb * 8:(hb + 1) * 8, lo:lo + 128], ptb.rearrange("p (a b) -> p a b", a=8), AF.Copy, [pt], YTb[hb * 8:(hb + 1) * 8])
                                k.mark(f"    L{i} t{tix} chunk{q} end")
                        if smp:
                            with k.phase():
                                SO = k.buf("so", [128, 16, 128], F32)
                                for g4 in range(4):
                                    pb = ps()
                                    for q in range(4):
                                        blk = g4 * 4 + q
                                        TR(pb, pb.t[:, q * 128:(q + 1) * 128], Sb[g4], S.t[:, blk * 128:(blk + 1) * 128], identf, inc=(q == 3), extra=[CST])
                                    CP(SO.t[:, g4 * 4:(g4 + 1) * 4, :], pb.t[:, :].rearrange("p (a b) -> p a b", a=4), [pb], [SO])
                                k.dma(k.sp, osp[j, :, :].rearrange("(a p) n -> p a n", p=128), SO.t[:, :, :], reads=[SO], final=True)
                                OXS = [k.buf(f"ox{a}", [3, 1024], F32) for a in range(2)]
                                for g4 in range(4):
                                    OX = OXS[g4 % 2]
                                    for hf in range(2):
                                        pb = ps()
                                        for q in range(4):
                                            cc = g4 * 8 + hf * 4 + q
                                            TR(pb, pb.t[:3, q * 128:(q + 1) * 128], HISTb[cc], HIST.t[:, cc, :], identf, inc=(q == 3), extra=[CST])
                                        CP(OX.t[:, hf * 512:(hf + 1) * 512], pb.t[:3, :], [pb], [OX])
                                    k.dma(k.sp, oxp[j, :, g4 * 1024:(g4 + 1) * 1024], OX.t[:, :], reads=[OX], final=True)
                            with k.phase():
                                DBC = k.buf("dbc", [128, 16, 32], F32)
                                XDS = k.buf("xds", [128, 16, NSMP], F32)
                                YS = k.buf("ysm", [128, 16, NSMP], F32)
                                with k.phase():
                                    SEL = k.buf("sel", [32, 2048], F32)
                                    k.dma(k.sp, SEL.t[:, :], sel_d[:, :], writes=[SEL])
                                    DTF = k.buf("dtf", [32, 32], F32)
                                    e2 = k.buf("e2", [32, NSMP], F32)
                                    a32 = k.buf("a32", [32, 1], F32)
                                    ACT(a32.t[:, :], V(f"alog32{j}")[:32, :], AF.Exp, [VEC], [a32])
                                    TS(a32.t[:, :], a32.t[:, :], -1.0, None, ALU.mult, None, [a32], [a32])
                                    pd = ps()
                                    MM(pd, pd.t[:32, 0:NSMP], [(sld.t[:, kk * 32:(kk + 1) * 32], HN.t[:, kk, 512:NL]) for kk in range(8)], [sld, HN])
                                    ACT(e2.t[:, :], pd.t[:32, 0:NSMP], AF.Exp, [pd, VEC], [e2], bias=V(f"dtb32{j}")[:32, :])
                                    ACT(DTF.t[:, 0:16], e2.t[:, :], AF.Ln, [e2], [DTF], bias=1.0)
                                    ACT(DTF.t[:, 16:32], DTF.t[:, 0:16], AF.Exp, [DTF, a32], [DTF], scale=a32.t[:, :])
                                    pbq = ps()
                                    for hh in range(16):
                                        MM(pbq, pbq.t[:, hh * 32:(hh + 1) * 32], [(SEL.t[:, hh * 128:(hh + 1) * 128], DTF.t[:, :])], [SEL, DTF], inc=(hh == 15))
                                    CP(DBC.t[:, :, :], pbq.t[:, :].rearrange("p (a b) -> p a b", a=16), [pbq], [DBC])
                                TT(XDS.t[:, :, :], XB.t[:, 0:16, 512:NL], DBC.t[:, :, 0:16], ALU.mult, XBb[0:16] + [DBC], [XDS])
                                DB = k.buf("dgb", [128, 8, 128], BF16)
                                DC = k.buf("dgc", [128, 8, 128], BF16)
                                CS = k.buf("cbs", [128, 8, 128], BF16)
                                SS = [k.buf(f"ss{a}", [128, 16, 128], F32) for a in range(2)]
                                ssb = [k.sub(SS[a], 16, "h") for a in range(2)]
                                TAs = [k.buf(f"ta{a}", [128, 16, 128], BF16) for a in range(1)]

                                def s_load(b):
                                    k.dma(k.sp, SS[b % 2].t[:, :, :], stss[j, b, :, :].rearrange("(a p) n -> p a n", p=128), writes=ssb[b % 2])

                                def s_prep(b):
                                    PTT(DB.t[:, :, :], identb.unsqueeze(1).to_broadcast([128, 8, 128]), XB.t[:, 16:24, 512 + b:512 + b + 1].to_broadcast([128, 8, 128]), ALU.mult, [CB] + XBb[16:24], [DB])
                                    PTT(DC.t[:, :, :], identb.unsqueeze(1).to_broadcast([128, 8, 128]), XB.t[:, 24:32, 512 + b:512 + b + 1].to_broadcast([128, 8, 128]), ALU.mult, [CB] + XBb[24:32], [DC])
                                    pbc = [ps() for _ in range(4)]
                                    for a in range(2):
                                        MM(pbc[a], pbc[a].t[:, :], [(onesb, DB.t[:, a * 4:(a + 1) * 4, :].rearrange("p a b -> p (a b)"))], [CB, DB])
                                    for a in range(2):
                                        MM(pbc[2 + a], pbc[2 + a].t[:, :], [(onesb, DC.t[:, a * 4:(a + 1) * 4, :].rearrange("p a b -> p (a b)"))], [CB, DC])
                                    return pbc

                                s_load(0)
                                pbc_next = s_prep(0)
                                def s_reduce(b):
                                    k.op(k.dve, lambda b=b: nc.vector.tensor_reduce(out=YS.t[:, :, b], in_=TAs[0].t[:, :, :], axis=AX.X, op=ALU.add), [TAs[0]], [YS])

                                for b in range(NSMP):
                                    ss = SS[b % 2]
                                    TA = TAs[0]
                                    if b + 1 < NSMP:
                                        s_load(b + 1)
                                    pbc = pbc_next
                                    if b + 1 < NSMP:
                                        pbc_next = s_prep(b + 1)
                                    for hh in range(16):
                                        ACT(ss.t[:, hh, :], ss.t[:, hh, :], AF.Copy, [ssb[b % 2][hh], DBC], [ssb[b % 2][hh]], scale=DBC.t[:, hh, 16 + b:17 + b])
                                    for hh in range(16):
                                        g = hh // 2
                                        STT(ss.t[:, hh, :], pbc[g // 4].t[:, (g % 4) * 128:(g % 4 + 1) * 128], XDS.t[:, hh, b:b + 1], ss.t[:, hh, :], ALU.mult, ALU.add, [pbc[g // 4], XDS, ssb[b % 2][hh]], [ssb[b % 2][hh]])
                                    if b >= 1:
                                        s_reduce(b - 1)
                                    k.dma(k.sp, oss[j, b, :, :].rearrange("(a p) n -> p a n", p=128), ss.t[:, :, :], reads=ssb[b % 2], final=True)
                                    for a in range(2):
                                        ACT(CS.t[:, a * 4:(a + 1) * 4, :], pbc[2 + a].t[:, :].rearrange("p (g n) -> p g n", n=128), AF.Copy, [pbc[2 + a]], [CS])
                                    PTT(TA.t[:, :, :].rearrange("p (g e) n -> p g e n", e=2),
                                        ss.t[:, :, :].rearrange("p (g e) n -> p g e n", e=2),
                                        CS.t[:, :, :].unsqueeze(2).to_broadcast([128, 8, 2, 128]),
                                        ALU.mult, ssb[b % 2] + [CS], [TA])
                                    if b == NSMP - 1:
                                        s_reduce(b)
                                CP(YT.t[:, :, 512:NL], YS.t[:, :, :], [YS], YTb)
                        k.mark(f"  L{i} t{tix} core/sample end")
                        zi = 0
                        for it in range(8):
                            sl = wload([(win[j, :, it * 256:(it + 1) * 256], 8, 256, 0)])
                            for e in range(2):
                                zc = it * 2 + e
                                for (l, g, n) in lsubs:
                                    pz = ps()
                                    MM(pz, pz.t[:, :n], [(sl.t[:, kk * 256 + e * 128: kk * 256 + (e + 1) * 128], HN.t[:, kk, l:l + n]) for kk in range(8)], [sl, HN])
                                    gq = GS[zi % 2]
                                    zi += 1
                                    ACT(gq.t[:, 0, :n], pz.t[:, :n], AF.Silu, [pz], [gq])
                                    TS(gq.t[:, 1, :n], XB.t[:, zc, l:l + n], V(f"sD{j}", zc), None, ALU.mult, None, [XBb[zc], VEC, gq], [gq])
                                    TT(YT.t[:, zc, l:l + n], YT.t[:, zc, l:l + n], gq.t[:, 1, :n], ALU.add, [YTb[zc], gq], [YTb[zc]])
                                    TT(YT.t[:, zc, l:l + n], YT.t[:, zc, l:l + n], gq.t[:, 0, :n], ALU.mult, [YTb[zc], gq], [YTb[zc]])
                        k.mark(f"  L{i} t{tix} z end")
                        gjobs = [(g8, l, g, n) for g8 in range(8) for (l, g, n) in lsubs]

                        def gn_sq(idx):
                            g8, l, g, n = gjobs[idx]
                            gq = GS[idx % 2]
                            ACT(gq.t[:, 0:2, :n], YT.t[:, 2 * g8:2 * g8 + 2, l:l + n], AF.Square, YTb[2 * g8:2 * g8 + 2], [gq])
                            pn = ps()
                            MM(pn, pn.t[:, :n], [(onesb, gq.t[:, e, :n]) for e in range(2)], [gq, CB])
                            return pn

                        pn_next = gn_sq(0)
                        for idx, (g8, l, g, n) in enumerate(gjobs):
                            pn = pn_next
                            if idx + 1 < len(gjobs):
                                pn_next = gn_sq(idx + 1)
                            rt = rts[idx % 2]
                            RSTD(rt, n, pn.t[:, :n], pn, 1.0 / 256)
                            for e in range(2):
                                STT(YT.t[:, 2 * g8 + e, l:l + n], YT.t[:, 2 * g8 + e, l:l + n], V(f"snorm{j}", 2 * g8 + e), rt.t[:, :n], ALU.mult, ALU.mult, [YTb[2 * g8 + e], rt, VEC], [YTb[2 * g8 + e]])
                        for m in range(8):
                            sl = wload([(wout[j, :, m * 128:(m + 1) * 128], 16, 128, 0)])
                            for (l, g, n) in lsubs:
                                po = ps()
                                MM(po, po.t[:, :n], [(wv(sl, 0, kk, 128), YT.t[:, kk, l:l + n]) for kk in range(16)], [sl] + YTb)
                                TT(X.t[:, m, g:g + n], po.t[:, :n], X.t[:, m, g:g + n], ALU.add, [po, X1(m, g)], [X1(m, g)])

            k.mark("start")
            for i in range(DEPTH):
                ffn(i, 1)
                k.mark(f"L{i} ffn1 end")
                if i % 2 == 0:
                    conv_mixer(i, i // 2)
                else:
                    ssd_mixer(i, i // 2)
                k.mark(f"L{i} mixer end")
                ffn(i, 2)
                k.mark(f"L{i} ffn2 end")
                ple(i)
                k.mark(f"L{i} ple end")

            with k.phase():
                YN = k.buf("YN", [128, 8, NT], F32)
                sqs = [k.buf(f"sq{a}", [128, 8, 512], BF16) for a in range(2)]
                rts = [k.buf(f"rt{a}", [128, 512], F32) for a in range(2)]
                ost = [k.buf(f"yo{a}", [128, D], F32) for a in range(2)]
                YNb = k.sub(YN, 5, "t")
                rmsnorm("final_norm", lambda do: [YNb[tix_of(do)]], lambda kk, do, n: YN.t[:, kk, do:do + n], [(s, s, n) for s, n in TILES], sqs, rts)
                for r in range(17):
                    R = 128 if r < 16 else NSMP
                    o = ost[r % 2]
                    for half in range(2):
                        pb = ps()
                        for q in range(4):
                            c = half * 4 + q
                            TR(pb, pb.t[:R, q * 128:(q + 1) * 128], YNb[tix_of(r * 128)], YN.t[:, c, r * 128:r * 128 + R], identf, inc=(q == 3), extra=[CST])
                        if half == 0:
                            ACT(o.t[:R, 0:512], pb.t[:R, :], AF.Copy, [pb], [o])
                        else:
                            CP(o.t[:R, 512:1024], pb.t[:R, :], [pb], [o])
                    dst = yp[r * 128:(r + 1) * 128, :] if r < 16 else ysd[:, :]
                    k.dma(k.sp, dst, o.t[:R, :], reads=[o], final=True)
                if not k.dry:
                    for dep in k.final.values():
                        k._wait(k.sp, dep)
                    if k.ddcnt:
                        nc.sync.wait_ge(dsem_dd, k.ddcnt)

        k.dry = True
        k.wplan = []
        k.ddcnt = 0
        body()
        k.dry = False
        k.wissued = 0
        k.ddcnt = 0
        body()
        build.marks = getattr(k, "marks", [])
    return nc


_NC_CACHE = {}


def make_in_maps(inp):
    inp = {k_: np.asarray(v) for k_, v in inp.items()}
    vecs = np.ascontiguousarray(np.concatenate([a for (_, _, a) in vec_entries(inp)], axis=1), dtype=np.float32)
    cst = const_array()
    sel = sel_array()
    wnames = ["w_ffn1_gate", "w_ffn1_up", "w_ffn1_down", "w_ffn2_gate", "w_ffn2_up", "w_ffn2_down", "w_ple_gate", "w_ple_proj",
              "cm_w_in", "cm_w_out", "ssd_w_in", "ssd_w_out"]
    shared = {nm: np.ascontiguousarray(inp[nm], dtype=np.float32) for nm in wnames}
    shared.update(vecs=vecs, cst=cst, sel=sel)
    in_maps = []
    for c in range(NCORES):
        b0 = c * NSMP
        m = dict(shared)
        m["xp"] = np.ascontiguousarray(inp["x_prompt"][c], dtype=np.float32)
        m["xs"] = np.ascontiguousarray(inp["x_sample"][b0:b0 + NSMP, 0], dtype=np.float32)
        m["stc"] = np.ascontiguousarray(inp["state_conv"][:, b0:b0 + NSMP].reshape(2, NSMP * 30, D), dtype=np.float32)
        m["stsc"] = np.ascontiguousarray(inp["state_ssd_conv"][:, b0:b0 + NSMP].reshape(2, NSMP * 3, CONVD), dtype=np.float32)
        m["stss"] = np.ascontiguousarray(inp["state_ssd"][:, b0:b0 + NSMP].reshape(2, NSMP, NH * 64, 128), dtype=np.float32)
        m["pp"] = np.ascontiguousarray(inp["p_prompt"][:, c], dtype=np.float32)
        m["psm"] = np.ascontiguousarray(inp["p_sample"][:, b0:b0 + NSMP, 0], dtype=np.float32)
        in_maps.append(m)
    return in_maps


def kernel(**inp):
    if "nc" not in _NC_CACHE:
        _NC_CACHE["nc"] = build()
    nc = _NC_CACHE["nc"]
    in_maps = make_in_maps(inp)
    res = run_bass_kernel_spmd(nc, in_maps, core_ids=list(range(NCORES)))
    R = res.results
    y_prompt = np.stack([R[c]["yp"] for c in range(NCORES)], 0).astype(np.float32)
    y_sample = np.concatenate([R[c]["ys"] for c in range(NCORES)], 0).reshape(NCORES * NSMP, 1, D).astype(np.float32)
    conv_p = np.stack([R[c]["ocp"] for c in range(NCORES)], 1).astype(np.float32)
    xbc_p = np.stack([R[c]["oxp"] for c in range(NCORES)], 1).astype(np.float32)
    ssm_p = np.stack([R[c]["osp"].reshape(2, NH, 64, 128) for c in range(NCORES)], 1).astype(np.float32)
    conv_s = np.concatenate([R[c]["ocs"] for c in range(NCORES)], 1).astype(np.float32)
    xbc_s = np.concatenate([R[c]["oxs"] for c in range(NCORES)], 1).astype(np.float32)
    ssm_s = np.concatenate([R[c]["oss"].reshape(2, NSMP, NH, 64, 128) for c in range(NCORES)], 1).astype(np.float32)
    return (y_prompt, y_sample, conv_p, xbc_p, ssm_p, conv_s, xbc_s, ssm_s)
```

```python
import numpy as np
from contextlib import ExitStack, contextmanager
import concourse.bass as bass
import concourse.mybir as mybir
from concourse.bass_utils import run_bass_kernel_spmd

F32 = mybir.dt.float32
BF16 = mybir.dt.bfloat16
AF = mybir.ActivationFunctionType
ALU = mybir.AluOpType
AX = mybir.AxisListType
P = 128
D = 1024
DFF = 2816
T = 2048
NSMP = 16
NT = T + NSMP
DEPTH = 4
DIN = 2048
CONVD = 4096
NH = 32
PLE = 256
CK = 31
EPS = 1e-6
NSLOT = 5
SLOTE = 2048
NCORES = 8
TILES = [(0, 512), (512, 512), (1024, 512), (1536, 512), (2048, 16)]


def fm(v):
    v = np.asarray(v, np.float32).reshape(-1)
    return np.ascontiguousarray(v.reshape(-1, 128).T)


def vec_entries(inp):
    E = []

    def add(name, n, fn):
        if inp is None:
            E.append((name, n, None))
        else:
            a = np.zeros((128, n), np.float32)
            b = fn()
            a[: b.shape[0], :] = b
            E.append((name, n, a))

    for i in range(DEPTH):
        for nm in ("norm_ffn1", "norm_mix", "norm_ffn2", "norm_ple"):
            add(f"{nm}{i}", 8, lambda nm=nm, i=i: fm(inp[nm][i]))
    add("final_norm", 8, lambda: fm(inp["final_norm"]))
    for j in range(2):
        add(f"cbin{j}", 16, lambda j=j: fm(inp["cm_b_in"][j]))
        add(f"cdw{j}", 8 * CK, lambda j=j: np.asarray(inp["cm_dw"][j], np.float32).T.reshape(8, 128, CK).transpose(1, 0, 2).reshape(128, 8 * CK))
        add(f"cdwb{j}", 8, lambda j=j: fm(inp["cm_dw_b"][j]))
        add(f"clng{j}", 8, lambda j=j: fm(inp["cm_ln_g"][j]))
        add(f"clnb{j}", 8, lambda j=j: fm(inp["cm_ln_b"][j]))
        add(f"cbout{j}", 8, lambda j=j: fm(inp["cm_b_out"][j]))
    for j in range(2):
        add(f"scw{j}", 32 * 4, lambda j=j: np.asarray(inp["ssd_conv_w"][j], np.float32).T.reshape(32, 128, 4).transpose(1, 0, 2).reshape(128, 128))
        add(f"scb{j}", 32, lambda j=j: fm(inp["ssd_conv_b"][j]))
        add(f"snorm{j}", 16, lambda j=j: fm(inp["ssd_norm"][j]))
        add(f"sD{j}", 16, lambda j=j: np.repeat(np.asarray(inp["ssd_D"][j], np.float32).reshape(16, 2, 1), 64, axis=2).transpose(1, 2, 0).reshape(128, 16))
        add(f"dtb32{j}", 1, lambda j=j: np.asarray(inp["ssd_dt_bias"][j], np.float32).reshape(32, 1))
        add(f"alog32{j}", 1, lambda j=j: np.asarray(inp["ssd_A_log"][j], np.float32).reshape(32, 1))
        add(f"dtbbc{j}", 32, lambda j=j: np.tile(np.asarray(inp["ssd_dt_bias"][j], np.float32).reshape(1, 32), (128, 1)))
        add(f"alogbc{j}", 32, lambda j=j: np.tile(np.asarray(inp["ssd_A_log"][j], np.float32).reshape(1, 32), (128, 1)))
    return E


def vec_offsets():
    off = {}
    o = 0
    for name, n, _ in vec_entries(None):
        off[name] = o
        o += n
    return off, o


def const_array():
    c = np.zeros((128, 512), np.float32)
    i = np.arange(128)
    c[:, 0:128] = np.eye(128, dtype=np.float32)
    c[:, 128:256] = (i[:, None] <= i[None, :]).astype(np.float32)
    c[:, 256:384] = (i[:, None] > i[None, :]).astype(np.float32)
    c[:, 384:512] = 1.0
    return c


def sel_array():
    s = np.zeros((32, 16, 128), np.float32)
    for hh in range(16):
        for m in range(128):
            s[2 * hh + m // 64, hh, m] = 1.0
    return s.reshape(32, 2048)


class Buf:
    def __init__(self, t, name):
        self.t = t
        self.name = name
        self.w = None
        self.r = {}
        self.dsem = None


class Eng:
    def __init__(self, raw, sem, name):
        self.raw = raw
        self.sem = sem
        self.name = name
        self.cnt = 0
        self.seen = {}


class K:
    def __init__(self, nc, es):
        self.nc = nc
        self.es = es
        self.dry = False
        self.alloc = es
        self.phase_bufs = None
        self.sem_free = []
        self.sem_tot = {}
        self.pending = {}
        self.final = {}
        self.nsem = 0
        self.uid = 0

    def init_engines(self):
        nc, es = self.nc, self.es

        def mk(raw, name):
            return Eng(raw, es.enter_context(nc.semaphore("sem_" + name)), name)

        self.pe = mk(nc.tensor, "pe")
        self.act = mk(nc.scalar, "act")
        self.dve = mk(nc.vector, "dve")
        self.pool = mk(nc.gpsimd, "pool")
        self.sp = mk(nc.sync, "sp")
        self.engs = [self.pe, self.act, self.dve, self.pool, self.sp]
        for e in self.engs:
            e.cnt = 0
            e.seen = {}

    def buf(self, name, shape, dt):
        self.uid += 1
        t = self.alloc.enter_context(self.nc.sbuf_tensor(f"{name}_{self.uid}", list(shape), dt))
        b = Buf(t, name)
        if self.phase_bufs is not None:
            self.phase_bufs.append(b)
            b.in_phase = True
        return b

    def sub(self, b, n, tag=""):
        out = []
        for i in range(n):
            s = Buf(b.t, f"{b.name}{tag}{i}")
            s.in_phase = getattr(b, "in_phase", False)
            if self.phase_bufs is not None and s.in_phase:
                self.phase_bufs.append(s)
            out.append(s)
        return out

    def _getsem(self, b):
        if b.dsem is None:
            if self.sem_free and getattr(b, "in_phase", False):
                b.dsem = self.sem_free.pop()
            else:
                self.nsem += 1
                sem = self.es.enter_context(self.nc.semaphore(f"dsem{self.nsem}"))
                b.dsem = (f"dsem{self.nsem}", sem)
                self.sem_tot[b.dsem[0]] = 0
        return b.dsem

    def _wait(self, e, dep):
        if dep is None:
            return
        key, sem, val = dep
        if e.seen.get(key, 0) >= val:
            return
        e.raw.wait_ge(sem, val)
        e.seen[key] = val

    def _deps(self, e, reads, writes, isdma=False):
        for b in reads:
            self._wait(e, b.w)
        same_ok = (not isdma) and e.name == "pe"
        for b in writes:
            if b.w is not None and not (same_ok and b.w[0] == e.name):
                self._wait(e, b.w)
            for k, (sem, val) in b.r.items():
                if not (same_ok and k == e.name):
                    self._wait(e, (k, sem, val))

    def op(self, e, fn, reads=(), writes=(), inc=True):
        if self.dry:
            return None
        self._deps(e, reads, writes)
        ins = fn()
        self.opidx = getattr(self, "opidx", 0) + 1
        for b in reads:
            b.last_read = self.opidx
        for b in writes:
            b.last_write = self.opidx
        if e.name == "pe":
            self.npe = getattr(self, "npe", 0) + 1
        if inc:
            e.cnt += 1
            ins.then_inc(e.sem, 1)
            c = e.cnt
        else:
            c = e.cnt + 1
        for b in reads:
            b.r[e.name] = (e.sem, c)
        for b in writes:
            b.w = (e.name, e.sem, c)
            b.r = {}
        return ins

    def dma(self, q, out, in_, reads=(), writes=(), final=False, **kw):
        if self.dry:
            return
        self._deps(q, reads, writes, isdma=True)
        ins = q.raw.dma_start(out=out, in_=in_, **kw)
        b = (list(writes) + list(reads))[0]
        key, sem = self._getsem(b)
        self.sem_tot[key] += 16
        tot = self.sem_tot[key]
        ins.then_inc(sem, 16)
        for w in writes:
            w.w = (key, sem, tot)
            w.r = {}
        for r in reads:
            r.r[key] = (sem, tot)
        self.pending[key] = (key, sem, tot)
        if final:
            self.final[key] = (key, sem, tot)

    def mark(self, label):
        if not self.dry:
            self.marks = getattr(self, "marks", [])
            self.marks.append((label, getattr(self, "npe", 0)))

    def barrier(self):
        if self.dry:
            return
        engs = [self.pe, self.act, self.dve, self.sp, self.pool]
        for e in engs:
            for e2 in engs:
                if e2 is not e and e2.cnt > 0:
                    self._wait(e, (e2.name, e2.sem, e2.cnt))
            for dep in self.pending.values():
                self._wait(e, dep)
        self.pending = {}

    @contextmanager
    def phase(self):
        old_alloc, old_bufs = self.alloc, self.phase_bufs
        with ExitStack() as st:
            self.alloc = st
            self.phase_bufs = []
            yield
            self.barrier()
            for b in self.phase_bufs:
                if b.dsem is not None:
                    self.sem_free.append(b.dsem)
        self.alloc, self.phase_bufs = old_alloc, old_bufs


def build():
    nc = bass.Bass("TRN2", target_bir_lowering=False)
    voff, NV = vec_offsets()

    def din(name, shape):
        return nc.dram_tensor(name, list(shape), F32, kind="ExternalInput").ap()

    def dout(name, shape):
        return nc.dram_tensor(name, list(shape), F32, kind="ExternalOutput").ap()

    xp = din("xp", [T, D])
    xs = din("xs", [NSMP, D])
    stc = din("stc", [2, NSMP * 30, D])
    stsc = din("stsc", [2, NSMP * 3, CONVD])
    stss = din("stss", [2, NSMP, NH * 64, 128])
    pp = din("pp", [DEPTH, T, PLE])
    psm = din("psm", [DEPTH, NSMP, PLE])
    W = {}
    for nm, shp in [("w_ffn1_gate", [DEPTH, D, DFF]), ("w_ffn1_up", [DEPTH, D, DFF]), ("w_ffn1_down", [DEPTH, DFF, D]),
                    ("w_ffn2_gate", [DEPTH, D, DFF]), ("w_ffn2_up", [DEPTH, D, DFF]), ("w_ffn2_down", [DEPTH, DFF, D]),
                    ("w_ple_gate", [DEPTH, D, D]), ("w_ple_proj", [DEPTH, PLE, D]),
                    ("cm_w_in", [2, D, 2 * D]), ("cm_w_out", [2, D, D]),
                    ("ssd_w_in", [2, D, 6176]), ("ssd_w_out", [2, DIN, D])]:
        W[nm] = din(nm, shp)
    vecs_d = din("vecs", [128, NV])
    cst_d = din("cst", [128, 512])
    sel_d = din("sel", [32, 2048])
    yp = dout("yp", [T, D])
    ysd = dout("ys", [NSMP, D])
    ocp = dout("ocp", [2, 30, D])
    oxp = dout("oxp", [2, 3, CONVD])
    osp = dout("osp", [2, NH * 64, 128])
    ocs = dout("ocs", [2, NSMP, 30, D])
    oxs = dout("oxs", [2, NSMP, 3, CONVD])
    oss = dout("oss", [2, NSMP, NH * 64, 128])

    with ExitStack() as es:
        k = K(nc, es)
        X = k.buf("X", [128, 8, NT], F32)
        VEC = k.buf("VEC", [128, NV], F32)
        CST = k.buf("CST", [128, 512], F32)
        CB = k.buf("CB", [128, 256], BF16)
        k.wslots = [k.buf(f"wslot{i}", [128, SLOTE], BF16) for i in range(NSLOT)]
        PS = [Buf(es.enter_context(nc.psum_tensor(f"psum{i}", [128, 512], F32)), f"psum{i}") for i in range(8)]
        dsem_dd = es.enter_context(nc.semaphore("dsem_dd"))
        k.init_engines()
        k.psi = 0

        identf = CST.t[:, 0:128]
        trif = CST.t[:, 128:256]
        ustrf = CST.t[:, 256:384]
        onesf = CST.t[:, 384:512]
        identb = CB.t[:, 0:128]
        onesb = CB.t[:, 128:256]

        Xb = [k.sub(X, 5, f"c{m}t") for m in range(8)]

        def tix_of(col):
            return min(col // 512, 4)

        def XT(col):
            return [Xb[m][tix_of(col)] for m in range(8)]

        def X1(m, col):
            return Xb[m][tix_of(col)]

        def V(name, c=0, n=1):
            o = voff[name] + c
            return VEC.t[:, o:o + n]

        def ps():
            if k.dry:
                return PS[0]
            free = [b for b in PS if getattr(b, "last_read", 0) >= getattr(b, "last_write", 0)]
            if free:
                b = min(free, key=lambda b: getattr(b, "last_read", 0))
            else:
                b = min(PS, key=lambda b: getattr(b, "last_write", 0))
            k.opidx = getattr(k, "opidx", 0) + 1
            b.last_write = k.opidx
            return b

        def ACT(out, in_, func, R, Wr, **kw):
            k.op(k.act, lambda: nc.scalar.activation(out=out, in_=in_, func=func, **kw), R, Wr)

        def TT(out, in0, in1, op, R, Wr):
            k.op(k.dve, lambda: nc.vector.tensor_tensor(out=out, in0=in0, in1=in1, op=op), R, Wr)

        def STT(out, in0, scalar, in1, op0, op1, R, Wr):
            k.op(k.dve, lambda: nc.vector.scalar_tensor_tensor(out=out, in0=in0, scalar=scalar, in1=in1, op0=op0, op1=op1), R, Wr)

        def TS(out, in0, s1, s2, op0, op1, R, Wr):
            if op1 is None:
                k.op(k.dve, lambda: nc.vector.tensor_scalar(out=out, in0=in0, scalar1=s1, scalar2=None, op0=op0), R, Wr)
            else:
                k.op(k.dve, lambda: nc.vector.tensor_scalar(out=out, in0=in0, scalar1=s1, scalar2=s2, op0=op0, op1=op1), R, Wr)

        def CP(out, in_, R, Wr):
            k.op(k.dve, lambda: nc.vector.tensor_copy(out=out, in_=in_), R, Wr)

        def RSTD(rt, n, src_ap, srcB, scale):
            ACT(rt.t[:, :n], src_ap, AF.Ln, [srcB], [rt], scale=scale, bias=EPS)
            ACT(rt.t[:, :n], rt.t[:, :n], AF.Exp, [rt], [rt], scale=-0.5)

        def RCP(out, in_, R, Wr):
            k.op(k.dve, lambda: nc.vector.reciprocal(out=out, in_=in_), R, Wr)

        def MEMSET(out, val, Wr):
            k.op(k.dve, lambda: nc.vector.memset(out, val), (), Wr)

        def MM(ob, out, pairs, R, inc=True, first=True, final=True):
            n = len(pairs)
            for i, (l, r) in enumerate(pairs):
                last = i == n - 1
                k.op(k.pe, lambda l=l, r=r, i=i, last=last: nc.tensor.matmul(out, lhsT=l, rhs=r, start=(first and i == 0), stop=(final and last)),
                     R if i == 0 else (), [ob] if i == 0 else (), inc=(last and inc))

        def PTT(out, in0, in1, op, R, Wr):
            k.op(k.pool, lambda: nc.gpsimd.tensor_tensor(out=out, in0=in0, in1=in1, op=op), R, Wr)

        def TR(ob, out, ib, in_, ident, inc=True, extra=()):
            k.op(k.pe, lambda: nc.tensor.transpose(out, in_, ident), [ib] + list(extra), [ob], inc=inc)

        def wload(specs):
            j = k.wj
            k.wj += 1
            if k.dry:
                k.wplan.append(specs)
                return k.wslots[j % NSLOT]
            base = k.whold if k.whold is not None else j
            while k.wissued < min(len(k.wplan), base + NSLOT):
                jj = k.wissued
                slot = k.wslots[jj % NSLOT]
                for (src, kc, ncols, off) in k.wplan[jj]:
                    dst = slot.t[:, off:off + kc * ncols].rearrange("p (k c) -> p k c", k=kc)
                    k.dma(k.pool, dst, src.rearrange("(k p) c -> p k c", p=128), writes=[slot])
                k.wissued += 1
            return k.wslots[j % NSLOT]

        def wv(slot, off, kk, ncols):
            return slot.t[:, off + kk * ncols: off + (kk + 1) * ncols]

        def body():
            k.wj = 0
            k.psi = 0
            k.whold = None
            k.dma(k.sp, VEC.t[:, :], vecs_d[:, :], writes=[VEC])
            k.dma(k.sp, CST.t[:, :], cst_d[:, :], writes=[CST])
            CP(CB.t[:, 0:128], identf, [CST], [CB])
            CP(CB.t[:, 128:256], onesf, [CST], [CB])

            with k.phase():
                stg = [k.buf(f"stg{i}", [128, D], F32) for i in range(4)]
                for r in range(17):
                    st = stg[r % 4]
                    R = 128 if r < 16 else NSMP
                    src = xp[r * 128:(r + 1) * 128, :] if r < 16 else xs[:, :]
                    k.dma(k.sp, st.t[:R, :], src, writes=[st])
                    for half in range(2):
                        pb = ps()
                        for q in range(4):
                            c = half * 4 + q
                            TR(pb, pb.t[:, q * 128:q * 128 + R], st, st.t[:R, c * 128:(c + 1) * 128], identf[:R, :R], inc=(q == 3), extra=[CST])
                        src_v = pb.t[:, :].rearrange("p (a b) -> p a b", a=4)[:, :, :R]
                        dst_v = X.t[:, half * 4:half * 4 + 4, r * 128:r * 128 + R]
                        xw = [X1(m, r * 128) for m in range(half * 4, half * 4 + 4)]
                        if half == 0:
                            ACT(dst_v, src_v, AF.Copy, [pb], xw)
                        else:
                            CP(dst_v, src_v, [pb], xw)

            def rmsnorm(gname, dstB_fn, dst_fn, subs, sqs, rts):
                for ti, (so, do, n) in enumerate(subs):
                    sq = sqs[ti % len(sqs)]
                    rt = rts[ti % len(rts)]
                    ACT(sq.t[:, 0:8, :n], X.t[:, :, so:so + n], AF.Square, XT(so), [sq])
                    pb = ps()
                    MM(pb, pb.t[:, :n], [(onesb, sq.t[:, kk, :n]) for kk in range(8)], [sq, CB])
                    RSTD(rt, n, pb.t[:, :n], pb, 1.0 / D)
                    for kk in range(8):
                        STT(dst_fn(kk, do, n), X.t[:, kk, so:so + n], V(gname, kk), rt.t[:, :n], ALU.mult, ALU.mult, [X1(kk, so), rt, VEC], dstB_fn(do))

            def ffn(i, which):
                wg, wu, wd = W[f"w_ffn{which}_gate"], W[f"w_ffn{which}_up"], W[f"w_ffn{which}_down"]
                with k.phase():
                    XN = k.buf("XN", [128, 8, NT], BF16)
                    H = k.buf("H", [128, 11, NT], BF16)
                    sqs = [k.buf(f"sq{a}", [128, 8, 512], BF16) for a in range(2)]
                    rts = [k.buf(f"rt{a}", [128, 512], F32) for a in range(2)]
                    sgs = [k.buf(f"sg{a}", [128, 512], F32) for a in range(3)]
                    XNb = k.sub(XN, 5, "t")
                    Hb = k.sub(H, 5, "t")
                    rmsnorm(f"norm_ffn{which}{i}", lambda do: [XNb[tix_of(do)]], lambda kk, do, n: XN.t[:, kk, do:do + n], [(s, s, n) for s, n in TILES], sqs, rts)
                    sgi = [0]

                    def ffn_a(sl, fi, s, n):
                        pa = ps()
                        pbb = ps()
                        MM(pa, pa.t[:, :n], [(wv(sl, 0, kk, 128), XN.t[:, kk, s:s + n]) for kk in range(8)], [sl, XNb[tix_of(s)]])
                        MM(pbb, pbb.t[:, :n], [(wv(sl, 1024, kk, 128), XN.t[:, kk, s:s + n]) for kk in range(8)], [sl, XNb[tix_of(s)]])
                        sg = sgs[sgi[0] % 3]
                        sgi[0] += 1
                        ACT(sg.t[:, :n], pa.t[:, :n], AF.Silu, [pa], [sg])
                        TT(H.t[:, fi, s:s + n], pbb.t[:, :n], sg.t[:, :n], ALU.mult, [pbb, sg], [Hb[tix_of(s)]])

                    def ffn_w(f):
                        return wload([(wg[i, :, f * 128:(f + 1) * 128], 8, 128, 0), (wu[i, :, f * 128:(f + 1) * 128], 8, 128, 1024)])

                    for half in range(2):
                        fis = list(range(11))
                        if half == 0:
                            k.whold = k.wj
                            sls = [ffn_w(fi) for fi in range(4)]
                            for (s, n) in TILES:
                                for fi in range(4):
                                    ffn_a(sls[fi], fi, s, n)
                            k.whold = None
                            fis = list(range(4, 11))
                        for fi in fis:
                            sl = ffn_w(half * 11 + fi)
                            for (s, n) in TILES:
                                ffn_a(sl, fi, s, n)
                        for m in range(8):
                            r0 = half * 1408
                            sl = wload([(wd[i, r0:r0 + 1408, m * 128:(m + 1) * 128], 11, 128, 0)])
                            for (s, n) in TILES:
                                pc = ps()
                                MM(pc, pc.t[:, :n], [(wv(sl, 0, fi, 128), H.t[:, fi, s:s + n]) for fi in range(11)], [sl, Hb[tix_of(s)]])
                                STT(X.t[:, m, s:s + n], pc.t[:, :n], 0.5, X.t[:, m, s:s + n], ALU.mult, ALU.add, [pc, X1(m, s)], [X1(m, s)])

            def ple(i):
                with k.phase():
                    XN = k.buf("XN", [128, 8, NT], BF16)
                    PT = k.buf("PT", [128, 2, NT], BF16)
                    sqs = [k.buf(f"sq{a}", [128, 8, 512], BF16) for a in range(2)]
                    rts = [k.buf(f"rt{a}", [128, 512], F32) for a in range(2)]
                    sgs = [k.buf(f"sg{a}", [128, 512], F32) for a in range(3)]
                    t2s = [k.buf(f"t2{a}", [128, 512], F32) for a in range(2)]
                    stg = [k.buf(f"pstg{a}", [128, PLE], F32) for a in range(6)]
                    PTb = k.sub(PT, 5, "t")
                    XNb = k.sub(XN, 5, "t")
                    rmsnorm(f"norm_ple{i}", lambda do: [XNb[tix_of(do)]], lambda kk, do, n: XN.t[:, kk, do:do + n], [(s, s, n) for s, n in TILES], sqs, rts)
                    for r in range(17):
                        st = stg[r % 6]
                        R = 128 if r < 16 else NSMP
                        src = pp[i, r * 128:(r + 1) * 128, :] if r < 16 else psm[i, :, :]
                        k.dma(k.sp, st.t[:R, :], src, writes=[st])
                        pb = ps()
                        for c in range(2):
                            TR(pb, pb.t[:, c * 128:c * 128 + R], st, st.t[:R, c * 128:(c + 1) * 128], identf[:R, :R], inc=(c == 1), extra=[CST])
                        ACT(PT.t[:, :, r * 128:r * 128 + R], pb.t[:, 0:256].rearrange("p (a b) -> p a b", a=2)[:, :, :R], AF.Copy, [pb], [PTb[tix_of(r * 128)]])
                    ci = [0]

                    def ple_w(m):
                        return wload([(W["w_ple_gate"][i, :, m * 128:(m + 1) * 128], 8, 128, 0), (W["w_ple_proj"][i, :, m * 128:(m + 1) * 128], 2, 128, 1024)])

                    def ple_c(sl, m, s, n):
                        pg = ps()
                        pq = ps()
                        MM(pg, pg.t[:, :n], [(wv(sl, 0, kk, 128), XN.t[:, kk, s:s + n]) for kk in range(8)], [sl, XNb[tix_of(s)]])
                        MM(pq, pq.t[:, :n], [(wv(sl, 1024, kk, 128), PT.t[:, kk, s:s + n]) for kk in range(2)], [sl, PTb[tix_of(s)]])
                        sg = sgs[ci[0] % 3]
                        t2 = t2s[ci[0] % 2]
                        ci[0] += 1
                        ACT(sg.t[:, :n], pg.t[:, :n], AF.Sigmoid, [pg], [sg])
                        TT(t2.t[:, :n], pq.t[:, :n], sg.t[:, :n], ALU.mult, [pq, sg], [t2])
                        TT(X.t[:, m, s:s + n], X.t[:, m, s:s + n], t2.t[:, :n], ALU.add, [X1(m, s), t2], [X1(m, s)])

                    k.whold = k.wj
                    sls = [ple_w(m) for m in range(4)]
                    for (s, n) in TILES:
                        for m in range(4):
                            ple_c(sls[m], m, s, n)
                    k.whold = None
                    for m in range(4, 8):
                        sl = ple_w(m)
                        for (s, n) in TILES:
                            ple_c(sl, m, s, n)

            def conv_mixer(i, j):
                with k.phase():
                    XN = k.buf("XN", [128, 8, NT], BF16)
                    Vb = k.buf("Vb", [128, 8, NT], BF16)
                    rts = [k.buf(f"rt{a}", [128, 512], F32) for a in range(2)]
                    sgs = [k.buf(f"sg{a}", [128, 512], F32) for a in range(3)]
                    UL = k.buf("ul", [128, 8, 32], F32)
                    USN = k.buf("usn", [128, 8, NSMP], F32)
                    XNb = k.sub(XN, 5, "t")
                    Vbb = k.sub(Vb, 8, "c")
                    with k.phase():
                        sqs = [k.buf(f"sq{a}", [128, 8, 512], BF16) for a in range(2)]
                        rmsnorm(f"norm_mix{i}", lambda do: [XNb[tix_of(do)]], lambda kk, do, n: XN.t[:, kk, do:do + n], [(s, s, n) for s, n in TILES], sqs, rts)
                    c1 = k.phase()
                    c1.__enter__()
                    UE = [k.buf(f"ue{a}", [128, 30 + T], BF16) for a in range(2)]
                    US = [k.buf(f"us{a}", [128, NSMP, CK], F32) for a in range(2)]
                    DG = [k.buf(f"dg{a}", [128, CK, 128], BF16) for a in range(2)]
                    STSS = [k.buf(f"sts{a}", [128, 4, 128], F32) for a in range(2)]
                    tmp = k.buf("ctmp", [128, NSMP, CK], F32)
                    red = k.buf("cred", [128, NSMP], F32)
                    if not k.dry:
                        nc.sync.dma_start(out=ocs[j, :, 0:29, :], in_=stc[j, :, :].rearrange("(b r) c -> b r c", r=30)[:, 1:30, :]).then_inc(dsem_dd, 16)
                        k.ddcnt += 16
                    sgi = 0
                    for c in range(8):
                        sl = wload([(W["cm_w_in"][j, :, c * 128:(c + 1) * 128], 8, 128, 0), (W["cm_w_in"][j, :, D + c * 128:D + (c + 1) * 128], 8, 128, 1024)])
                        ue = UE[c % 2]
                        us = US[c % 2]
                        dg = DG[c % 2]
                        MEMSET(ue.t[:, 0:30], 0.0, [ue])
                        TT(dg.t[:, :, :], identb.unsqueeze(1).to_broadcast([128, CK, 128]), V(f"cdw{j}", c * CK, CK).unsqueeze(2).to_broadcast([128, CK, 128]), ALU.mult, [CB, VEC], [dg])
                        STS = STSS[c % 2]
                        k.dma(k.sp, STS.t[:, 0:3, :], stc[j, 0:384, c * 128:(c + 1) * 128].rearrange("(a p) c -> p a c", p=128), writes=[STS])
                        k.dma(k.sp, STS.t[:96, 3, :], stc[j, 384:480, c * 128:(c + 1) * 128], writes=[STS])
                        pb = ps()
                        for a in range(4):
                            R = 128 if a < 3 else 96
                            TR(pb, pb.t[:, a * 128:a * 128 + R], STS, STS.t[:R, a, :], identf[:R, :R], inc=(a == 3), extra=[CST])
                        CP(us.t[:, :, 0:30], pb.t[:, 0:480].rearrange("p (b r) -> p b r", r=30), [pb], [us])
                        for (s, n) in TILES:
                            pa = ps()
                            pg = ps()
                            MM(pa, pa.t[:, :n], [(wv(sl, 0, kk, 128), XN.t[:, kk, s:s + n]) for kk in range(8)], [sl, XNb[tix_of(s)]])
                            MM(pg, pg.t[:, :n], [(wv(sl, 1024, kk, 128), XN.t[:, kk, s:s + n]) for kk in range(8)], [sl, XNb[tix_of(s)]])
                            sg = sgs[sgi % 3]
                            sgi += 1
                            ACT(sg.t[:, :n], pg.t[:, :n], AF.Sigmoid, [pg, VEC], [sg], bias=V(f"cbin{j}", 8 + c))
                            if s < T:
                                STT(ue.t[:, 30 + s:30 + s + n], pa.t[:, :n], V(f"cbin{j}", c), sg.t[:, :n], ALU.add, ALU.mult, [pa, sg, VEC], [ue])
                                if s + n == T:
                                    STT(UL.t[:, c, :], pa.t[:, n - 32:n], V(f"cbin{j}", c), sg.t[:, n - 32:n], ALU.add, ALU.mult, [pa, sg, VEC], [UL])
                            else:
                                STT(us.t[:, :, 30], pa.t[:, :n], V(f"cbin{j}", c), sg.t[:, :n], ALU.add, ALU.mult, [pa, sg, VEC], [us])
                        for (s, n) in TILES[:4]:
                            pv = ps()
                            MM(pv, pv.t[:, :n], [(dg.t[:, kk, :], ue.t[:, s + kk:s + kk + n]) for kk in range(CK)], [dg, ue])
                            ACT(Vb.t[:, c, s:s + n], pv.t[:, :n], AF.Identity, [pv, VEC], [Vbb[c]], bias=V(f"cdwb{j}", c))
                        TT(tmp.t[:, :, :], us.t[:, :, :], V(f"cdw{j}", c * CK, CK).unsqueeze(1).to_broadcast([128, NSMP, CK]), ALU.mult, [us, VEC], [tmp])
                        k.op(k.dve, lambda: nc.vector.tensor_reduce(out=red.t[:, :], in_=tmp.t[:, :, :], axis=AX.X, op=ALU.add), [tmp], [red])
                        TS(Vb.t[:, c, T:NT], red.t[:, :], V(f"cdwb{j}", c), None, ALU.add, None, [red, VEC], [Vbb[c]])
                        CP(USN.t[:, c, :], us.t[:, :, 30], [us], [USN])
                    c1.__exit__(None, None, None)
                    c2 = k.phase()
                    c2.__enter__()
                    sqs = [k.buf(f"sq{a}", [128, 8, 512], BF16) for a in range(2)]
                    m1s = [k.buf(f"m1{a}", [128, 512], F32) for a in range(2)]
                    m2s = [k.buf(f"m2{a}", [128, 512], F32) for a in range(2)]
                    dts_ = [k.buf(f"dt{a}", [128, 512], F32) for a in range(2)]
                    for ti, (s, n) in enumerate(TILES):
                        sq = sqs[ti % 2]
                        rt = rts[ti % 2]
                        m1 = m1s[ti % 2]
                        m2 = m2s[ti % 2]
                        ACT(sq.t[:, :, :n], Vb.t[:, :, s:s + n], AF.Square, Vbb, [sq])
                        p1 = ps()
                        p2 = ps()
                        MM(p1, p1.t[:, :n], [(onesb, Vb.t[:, kk, s:s + n]) for kk in range(8)], Vbb + [CB])
                        MM(p2, p2.t[:, :n], [(onesb, sq.t[:, kk, :n]) for kk in range(8)], [sq, CB])
                        ACT(m1.t[:, :n], p1.t[:, :n], AF.Copy, [p1], [m1], scale=1.0 / D)
                        TT(m2.t[:, :n], m1.t[:, :n], m1.t[:, :n], ALU.mult, [m1], [m2])
                        STT(m2.t[:, :n], p2.t[:, :n], 1.0 / D, m2.t[:, :n], ALU.mult, ALU.subtract, [p2, m2], [m2])
                        RSTD(rt, n, m2.t[:, :n], m2, 1.0)
                        for kk in range(8):
                            dtb = dts_[kk % 2]
                            TT(dtb.t[:, :n], Vb.t[:, kk, s:s + n], m1.t[:, :n], ALU.subtract, [Vbb[kk], m1], [dtb])
                            TT(dtb.t[:, :n], dtb.t[:, :n], rt.t[:, :n], ALU.mult, [dtb, rt], [dtb])
                            ACT(XN.t[:, kk, s:s + n], dtb.t[:, :n], AF.Silu, [dtb, VEC], [XNb[tix_of(s)]], scale=V(f"clng{j}", kk), bias=V(f"clnb{j}", kk))
                    def co_w(m):
                        return wload([(W["cm_w_out"][j, :, m * 128:(m + 1) * 128], 8, 128, 0)])

                    def co_c(sl, m, s, n):
                        pc = ps()
                        MM(pc, pc.t[:, :n], [(wv(sl, 0, kk, 128), XN.t[:, kk, s:s + n]) for kk in range(8)], [sl, XNb[tix_of(s)]])
                        STT(X.t[:, m, s:s + n], pc.t[:, :n], V(f"cbout{j}", m), X.t[:, m, s:s + n], ALU.add, ALU.add, [pc, X1(m, s), VEC], [X1(m, s)])

                    k.whold = k.wj
                    sls = [co_w(m) for m in range(4)]
                    for (s, n) in TILES:
                        for m in range(4):
                            co_c(sls[m], m, s, n)
                    k.whold = None
                    for m in range(4, 8):
                        sl = co_w(m)
                        for (s, n) in TILES:
                            co_c(sl, m, s, n)
                    c2.__exit__(None, None, None)
                    OST = k.buf("ost", [32, D], F32)
                    for half in range(2):
                        pb = ps()
                        for q in range(4):
                            c = half * 4 + q
                            TR(pb, pb.t[:32, q * 128:(q + 1) * 128], UL, UL.t[:, c, :], identf, inc=(q == 3), extra=[CST])
                        CP(OST.t[:32, half * 512:(half + 1) * 512], pb.t[:32, :], [pb], [OST])
                    k.dma(k.sp, ocp[j, :, :], OST.t[2:32, :], reads=[OST], final=True)
                    OS2 = k.buf("os2", [NSMP, D], F32)
                    for half in range(2):
                        pb = ps()
                        for q in range(4):
                            c = half * 4 + q
                            TR(pb, pb.t[:NSMP, q * 128:(q + 1) * 128], USN, USN.t[:, c, :], identf, inc=(q == 3), extra=[CST])
                        CP(OS2.t[:, half * 512:(half + 1) * 512], pb.t[:NSMP, :], [pb], [OS2])
                    k.dma(k.sp, ocs[j, :, 29, :], OS2.t[:, :], reads=[OS2], final=True)

            def ssd_mixer(i, j):
                win, wout = W["ssd_w_in"], W["ssd_w_out"]
                with k.phase():
                    NL = 512 + NSMP
                    HN = k.buf("HN", [128, 8, NL], BF16)
                    XB = k.buf("XB", [128, 32, NL], BF16)
                    YT = k.buf("YT", [128, 16, NL], BF16)
                    XBb = k.sub(XB, 32, "c")
                    YTb = k.sub(YT, 16, "c")
                    PRE = [k.buf(f"pre{a}", [128, 515], BF16) for a in range(2)]
                    DGS = [k.buf(f"dgs{a}", [128, 4, 128], BF16) for a in range(2)]
                    HIST = k.buf("hist", [128, 32, 3], F32)
                    HISTb = k.sub(HIST, 32, "c")
                    GS = [k.buf(f"gsq{a}", [128, 2, 512], BF16) for a in range(2)]
                    rts = [k.buf(f"rt{a}", [128, 512], F32) for a in range(2)]
                    sgs = rts
                    ABC = k.buf("abc", [128, 32], F32)
                    S = k.buf("S", [128, DIN], F32)
                    SB = k.buf("SB", [128, DIN], BF16)
                    Sb = k.sub(S, 4, "q")
                    SBb = k.sub(SB, 4, "q")
                    MEMSET(HIST.t[:, :, :], 0.0, HISTb)
                    MEMSET(S.t[:, :], 0.0, Sb)
                    MEMSET(SB.t[:, :], 0.0, SBb)
                    ACT(ABC.t[:, :], V(f"alogbc{j}", 0, 32), AF.Exp, [VEC], [ABC])
                    TS(ABC.t[:, :], ABC.t[:, :], -1.0, None, ALU.mult, None, [ABC], [ABC])
                    ci = 0
                    gi = 0
                    for tix in range(4):
                        s0 = tix * 512
                        smp = tix == 3
                        lsubs = [(0, s0, 512)] + ([(512, T, NSMP)] if smp else [])
                        for (l, g, n) in lsubs:
                            pb = ps()
                            for rnd in range(4):
                                sq = GS[gi % 2]
                                gi += 1
                                ACT(sq.t[:, 0:2, :n], X.t[:, 2 * rnd:2 * rnd + 2, g:g + n], AF.Square, [X1(2 * rnd, g), X1(2 * rnd + 1, g)], [sq])
                                MM(pb, pb.t[:, :n], [(onesb, sq.t[:, e, :n]) for e in range(2)], [sq, CB], first=(rnd == 0), final=(rnd == 3))
                            rt = rts[0]
                            RSTD(rt, n, pb.t[:, :n], pb, 1.0 / D)
                            for kk in range(8):
                                STT(HN.t[:, kk, l:l + n], X.t[:, kk, g:g + n], V(f"norm_mix{i}", kk), rt.t[:, :n], ALU.mult, ALU.mult, [X1(kk, g), rt, VEC], [HN])
                        if smp:
                            xph = k.phase()
                            xph.__enter__()
                            STXS = [k.buf(f"stx{a}", [48, 1024], F32) for a in range(2)]
                            STT_ = k.buf("stT", [128, 32, NSMP, 3], F32)
                            NEWP = k.buf("newp", [128, 32, NSMP], F32)
                            if not k.dry:
                                nc.sync.dma_start(out=oxs[j, :, 0:2, :], in_=stsc[j, :, :].rearrange("(b r) c -> b r c", r=3)[:, 1:3, :]).then_inc(dsem_dd, 16)
                                k.ddcnt += 16
                            for g4 in range(4):
                                STX = STXS[g4 % 2]
                                k.dma(k.sp, STX.t[:, :], stsc[j, :, g4 * 1024:(g4 + 1) * 1024], writes=[STX])
                                pb = ps()
                                for q in range(8):
                                    TR(pb, pb.t[:, q * 48:(q + 1) * 48], STX, STX.t[:, q * 128:(q + 1) * 128], identf[:48, :48], inc=(q == 7), extra=[CST])
                                CP(STT_.t[:, g4 * 8:(g4 + 1) * 8, :, :], pb.t[:, 0:384].rearrange("p (c b r) -> p c b r", c=8, b=NSMP), [pb], [STT_])
                            ctm = k.buf("ctm", [128, NSMP, 3], F32)
                            cr = k.buf("cr", [128, NSMP], F32)
                        sls = {}

                        def emit_proj(cc):
                            it, e = divmod(cc, 2)
                            if e == 0:
                                sls[it] = wload([(win[j, :, DIN + it * 256: DIN + (it + 1) * 256], 8, 256, 0)])
                            sl = sls[it]
                            lw = [sl.t[:, kk * 256 + e * 128: kk * 256 + (e + 1) * 128] for kk in range(8)]
                            pa = ps()
                            MM(pa, pa.t[:, :512], [(lw[kk], HN.t[:, kk, 0:512]) for kk in range(8)], [sl, HN])
                            pq = None
                            if smp:
                                pq = ps()
                                MM(pq, pq.t[:, :NSMP], [(lw[kk], HN.t[:, kk, 512:NL]) for kk in range(8)], [sl, HN])
                            return pa, pq

                        stA = {}
                        stB = {}
                        stC = {}

                        def stage_B(cc):
                            nonlocal ci
                            pa, pq = stA.pop(cc)
                            pre = PRE[ci % 2]
                            dgs = DGS[ci % 2]
                            ci += 1
                            TT(dgs.t[:, :, :], identb.unsqueeze(1).to_broadcast([128, 4, 128]), V(f"scw{j}", cc * 4, 4).unsqueeze(2).to_broadcast([128, 4, 128]), ALU.mult, [CB, VEC], [dgs])
                            CP(pre.t[:, 0:3], HIST.t[:, cc, :], [HISTb[cc]], [pre])
                            ACT(pre.t[:, 3:515], pa.t[:, :512], AF.Copy, [pa], [pre])
                            CP(HIST.t[:, cc, :], pa.t[:, 509:512], [pa], [HISTb[cc]])
                            if smp:
                                CP(NEWP.t[:, cc, :], pq.t[:, :NSMP], [pq], [NEWP])
                                TT(ctm.t[:, :, :], STT_.t[:, cc, :, :], V(f"scw{j}", cc * 4, 3).unsqueeze(1).to_broadcast([128, NSMP, 3]), ALU.mult, [STT_, VEC], [ctm])
                                k.op(k.dve, lambda: nc.vector.tensor_reduce(out=cr.t[:, :], in_=ctm.t[:, :, :], axis=AX.X, op=ALU.add), [ctm], [cr])
                                STT(cr.t[:, :], pq.t[:, :NSMP], V(f"scw{j}", cc * 4 + 3), cr.t[:, :], ALU.mult, ALU.add, [pq, cr, VEC], [cr])
                                ACT(XB.t[:, cc, 512:NL], cr.t[:, :], AF.Silu, [cr, VEC], [XBb[cc]], bias=V(f"scb{j}", cc))
                            stB[cc] = (pre, dgs)

                        def stage_C(cc):
                            pre, dgs = stB.pop(cc)
                            pc = ps()
                            MM(pc, pc.t[:, :512], [(dgs.t[:, kk, :], pre.t[:, kk:kk + 512]) for kk in range(4)], [dgs, pre])
                            stC[cc] = pc

                        def stage_D(cc):
                            pc = stC.pop(cc)
                            ACT(XB.t[:, cc, 0:512], pc.t[:, :512], AF.Silu, [pc, VEC], [XBb[cc]], bias=V(f"scb{j}", cc))

                        for it_ in range(-2, 33):
                            if 0 <= it_ + 2 < 32:
                                stA[it_ + 2] = emit_proj(it_ + 2)
                            if 0 <= it_ + 1 < 32:
                                stage_B(it_ + 1)
                            if 0 <= it_ < 32:
                                stage_C(it_)
                            if 0 <= it_ - 1 < 32:
                                stage_D(it_ - 1)
                        if smp:
                            OX2 = k.buf("ox2", [NSMP, 1024], F32)
                            for g4 in range(4):
                                for hf in range(2):
                                    pb = ps()
                                    for q in range(4):
                                        cc = g4 * 8 + hf * 4 + q
                                        TR(pb, pb.t[:NSMP, q * 128:(q + 1) * 128], NEWP, NEWP.t[:, cc, :], identf, inc=(q == 3), extra=[CST])
                                    CP(OX2.t[:, hf * 512:(hf + 1) * 512], pb.t[:NSMP, :], [pb], [OX2])
                                k.dma(k.sp, oxs[j, :, 2, g4 * 1024:(g4 + 1) * 1024], OX2.t[:, :], reads=[OX2], final=True)
                            xph.__exit__(None, None, None)
                        k.mark(f"  L{i} t{tix} xBC+conv end")
                        sld = wload([(win[j, :, DIN + CONVD: DIN + CONVD + 32], 8, 32, 0)])
                        with k.phase():
                            Rg = [k.buf(f"Rg{a}", [128, 4, 128], F32) for a in range(3)]
                            Lg = [k.buf(f"Lg{a}", [128, 4, 128], BF16) for a in range(3)]
                            MTg = [k.buf(f"MT{a}", [128, 4, 128], BF16) for a in range(3)]
                            CBM = k.buf("cbm", [128, 8, 128], BF16)
                            CBMb = k.sub(CBM, 8, "g")
                            XDT = k.buf("xdt", [128, DIN], BF16)
                            XDD = k.buf("xdd", [128, DIN], BF16)
                            XDTb = k.sub(XDT, 2, "h")
                            XDDb = k.sub(XDD, 2, "h")
                            BTM = k.buf("btm", [128, 1024], BF16)
                            YTM = k.buf("ytm", [128, DIN], BF16)
                            YTMb = k.sub(YTM, 4, "q")
                            T1s = [k.buf(f"t1{a}", [128, 512], BF16) for a in range(1)]
                            sms = [{nm: k.buf(nm + str(a), [128, 32], F32) for nm in ("dt", "a", "acs", "ea", "cd", "dd", "dec", "dtd", "e1")} for a in range(2)]

                            def prologue_stage(q, st):
                                sm = sms[q % 2]
                                lo = q * 128
                                if st == 0:
                                    pd = ps()
                                    MM(pd, pd.t[:, 0:32], [(HN.t[:, kk, lo:lo + 128], sld.t[:, kk * 32:(kk + 1) * 32]) for kk in range(8)], [sld, HN])
                                    TT(sm["e1"].t[:, :], pd.t[:, 0:32], V(f"dtbbc{j}", 0, 32), ALU.add, [pd, VEC], [sm["e1"]])
                                elif st == 1:
                                    ACT(sm["e1"].t[:, :], sm["e1"].t[:, :], AF.Exp, [sm["e1"]], [sm["e1"]])
                                    ACT(sm["dt"].t[:, :], sm["e1"].t[:, :], AF.Ln, [sm["e1"]], [sm["dt"]], bias=1.0)
                                elif st == 2:
                                    TT(sm["a"].t[:, :], sm["dt"].t[:, :], ABC.t[:, :], ALU.mult, [sm["dt"], ABC], [sm["a"]])
                                elif st == 3:
                                    pcs = ps()
                                    sm["pcs"] = pcs
                                    MM(pcs, pcs.t[:, 0:32], [(trif, sm["a"].t[:, :])], [CST, sm["a"]], inc=False)
                                    MM(pcs, pcs.t[:, 32:64], [(onesf, sm["a"].t[:, :])], [CST, sm["a"]])
                                elif st == 4:
                                    pcs = sm["pcs"]
                                    ACT(sm["acs"].t[:, :], pcs.t[:, 0:32], AF.Copy, [pcs], [sm["acs"]])
                                    ACT(sm["ea"].t[:, :], pcs.t[:, 0:32], AF.Exp, [pcs], [sm["ea"]])
                                    ACT(sm["cd"].t[:, :], pcs.t[:, 32:64], AF.Exp, [pcs], [sm["cd"]])
                                elif st == 5:
                                    pcs = sm["pcs"]
                                    TT(sm["dd"].t[:, :], pcs.t[:, 32:64], sm["acs"].t[:, :], ALU.subtract, [pcs, sm["acs"]], [sm["dd"]])
                                elif st == 6:
                                    ACT(sm["dec"].t[:, :], sm["dd"].t[:, :], AF.Exp, [sm["dd"]], [sm["dec"]])
                                elif st == 7:
                                    TT(sm["dtd"].t[:, :], sm["dt"].t[:, :], sm["dec"].t[:, :], ALU.mult, [sm["dt"], sm["dec"]], [sm["dtd"]])

                            def prologue(q):
                                for st in range(8):
                                    prologue_stage(q, st)

                            prologue(0)
                            for q in range(4):
                                sm = sms[q % 2]
                                lo = q * 128
                                pts = []
                                for hb in range(2):
                                    pt = ps()
                                    ptb = pt.t[:, :].bitcast(BF16)
                                    for e in range(8):
                                        hh = hb * 8 + e
                                        TR(pt, ptb[:, e * 128:(e + 1) * 128], XBb[hh], XB.t[:, hh, lo:lo + 128], identb, inc=(e == 7), extra=[CB])
                                    pts.append((pt, ptb))
                                ptB = ps()
                                ptBb = ptB.t[:, :].bitcast(BF16)
                                for g in range(8):
                                    TR(ptB, ptBb[:, g * 128:(g + 1) * 128], XBb[16 + g], XB.t[:, 16 + g, lo:lo + 128], identb, inc=(g == 7), extra=[CB])
                                pcbs = []
                                for hf in range(2):
                                    pcb = ps()
                                    for e in range(4):
                                        g = hf * 4 + e
                                        MM(pcb, pcb.t[:, e * 128:(e + 1) * 128], [(XB.t[:, 16 + g, lo:lo + 128], XB.t[:, 24 + g, lo:lo + 128])], [XBb[16 + g], XBb[24 + g]], inc=(e == 3))
                                    pcbs.append(pcb)
                                rg_of = {}

                                def emit_R(g):
                                    rg = Rg[g % 3]
                                    rg_of[g] = rg
                                    PTT(rg.t[:, :, :], sm["a"].t[:, g * 4:(g + 1) * 4].unsqueeze(2).to_broadcast([128, 4, 128]), trif.unsqueeze(1).to_broadcast([128, 4, 128]), ALU.mult, [sm["a"], CST], [rg])

                                for g in range(3):
                                    emit_R(g)
                                for hb in range(2):
                                    pt, ptb = pts[hb]
                                    pv3 = ptb.rearrange("p (h d) -> p h d", d=64)
                                    TT(XDT.t[:, hb * 1024:(hb + 1) * 1024].rearrange("p (h d) -> p h d", d=64), pv3, sm["dt"].t[:, hb * 16:(hb + 1) * 16].unsqueeze(2).to_broadcast([128, 16, 64]), ALU.mult, [pt, sm["dt"]], [XDTb[hb]])
                                    TT(XDD.t[:, hb * 1024:(hb + 1) * 1024].rearrange("p (h d) -> p h d", d=64), pv3, sm["dtd"].t[:, hb * 16:(hb + 1) * 16].unsqueeze(2).to_broadcast([128, 16, 64]), ALU.mult, [pt, sm["dtd"]], [XDDb[hb]])
                                ACT(BTM.t[:, :], ptBb, AF.Copy, [ptB], [BTM])
                                for hf in range(2):
                                    TT(CBM.t[:, hf * 4:(hf + 1) * 4, :], pcbs[hf].t[:, :].rearrange("p (a b) -> p a b", a=4), trif.unsqueeze(1).to_broadcast([128, 4, 128]), ALU.mult, [pcbs[hf], CST], CBMb[hf * 4:(hf + 1) * 4])
                                psegs = {}

                                def emit_seg(g):
                                    rg = rg_of[g]
                                    pseg = ps()
                                    MM(pseg, pseg.t[:, :], [(ustrf, rg.t[:, :, :].rearrange("p a b -> p (a b)"))], [CST, rg])
                                    if g + 3 < 8:
                                        emit_R(g + 3)
                                    lg = Lg[g % 3]
                                    ACT(lg.t[:, :, :].rearrange("p a b -> p (a b)"), pseg.t[:, :], AF.Exp, [pseg], [lg])
                                    mt = MTg[g % 3]
                                    TT(mt.t[:, :, :], lg.t[:, :, :], CBM.t[:, g, :].unsqueeze(1).to_broadcast([128, 4, 128]), ALU.mult, [lg, CBMb[g]], [mt])
                                    return mt

                                mts = {0: emit_seg(0), 1: emit_seg(1)}
                                pyd = pyo = None
                                for g in range(8):
                                    b4, e2_ = divmod(g, 2)
                                    if q + 1 < 4:
                                        prologue_stage(q + 1, g)
                                    if g + 2 < 8:
                                        mts[g + 2] = emit_seg(g + 2)
                                    if e2_ == 0:
                                        pyd = ps()
                                        pyo = ps()
                                    mt = mts[g]
                                    for r4 in range(4):
                                        h = g * 4 + r4
                                        e = e2_ * 4 + r4
                                        MM(pyd, pyd.t[:, e * 64:(e + 1) * 64], [(mt.t[:, r4, :], XDT.t[:, h * 64:(h + 1) * 64])], [mt, XDTb[h // 16]], inc=(r4 == 3))
                                    MM(pyo, pyo.t[:, e2_ * 256:(e2_ + 1) * 256], [(XB.t[:, 24 + g, lo:lo + 128], SB.t[:, g * 256:(g + 1) * 256])], [XBb[24 + g], SBb[b4]])
                                    if e2_ == 1:
                                        T1 = T1s[0]
                                        TT(T1.t[:, :].rearrange("p (h d) -> p h d", d=64), pyo.t[:, :].rearrange("p (h d) -> p h d", d=64), sm["ea"].t[:, b4 * 8:(b4 + 1) * 8].unsqueeze(2).to_broadcast([128, 8, 64]), ALU.mult, [pyo, sm["ea"]], [T1])
                                        TT(YTM.t[:, b4 * 512:(b4 + 1) * 512], pyd.t[:, :], T1.t[:, :], ALU.add, [pyd, T1], [YTMb[b4]])
                                for b4 in range(4):
                                    pst = ps()
                                    for e in range(2):
                                        g = b4 * 2 + e
                                        MM(pst, pst.t[:, e * 256:(e + 1) * 256], [(BTM.t[:, g * 128:(g + 1) * 128], XDD.t[:, g * 256:(g + 1) * 256])], [BTM, XDDb[g // 4]], inc=(e == 1))
                                    sv = S.t[:, b4 * 512:(b4 + 1) * 512]
                                    TT(sv.rearrange("p (h d) -> p h d", d=64), sv.rearrange("p (h d) -> p h d", d=64), sm["cd"].t[:, b4 * 8:(b4 + 1) * 8].unsqueeze(2).to_broadcast([128, 8, 64]), ALU.mult, [Sb[b4], sm["cd"]], [Sb[b4]])
                                    TT(sv, pst.t[:, :], sv, ALU.add, [pst, Sb[b4]], [Sb[b4]])
                                    ACT(SB.t[:, b4 * 512:(b4 + 1) * 512], sv, AF.Copy, [Sb[b4]], [SBb[b4]])
                                for hb in range(2):
                                    pt = ps()
                                    ptb = pt.t[:, :].bitcast(BF16)
                                    for e in range(8):
                                        hh = hb * 8 + e
                                        TR(pt, ptb[:, e * 128:(e + 1) * 128], YTMb[hh // 4], YTM.t[:, hh * 128:(hh + 1) * 128], identb, inc=(e == 7), extra=[CB])
                                    ACT(YT.t[:, hb * 8:(hb + 1) * 8, lo:lo + 128], ptb.rearrange("p (a b) -> p a b", a=8), AF.Copy, [pt], YTb[hb * 8:(hb + 1) * 8])
                                k.mark(f"    L{i} t{tix} chunk{q} end")
                        if smp:
                            with k.phase():
                                SO = k.buf("so", [128, 16, 128], F32)
                                for g4 in range(4):
                                    pb = ps()
                                    for q in range(4):
                                        blk = g4 * 4 + q
                                        TR(pb, pb.t[:, q * 128:(q + 1) * 128], Sb[g4], S.t[:, blk * 128:(blk + 1) * 128], identf, inc=(q == 3), extra=[CST])
                                    CP(SO.t[:, g4 * 4:(g4 + 1) * 4, :], pb.t[:, :].rearrange("p (a b) -> p a b", a=4), [pb], [SO])
                                k.dma(k.sp, osp[j, :, :].rearrange("(a p) n -> p a n", p=128), SO.t[:, :, :], reads=[SO], final=True)
                                OXS = [k.buf(f"ox{a}", [3, 1024], F32) for a in range(2)]
                                for g4 in range(4):
                                    OX = OXS[g4 % 2]
                                    for hf in range(2):
                                        pb = ps()
                                        for q in range(4):
                                            cc = g4 * 8 + hf * 4 + q
                                            TR(pb, pb.t[:3, q * 128:(q + 1) * 128], HISTb[cc], HIST.t[:, cc, :], identf, inc=(q == 3), extra=[CST])
                                        CP(OX.t[:, hf * 512:(hf + 1) * 512], pb.t[:3, :], [pb], [OX])
                                    k.dma(k.sp, oxp[j, :, g4 * 1024:(g4 + 1) * 1024], OX.t[:, :], reads=[OX], final=True)
                            with k.phase():
                                DBC = k.buf("dbc", [128, 16, 32], F32)
                                XDS = k.buf("xds", [128, 16, NSMP], F32)
                                YS = k.buf("ysm", [128, 16, NSMP], F32)
                                with k.phase():
                                    SEL = k.buf("sel", [32, 2048], F32)
                                    k.dma(k.sp, SEL.t[:, :], sel_d[:, :], writes=[SEL])
                                    DTF = k.buf("dtf", [32, 32], F32)
                                    e2 = k.buf("e2", [32, NSMP], F32)
                                    a32 = k.buf("a32", [32, 1], F32)
                                    ACT(a32.t[:, :], V(f"alog32{j}")[:32, :], AF.Exp, [VEC], [a32])
                                    TS(a32.t[:, :], a32.t[:, :], -1.0, None, ALU.mult, None, [a32], [a32])
                                    pd = ps()
                                    MM(pd, pd.t[:32, 0:NSMP], [(sld.t[:, kk * 32:(kk + 1) * 32], HN.t[:, kk, 512:NL]) for kk in range(8)], [sld, HN])
                                    ACT(e2.t[:, :], pd.t[:32, 0:NSMP], AF.Exp, [pd, VEC], [e2], bias=V(f"dtb32{j}")[:32, :])
                                    ACT(DTF.t[:, 0:16], e2.t[:, :], AF.Ln, [e2], [DTF], bias=1.0)
                                    ACT(DTF.t[:, 16:32], DTF.t[:, 0:16], AF.Exp, [DTF, a32], [DTF], scale=a32.t[:, :])
                                    pbq = ps()
                                    for hh in range(16):
                                        MM(pbq, pbq.t[:, hh * 32:(hh + 1) * 32], [(SEL.t[:, hh * 128:(hh + 1) * 128], DTF.t[:, :])], [SEL, DTF], inc=(hh == 15))
                                    CP(DBC.t[:, :, :], pbq.t[:, :].rearrange("p (a b) -> p a b", a=16), [pbq], [DBC])
                                TT(XDS.t[:, :, :], XB.t[:, 0:16, 512:NL], DBC.t[:, :, 0:16], ALU.mult, XBb[0:16] + [DBC], [XDS])
                                DB = k.buf("dgb", [128, 8, 128], BF16)
                                DC = k.buf("dgc", [128, 8, 128], BF16)
                                CS = k.buf("cbs", [128, 8, 128], BF16)
                                SS = [k.buf(f"ss{a}", [128, 16, 128], F32) for a in range(2)]
                                ssb = [k.sub(SS[a], 16, "h") for a in range(2)]
                                TA = k.buf("ta", [128, 16, 128], BF16)

                                def s_load(b):
                                    k.dma(k.sp, SS[b % 2].t[:, :, :], stss[j, b, :, :].rearrange("(a p) n -> p a n", p=128), writes=ssb[b % 2])

                                def s_prep(b):
                                    PTT(DB.t[:, :, :], identb.unsqueeze(1).to_broadcast([128, 8, 128]), XB.t[:, 16:24, 512 + b:512 + b + 1].to_broadcast([128, 8, 128]), ALU.mult, [CB] + XBb[16:24], [DB])
                                    PTT(DC.t[:, :, :], identb.unsqueeze(1).to_broadcast([128, 8, 128]), XB.t[:, 24:32, 512 + b:512 + b + 1].to_broadcast([128, 8, 128]), ALU.mult, [CB] + XBb[24:32], [DC])
                                    pbc = [ps() for _ in range(4)]
                                    for a in range(2):
                                        MM(pbc[a], pbc[a].t[:, :], [(onesb, DB.t[:, a * 4:(a + 1) * 4, :].rearrange("p a b -> p (a b)"))], [CB, DB])
                                    for a in range(2):
                                        MM(pbc[2 + a], pbc[2 + a].t[:, :], [(onesb, DC.t[:, a * 4:(a + 1) * 4, :].rearrange("p a b -> p (a b)"))], [CB, DC])
                                    return pbc

                                s_load(0)
                                pbc_next = s_prep(0)
                                for b in range(NSMP):
                                    ss = SS[b % 2]
                                    if b + 1 < NSMP:
                                        s_load(b + 1)
                                    pbc = pbc_next
                                    if b + 1 < NSMP:
                                        pbc_next = s_prep(b + 1)
                                    for hh in range(16):
                                        ACT(ss.t[:, hh, :], ss.t[:, hh, :], AF.Copy, [ssb[b % 2][hh], DBC], [ssb[b % 2][hh]], scale=DBC.t[:, hh, 16 + b:17 + b])
                                    for hh in range(16):
                                        g = hh // 2
                                        STT(ss.t[:, hh, :], pbc[g // 4].t[:, (g % 4) * 128:(g % 4 + 1) * 128], XDS.t[:, hh, b:b + 1], ss.t[:, hh, :], ALU.mult, ALU.add, [pbc[g // 4], XDS, ssb[b % 2][hh]], [ssb[b % 2][hh]])
                                    k.dma(k.sp, oss[j, b, :, :].rearrange("(a p) n -> p a n", p=128), ss.t[:, :, :], reads=ssb[b % 2], final=True)
                                    for a in range(2):
                                        ACT(CS.t[:, a * 4:(a + 1) * 4, :], pbc[2 + a].t[:, :].rearrange("p (g n) -> p g n", n=128), AF.Copy, [pbc[2 + a]], [CS])
                                    PTT(TA.t[:, :, :].rearrange("p (g e) n -> p g e n", e=2),
                                        ss.t[:, :, :].rearrange("p (g e) n -> p g e n", e=2),
                                        CS.t[:, :, :].unsqueeze(2).to_broadcast([128, 8, 2, 128]),
                                        ALU.mult, ssb[b % 2] + [CS], [TA])
                                    k.op(k.dve, lambda b=b: nc.vector.tensor_reduce(out=YS.t[:, :, b], in_=TA.t[:, :, :], axis=AX.X, op=ALU.add), [TA], [YS])
                                CP(YT.t[:, :, 512:NL], YS.t[:, :, :], [YS], YTb)
                        k.mark(f"  L{i} t{tix} core/sample end")
                        zi = 0
                        for it in range(8):
                            sl = wload([(win[j, :, it * 256:(it + 1) * 256], 8, 256, 0)])
                            for e in range(2):
                                zc = it * 2 + e
                                for (l, g, n) in lsubs:
                                    pz = ps()
                                    MM(pz, pz.t[:, :n], [(sl.t[:, kk * 256 + e * 128: kk * 256 + (e + 1) * 128], HN.t[:, kk, l:l + n]) for kk in range(8)], [sl, HN])
                                    gq = GS[zi % 2]
                                    zi += 1
                                    ACT(gq.t[:, 0, :n], pz.t[:, :n], AF.Silu, [pz], [gq])
                                    TS(gq.t[:, 1, :n], XB.t[:, zc, l:l + n], V(f"sD{j}", zc), None, ALU.mult, None, [XBb[zc], VEC, gq], [gq])
                                    TT(YT.t[:, zc, l:l + n], YT.t[:, zc, l:l + n], gq.t[:, 1, :n], ALU.add, [YTb[zc], gq], [YTb[zc]])
                                    TT(YT.t[:, zc, l:l + n], YT.t[:, zc, l:l + n], gq.t[:, 0, :n], ALU.mult, [YTb[zc], gq], [YTb[zc]])
                        k.mark(f"  L{i} t{tix} z end")
                        gjobs = [(g8, l, g, n) for g8 in range(8) for (l, g, n) in lsubs]

                        def gn_sq(idx):
                            g8, l, g, n = gjobs[idx]
                            gq = GS[idx % 2]
                            ACT(gq.t[:, 0:2, :n], YT.t[:, 2 * g8:2 * g8 + 2, l:l + n], AF.Square, YTb[2 * g8:2 * g8 + 2], [gq])
                            pn = ps()
                            MM(pn, pn.t[:, :n], [(onesb, gq.t[:, e, :n]) for e in range(2)], [gq, CB])
                            return pn

                        pn_next = gn_sq(0)
                        for idx, (g8, l, g, n) in enumerate(gjobs):
                            pn = pn_next
                            if idx + 1 < len(gjobs):
                                pn_next = gn_sq(idx + 1)
                            rt = rts[idx % 2]
                            RSTD(rt, n, pn.t[:, :n], pn, 1.0 / 256)
                            for e in range(2):
                                STT(YT.t[:, 2 * g8 + e, l:l + n], YT.t[:, 2 * g8 + e, l:l + n], V(f"snorm{j}", 2 * g8 + e), rt.t[:, :n], ALU.mult, ALU.mult, [YTb[2 * g8 + e], rt, VEC], [YTb[2 * g8 + e]])
                        for m in range(8):
                            sl = wload([(wout[j, :, m * 128:(m + 1) * 128], 16, 128, 0)])
                            for (l, g, n) in lsubs:
                                po = ps()
                                MM(po, po.t[:, :n], [(wv(sl, 0, kk, 128), YT.t[:, kk, l:l + n]) for kk in range(16)], [sl] + YTb)
                                TT(X.t[:, m, g:g + n], po.t[:, :n], X.t[:, m, g:g + n], ALU.add, [po, X1(m, g)], [X1(m, g)])

            k.mark("start")
            for i in range(DEPTH):
                ffn(i, 1)
                k.mark(f"L{i} ffn1 end")
                if i % 2 == 0:
                    conv_mixer(i, i // 2)
                else:
                    ssd_mixer(i, i // 2)
                k.mark(f"L{i} mixer end")
                ffn(i, 2)
                k.mark(f"L{i} ffn2 end")
                ple(i)
                k.mark(f"L{i} ple end")

            with k.phase():
                YN = k.buf("YN", [128, 8, NT], F32)
                sqs = [k.buf(f"sq{a}", [128, 8, 512], BF16) for a in range(2)]
                rts = [k.buf(f"rt{a}", [128, 512], F32) for a in range(2)]
                ost = [k.buf(f"yo{a}", [128, D], F32) for a in range(2)]
                YNb = k.sub(YN, 5, "t")
                rmsnorm("final_norm", lambda do: [YNb[tix_of(do)]], lambda kk, do, n: YN.t[:, kk, do:do + n], [(s, s, n) for s, n in TILES], sqs, rts)
                for r in range(17):
                    R = 128 if r < 16 else NSMP
                    o = ost[r % 2]
                    for half in range(2):
                        pb = ps()
                        for q in range(4):
                            c = half * 4 + q
                            TR(pb, pb.t[:R, q * 128:(q + 1) * 128], YNb[tix_of(r * 128)], YN.t[:, c, r * 128:r * 128 + R], identf, inc=(q == 3), extra=[CST])
                        if half == 0:
                            ACT(o.t[:R, 0:512], pb.t[:R, :], AF.Copy, [pb], [o])
                        else:
                            CP(o.t[:R, 512:1024], pb.t[:R, :], [pb], [o])
                    dst = yp[r * 128:(r + 1) * 128, :] if r < 16 else ysd[:, :]
                    k.dma(k.sp, dst, o.t[:R, :], reads=[o], final=True)
                if not k.dry:
                    for dep in k.final.values():
                        k._wait(k.sp, dep)
                    if k.ddcnt:
                        nc.sync.wait_ge(dsem_dd, k.ddcnt)

        k.dry = True
        k.wplan = []
        k.ddcnt = 0
        body()
        k.dry = False
        k.wissued = 0
        k.ddcnt = 0
        body()
        build.marks = getattr(k, "marks", [])
    return nc


_NC_CACHE = {}


def make_in_maps(inp):
    inp = {k_: np.asarray(v) for k_, v in inp.items()}
    vecs = np.ascontiguousarray(np.concatenate([a for (_, _, a) in vec_entries(inp)], axis=1), dtype=np.float32)
    cst = const_array()
    sel = sel_array()
    wnames = ["w_ffn1_gate", "w_ffn1_up", "w_ffn1_down", "w_ffn2_gate", "w_ffn2_up", "w_ffn2_down", "w_ple_gate", "w_ple_proj",
              "cm_w_in", "cm_w_out", "ssd_w_in", "ssd_w_out"]
    shared = {nm: np.ascontiguousarray(inp[nm], dtype=np.float32) for nm in wnames}
    shared.update(vecs=vecs, cst=cst, sel=sel)
    in_maps = []
    for c in range(NCORES):
        b0 = c * NSMP
        m = dict(shared)
        m["xp"] = np.ascontiguousarray(inp["x_prompt"][c], dtype=np.float32)
        m["xs"] = np.ascontiguousarray(inp["x_sample"][b0:b0 + NSMP, 0], dtype=np.float32)
        m["stc"] = np.ascontiguousarray(inp["state_conv"][:, b0:b0 + NSMP].reshape(2, NSMP * 30, D), dtype=np.float32)
        m["stsc"] = np.ascontiguousarray(inp["state_ssd_conv"][:, b0:b0 + NSMP].reshape(2, NSMP * 3, CONVD), dtype=np.float32)
        m["stss"] = np.ascontiguousarray(inp["state_ssd"][:, b0:b0 + NSMP].reshape(2, NSMP, NH * 64, 128), dtype=np.float32)
        m["pp"] = np.ascontiguousarray(inp["p_prompt"][:, c], dtype=np.float32)
        m["psm"] = np.ascontiguousarray(inp["p_sample"][:, b0:b0 + NSMP, 0], dtype=np.float32)
        in_maps.append(m)
    return in_maps


def kernel(**inp):
    if "nc" not in _NC_CACHE:
        _NC_CACHE["nc"] = build()
    nc = _NC_CACHE["nc"]
    in_maps = make_in_maps(inp)
    res = run_bass_kernel_spmd(nc, in_maps, core_ids=list(range(NCORES)))
    R = res.results
    y_prompt = np.stack([R[c]["yp"] for c in range(NCORES)], 0).astype(np.float32)
    y_sample = np.concatenate([R[c]["ys"] for c in range(NCORES)], 0).reshape(NCORES * NSMP, 1, D).astype(np.float32)
    conv_p = np.stack([R[c]["ocp"] for c in range(NCORES)], 1).astype(np.float32)
    xbc_p = np.stack([R[c]["oxp"] for c in range(NCORES)], 1).astype(np.float32)
    ssm_p = np.stack([R[c]["osp"].reshape(2, NH, 64, 128) for c in range(NCORES)], 1).astype(np.float32)
    conv_s = np.concatenate([R[c]["ocs"] for c in range(NCORES)], 1).astype(np.float32)
    xbc_s = np.concatenate([R[c]["oxs"] for c in range(NCORES)], 1).astype(np.float32)
    ssm_s = np.concatenate([R[c]["oss"].reshape(2, NSMP, NH, 64, 128) for c in range(NCORES)], 1).astype(np.float32)
    return (y_prompt, y_sample, conv_p, xbc_p, ssm_p, conv_s, xbc_s, ssm_s)
```

```python
import numpy as np
from contextlib import ExitStack, contextmanager
import concourse.bass as bass
import concourse.mybir as mybir
from concourse.bass_utils import run_bass_kernel_spmd

F32 = mybir.dt.float32
BF16 = mybir.dt.bfloat16
AF = mybir.ActivationFunctionType
ALU = mybir.AluOpType
AX = mybir.AxisListType
P = 128
D = 1024
DFF = 2816
T = 2048
NSMP = 16
NT = T + NSMP
DEPTH = 4
DIN = 2048
CONVD = 4096
NH = 32
PLE = 256
CK = 31
EPS = 1e-6
NSLOT = 5
SLOTE = 2048
NCORES = 8
TILES = [(0, 512), (512, 512), (1024, 512), (1536, 512), (2048, 16)]


def fm(v):
    v = np.asarray(v, np.float32).reshape(-1)
    return np.ascontiguousarray(v.reshape(-1, 128).T)


def vec_entries(inp):
    E = []

    def add(name, n, fn):
        if inp is None:
            E.append((name, n, None))
        else:
            a = np.zeros((128, n), np.float32)
            b = fn()
            a[: b.shape[0], :] = b
            E.append((name, n, a))

    for i in range(DEPTH):
        for nm in ("norm_ffn1", "norm_mix", "norm_ffn2", "norm_ple"):
            add(f"{nm}{i}", 8, lambda nm=nm, i=i: fm(inp[nm][i]))
    add("final_norm", 8, lambda: fm(inp["final_norm"]))
    for j in range(2):
        add(f"cbin{j}", 16, lambda j=j: fm(inp["cm_b_in"][j]))
        add(f"cdw{j}", 8 * CK, lambda j=j: np.asarray(inp["cm_dw"][j], np.float32).T.reshape(8, 128, CK).transpose(1, 0, 2).reshape(128, 8 * CK))
        add(f"cdwb{j}", 8, lambda j=j: fm(inp["cm_dw_b"][j]))
        add(f"clng{j}", 8, lambda j=j: fm(inp["cm_ln_g"][j]))
        add(f"clnb{j}", 8, lambda j=j: fm(inp["cm_ln_b"][j]))
        add(f"cbout{j}", 8, lambda j=j: fm(inp["cm_b_out"][j]))
    for j in range(2):
        add(f"scw{j}", 32 * 4, lambda j=j: np.asarray(inp["ssd_conv_w"][j], np.float32).T.reshape(32, 128, 4).transpose(1, 0, 2).reshape(128, 128))
        add(f"scb{j}", 32, lambda j=j: fm(inp["ssd_conv_b"][j]))
        add(f"snorm{j}", 16, lambda j=j: fm(inp["ssd_norm"][j]))
        add(f"sD{j}", 16, lambda j=j: np.repeat(np.asarray(inp["ssd_D"][j], np.float32).reshape(16, 2, 1), 64, axis=2).transpose(1, 2, 0).reshape(128, 16))
        add(f"dtb32{j}", 1, lambda j=j: np.asarray(inp["ssd_dt_bias"][j], np.float32).reshape(32, 1))
        add(f"alog32{j}", 1, lambda j=j: np.asarray(inp["ssd_A_log"][j], np.float32).reshape(32, 1))
        add(f"dtbbc{j}", 32, lambda j=j: np.tile(np.asarray(inp["ssd_dt_bias"][j], np.float32).reshape(1, 32), (128, 1)))
        add(f"alogbc{j}", 32, lambda j=j: np.tile(np.asarray(inp["ssd_A_log"][j], np.float32).reshape(1, 32), (128, 1)))
    return E


def vec_offsets():
    off = {}
    o = 0
    for name, n, _ in vec_entries(None):
        off[name] = o
        o += n
    return off, o


def const_array():
    c = np.zeros((128, 512), np.float32)
    i = np.arange(128)
    c[:, 0:128] = np.eye(128, dtype=np.float32)
    c[:, 128:256] = (i[:, None] <= i[None, :]).astype(np.float32)
    c[:, 256:384] = (i[:, None] > i[None, :]).astype(np.float32)
    c[:, 384:512] = 1.0
    return c


def sel_array():
    s = np.zeros((32, 16, 128), np.float32)
    for hh in range(16):
        for m in range(128):
            s[2 * hh + m // 64, hh, m] = 1.0
    return s.reshape(32, 2048)


class Buf:
    def __init__(self, t, name):
        self.t = t
        self.name = name
        self.w = None
        self.r = {}
        self.dsem = None


class Eng:
    def __init__(self, raw, sem, name):
        self.raw = raw
        self.sem = sem
        self.name = name
        self.cnt = 0
        self.seen = {}


class K:
    def __init__(self, nc, es):
        self.nc = nc
        self.es = es
        self.dry = False
        self.alloc = es
        self.phase_bufs = None
        self.sem_free = []
        self.sem_tot = {}
        self.pending = {}
        self.final = {}
        self.nsem = 0
        self.uid = 0

    def init_engines(self):
        nc, es = self.nc, self.es

        def mk(raw, name):
            return Eng(raw, es.enter_context(nc.semaphore("sem_" + name)), name)

        self.pe = mk(nc.tensor, "pe")
        self.act = mk(nc.scalar, "act")
        self.dve = mk(nc.vector, "dve")
        self.pool = mk(nc.gpsimd, "pool")
        self.sp = mk(nc.sync, "sp")
        self.engs = [self.pe, self.act, self.dve, self.pool, self.sp]
        for e in self.engs:
            e.cnt = 0
            e.seen = {}

    def buf(self, name, shape, dt):
        self.uid += 1
        t = self.alloc.enter_context(self.nc.sbuf_tensor(f"{name}_{self.uid}", list(shape), dt))
        b = Buf(t, name)
        if self.phase_bufs is not None:
            self.phase_bufs.append(b)
            b.in_phase = True
        return b

    def sub(self, b, n, tag=""):
        out = []
        for i in range(n):
            s = Buf(b.t, f"{b.name}{tag}{i}")
            s.in_phase = getattr(b, "in_phase", False)
            if self.phase_bufs is not None and s.in_phase:
                self.phase_bufs.append(s)
            out.append(s)
        return out

    def _getsem(self, b):
        if b.dsem is None:
            if self.sem_free and getattr(b, "in_phase", False):
                b.dsem = self.sem_free.pop()
            else:
                self.nsem += 1
                sem = self.es.enter_context(self.nc.semaphore(f"dsem{self.nsem}"))
                b.dsem = (f"dsem{self.nsem}", sem)
                self.sem_tot[b.dsem[0]] = 0
        return b.dsem

    def _wait(self, e, dep):
        if dep is None:
            return
        key, sem, val = dep
        if e.seen.get(key, 0) >= val:
            return
        e.raw.wait_ge(sem, val)
        e.seen[key] = val

    def _deps(self, e, reads, writes, isdma=False):
        for b in reads:
            self._wait(e, b.w)
        same_ok = (not isdma) and e.name == "pe"
        for b in writes:
            if b.w is not None and not (same_ok and b.w[0] == e.name):
                self._wait(e, b.w)
            for k, (sem, val) in b.r.items():
                if not (same_ok and k == e.name):
                    self._wait(e, (k, sem, val))

    def op(self, e, fn, reads=(), writes=(), inc=True):
        if self.dry:
            return None
        self._deps(e, reads, writes)
        ins = fn()
        self.opidx = getattr(self, "opidx", 0) + 1
        for b in reads:
            b.last_read = self.opidx
        for b in writes:
            b.last_write = self.opidx
        if e.name == "pe":
            self.npe = getattr(self, "npe", 0) + 1
        if inc:
            e.cnt += 1
            ins.then_inc(e.sem, 1)
            c = e.cnt
        else:
            c = e.cnt + 1
        for b in reads:
            b.r[e.name] = (e.sem, c)
        for b in writes:
            b.w = (e.name, e.sem, c)
            b.r = {}
        return ins

    def dma(self, q, out, in_, reads=(), writes=(), final=False, **kw):
        if self.dry:
            return
        self._deps(q, reads, writes, isdma=True)
        ins = q.raw.dma_start(out=out, in_=in_, **kw)
        b = (list(writes) + list(reads))[0]
        key, sem = self._getsem(b)
        self.sem_tot[key] += 16
        tot = self.sem_tot[key]
        ins.then_inc(sem, 16)
        for w in writes:
            w.w = (key, sem, tot)
            w.r = {}
        for r in reads:
            r.r[key] = (sem, tot)
        self.pending[key] = (key, sem, tot)
        if final:
            self.final[key] = (key, sem, tot)

    def mark(self, label):
        if not self.dry:
            self.marks = getattr(self, "marks", [])
            self.marks.append((label, getattr(self, "npe", 0)))

    def barrier(self):
        if self.dry:
            return
        engs = [self.pe, self.act, self.dve, self.sp, self.pool]
        for e in engs:
            for e2 in engs:
                if e2 is not e and e2.cnt > 0:
                    self._wait(e, (e2.name, e2.sem, e2.cnt))
            for dep in self.pending.values():
                self._wait(e, dep)
        self.pending = {}

    @contextmanager
    def phase(self):
        old_alloc, old_bufs = self.alloc, self.phase_bufs
        with ExitStack() as st:
            self.alloc = st
            self.phase_bufs = []
            yield
            self.barrier()
            for b in self.phase_bufs:
                if b.dsem is not None:
                    self.sem_free.append(b.dsem)
        self.alloc, self.phase_bufs = old_alloc, old_bufs


def build():
    nc = bass.Bass("TRN2", target_bir_lowering=False)
    voff, NV = vec_offsets()

    def din(name, shape):
        return nc.dram_tensor(name, list(shape), F32, kind="ExternalInput").ap()

    def dout(name, shape):
        return nc.dram_tensor(name, list(shape), F32, kind="ExternalOutput").ap()

    xp = din("xp", [T, D])
    xs = din("xs", [NSMP, D])
    stc = din("stc", [2, NSMP * 30, D])
    stsc = din("stsc", [2, NSMP * 3, CONVD])
    stss = din("stss", [2, NSMP, NH * 64, 128])
    pp = din("pp", [DEPTH, T, PLE])
    psm = din("psm", [DEPTH, NSMP, PLE])
    W = {}
    for nm, shp in [("w_ffn1_gate", [DEPTH, D, DFF]), ("w_ffn1_up", [DEPTH, D, DFF]), ("w_ffn1_down", [DEPTH, DFF, D]),
                    ("w_ffn2_gate", [DEPTH, D, DFF]), ("w_ffn2_up", [DEPTH, D, DFF]), ("w_ffn2_down", [DEPTH, DFF, D]),
                    ("w_ple_gate", [DEPTH, D, D]), ("w_ple_proj", [DEPTH, PLE, D]),
                    ("cm_w_in", [2, D, 2 * D]), ("cm_w_out", [2, D, D]),
                    ("ssd_w_in", [2, D, 6176]), ("ssd_w_out", [2, DIN, D])]:
        W[nm] = din(nm, shp)
    vecs_d = din("vecs", [128, NV])
    cst_d = din("cst", [128, 512])
    sel_d = din("sel", [32, 2048])
    yp = dout("yp", [T, D])
    ysd = dout("ys", [NSMP, D])
    ocp = dout("ocp", [2, 30, D])
    oxp = dout("oxp", [2, 3, CONVD])
    osp = dout("osp", [2, NH * 64, 128])
    ocs = dout("ocs", [2, NSMP, 30, D])
    oxs = dout("oxs", [2, NSMP, 3, CONVD])
    oss = dout("oss", [2, NSMP, NH * 64, 128])

    with ExitStack() as es:
        k = K(nc, es)
        X = k.buf("X", [128, 8, NT], F32)
        VEC = k.buf("VEC", [128, NV], F32)
        CST = k.buf("CST", [128, 512], F32)
        CB = k.buf("CB", [128, 256], BF16)
        k.wslots = [k.buf(f"wslot{i}", [128, SLOTE], BF16) for i in range(NSLOT)]
        PS = [Buf(es.enter_context(nc.psum_tensor(f"psum{i}", [128, 512], F32)), f"psum{i}") for i in range(8)]
        dsem_dd = es.enter_context(nc.semaphore("dsem_dd"))
        k.init_engines()
        k.psi = 0

        identf = CST.t[:, 0:128]
        trif = CST.t[:, 128:256]
        ustrf = CST.t[:, 256:384]
        onesf = CST.t[:, 384:512]
        identb = CB.t[:, 0:128]
        onesb = CB.t[:, 128:256]

        Xb = [k.sub(X, 5, f"c{m}t") for m in range(8)]

        def tix_of(col):
            return min(col // 512, 4)

        def XT(col):
            return [Xb[m][tix_of(col)] for m in range(8)]

        def X1(m, col):
            return Xb[m][tix_of(col)]

        def V(name, c=0, n=1):
            o = voff[name] + c
            return VEC.t[:, o:o + n]

        def ps():
            if k.dry:
                return PS[0]
            free = [b for b in PS if getattr(b, "last_read", 0) >= getattr(b, "last_write", 0)]
            if free:
                b = min(free, key=lambda b: getattr(b, "last_read", 0))
            else:
                b = min(PS, key=lambda b: getattr(b, "last_write", 0))
            k.opidx = getattr(k, "opidx", 0) + 1
            b.last_write = k.opidx
            return b

        def ACT(out, in_, func, R, Wr, **kw):
            k.op(k.act, lambda: nc.scalar.activation(out=out, in_=in_, func=func, **kw), R, Wr)

        def TT(out, in0, in1, op, R, Wr):
            k.op(k.dve, lambda: nc.vector.tensor_tensor(out=out, in0=in0, in1=in1, op=op), R, Wr)

        def STT(out, in0, scalar, in1, op0, op1, R, Wr):
            k.op(k.dve, lambda: nc.vector.scalar_tensor_tensor(out=out, in0=in0, scalar=scalar, in1=in1, op0=op0, op1=op1), R, Wr)

        def TS(out, in0, s1, s2, op0, op1, R, Wr):
            if op1 is None:
                k.op(k.dve, lambda: nc.vector.tensor_scalar(out=out, in0=in0, scalar1=s1, scalar2=None, op0=op0), R, Wr)
            else:
                k.op(k.dve, lambda: nc.vector.tensor_scalar(out=out, in0=in0, scalar1=s1, scalar2=s2, op0=op0, op1=op1), R, Wr)

        def CP(out, in_, R, Wr):
            k.op(k.dve, lambda: nc.vector.tensor_copy(out=out, in_=in_), R, Wr)

        def RSTD(rt, n, src_ap, srcB, scale):
            ACT(rt.t[:, :n], src_ap, AF.Ln, [srcB], [rt], scale=scale, bias=EPS)
            ACT(rt.t[:, :n], rt.t[:, :n], AF.Exp, [rt], [rt], scale=-0.5)

        def RCP(out, in_, R, Wr):
            k.op(k.dve, lambda: nc.vector.reciprocal(out=out, in_=in_), R, Wr)

        def MEMSET(out, val, Wr):
            k.op(k.dve, lambda: nc.vector.memset(out, val), (), Wr)

        def MM(ob, out, pairs, R, inc=True, first=True, final=True):
            n = len(pairs)
            for i, (l, r) in enumerate(pairs):
                last = i == n - 1
                k.op(k.pe, lambda l=l, r=r, i=i, last=last: nc.tensor.matmul(out, lhsT=l, rhs=r, start=(first and i == 0), stop=(final and last)),
                     R if i == 0 else (), [ob] if i == 0 else (), inc=(last and inc))

        def PTT(out, in0, in1, op, R, Wr):
            k.op(k.pool, lambda: nc.gpsimd.tensor_tensor(out=out, in0=in0, in1=in1, op=op), R, Wr)

        def TR(ob, out, ib, in_, ident, inc=True, extra=()):
            k.op(k.pe, lambda: nc.tensor.transpose(out, in_, ident), [ib] + list(extra), [ob], inc=inc)

        def wload(specs):
            j = k.wj
            k.wj += 1
            if k.dry:
                k.wplan.append(specs)
                return k.wslots[j % NSLOT]
            base = k.whold if k.whold is not None else j
            while k.wissued < min(len(k.wplan), base + NSLOT):
                jj = k.wissued
                slot = k.wslots[jj % NSLOT]
                for (src, kc, ncols, off) in k.wplan[jj]:
                    dst = slot.t[:, off:off + kc * ncols].rearrange("p (k c) -> p k c", k=kc)
                    k.dma(k.pool, dst, src.rearrange("(k p) c -> p k c", p=128), writes=[slot])
                k.wissued += 1
            return k.wslots[j % NSLOT]

        def wv(slot, off, kk, ncols):
            return slot.t[:, off + kk * ncols: off + (kk + 1) * ncols]

        def body():
            k.wj = 0
            k.psi = 0
            k.whold = None
            k.dma(k.sp, VEC.t[:, :], vecs_d[:, :], writes=[VEC])
            k.dma(k.sp, CST.t[:, :], cst_d[:, :], writes=[CST])
            CP(CB.t[:, 0:128], identf, [CST], [CB])
            CP(CB.t[:, 128:256], onesf, [CST], [CB])

            with k.phase():
                stg = [k.buf(f"stg{i}", [128, D], F32) for i in range(4)]
                for r in range(17):
                    st = stg[r % 4]
                    R = 128 if r < 16 else NSMP
                    src = xp[r * 128:(r + 1) * 128, :] if r < 16 else xs[:, :]
                    k.dma(k.sp, st.t[:R, :], src, writes=[st])
                    for half in range(2):
                        pb = ps()
                        for q in range(4):
                            c = half * 4 + q
                            TR(pb, pb.t[:, q * 128:q * 128 + R], st, st.t[:R, c * 128:(c + 1) * 128], identf[:R, :R], inc=(q == 3), extra=[CST])
                        src_v = pb.t[:, :].rearrange("p (a b) -> p a b", a=4)[:, :, :R]
                        dst_v = X.t[:, half * 4:half * 4 + 4, r * 128:r * 128 + R]
                        xw = [X1(m, r * 128) for m in range(half * 4, half * 4 + 4)]
                        if half == 0:
                            ACT(dst_v, src_v, AF.Copy, [pb], xw)
                        else:
                            CP(dst_v, src_v, [pb], xw)

            def rmsnorm(gname, dstB_fn, dst_fn, subs, sqs, rts):
                for ti, (so, do, n) in enumerate(subs):
                    sq = sqs[ti % len(sqs)]
                    rt = rts[ti % len(rts)]
                    ACT(sq.t[:, 0:8, :n], X.t[:, :, so:so + n], AF.Square, XT(so), [sq])
                    pb = ps()
                    MM(pb, pb.t[:, :n], [(onesb, sq.t[:, kk, :n]) for kk in range(8)], [sq, CB])
                    RSTD(rt, n, pb.t[:, :n], pb, 1.0 / D)
                    for kk in range(8):
                        STT(dst_fn(kk, do, n), X.t[:, kk, so:so + n], V(gname, kk), rt.t[:, :n], ALU.mult, ALU.mult, [X1(kk, so), rt, VEC], dstB_fn(do))

            def ffn(i, which):
                wg, wu, wd = W[f"w_ffn{which}_gate"], W[f"w_ffn{which}_up"], W[f"w_ffn{which}_down"]
                with k.phase():
                    XN = k.buf("XN", [128, 8, NT], BF16)
                    H = k.buf("H", [128, 11, NT], BF16)
                    sqs = [k.buf(f"sq{a}", [128, 8, 512], BF16) for a in range(2)]
                    rts = [k.buf(f"rt{a}", [128, 512], F32) for a in range(2)]
                    sgs = [k.buf(f"sg{a}", [128, 512], F32) for a in range(3)]
                    XNb = k.sub(XN, 5, "t")
                    Hb = k.sub(H, 5, "t")
                    rmsnorm(f"norm_ffn{which}{i}", lambda do: [XNb[tix_of(do)]], lambda kk, do, n: XN.t[:, kk, do:do + n], [(s, s, n) for s, n in TILES], sqs, rts)
                    sgi = [0]

                    def ffn_a(sl, fi, s, n):
                        pa = ps()
                        pbb = ps()
                        MM(pa, pa.t[:, :n], [(wv(sl, 0, kk, 128), XN.t[:, kk, s:s + n]) for kk in range(8)], [sl, XNb[tix_of(s)]])
                        MM(pbb, pbb.t[:, :n], [(wv(sl, 1024, kk, 128), XN.t[:, kk, s:s + n]) for kk in range(8)], [sl, XNb[tix_of(s)]])
                        sg = sgs[sgi[0] % 3]
                        sgi[0] += 1
                        ACT(sg.t[:, :n], pa.t[:, :n], AF.Silu, [pa], [sg])
                        TT(H.t[:, fi, s:s + n], pbb.t[:, :n], sg.t[:, :n], ALU.mult, [pbb, sg], [Hb[tix_of(s)]])

                    def ffn_w(f):
                        return wload([(wg[i, :, f * 128:(f + 1) * 128], 8, 128, 0), (wu[i, :, f * 128:(f + 1) * 128], 8, 128, 1024)])

                    for half in range(2):
                        fis = list(range(11))
                        if half == 0:
                            k.whold = k.wj
                            sls = [ffn_w(fi) for fi in range(4)]
                            for (s, n) in TILES:
                                for fi in range(4):
                                    ffn_a(sls[fi], fi, s, n)
                            k.whold = None
                            fis = list(range(4, 11))
                        for fi in fis:
                            sl = ffn_w(half * 11 + fi)
                            for (s, n) in TILES:
                                ffn_a(sl, fi, s, n)
                        for m in range(8):
                            r0 = half * 1408
                            sl = wload([(wd[i, r0:r0 + 1408, m * 128:(m + 1) * 128], 11, 128, 0)])
                            for (s, n) in TILES:
                                pc = ps()
                                MM(pc, pc.t[:, :n], [(wv(sl, 0, fi, 128), H.t[:, fi, s:s + n]) for fi in range(11)], [sl, Hb[tix_of(s)]])
                                STT(X.t[:, m, s:s + n], pc.t[:, :n], 0.5, X.t[:, m, s:s + n], ALU.mult, ALU.add, [pc, X1(m, s)], [X1(m, s)])

            def ple(i):
                with k.phase():
                    XN = k.buf("XN", [128, 8, NT], BF16)
                    PT = k.buf("PT", [128, 2, NT], BF16)
                    sqs = [k.buf(f"sq{a}", [128, 8, 512], BF16) for a in range(2)]
                    rts = [k.buf(f"rt{a}", [128, 512], F32) for a in range(2)]
                    sgs = [k.buf(f"sg{a}", [128, 512], F32) for a in range(3)]
                    t2s = [k.buf(f"t2{a}", [128, 512], F32) for a in range(2)]
                    stg = [k.buf(f"pstg{a}", [128, PLE], F32) for a in range(6)]
                    PTb = k.sub(PT, 5, "t")
                    XNb = k.sub(XN, 5, "t")
                    rmsnorm(f"norm_ple{i}", lambda do: [XNb[tix_of(do)]], lambda kk, do, n: XN.t[:, kk, do:do + n], [(s, s, n) for s, n in TILES], sqs, rts)
                    for r in range(17):
                        st = stg[r % 6]
                        R = 128 if r < 16 else NSMP
                        src = pp[i, r * 128:(r + 1) * 128, :] if r < 16 else psm[i, :, :]
                        k.dma(k.sp, st.t[:R, :], src, writes=[st])
                        pb = ps()
                        for c in range(2):
                            TR(pb, pb.t[:, c * 128:c * 128 + R], st, st.t[:R, c * 128:(c + 1) * 128], identf[:R, :R], inc=(c == 1), extra=[CST])
                        ACT(PT.t[:, :, r * 128:r * 128 + R], pb.t[:, 0:256].rearrange("p (a b) -> p a b", a=2)[:, :, :R], AF.Copy, [pb], [PTb[tix_of(r * 128)]])
                    ci = [0]

                    def ple_w(m):
                        return wload([(W["w_ple_gate"][i, :, m * 128:(m + 1) * 128], 8, 128, 0), (W["w_ple_proj"][i, :, m * 128:(m + 1) * 128], 2, 128, 1024)])

                    def ple_c(sl, m, s, n):
                        pg = ps()
                        pq = ps()
                        MM(pg, pg.t[:, :n], [(wv(sl, 0, kk, 128), XN.t[:, kk, s:s + n]) for kk in range(8)], [sl, XNb[tix_of(s)]])
                        MM(pq, pq.t[:, :n], [(wv(sl, 1024, kk, 128), PT.t[:, kk, s:s + n]) for kk in range(2)], [sl, PTb[tix_of(s)]])
                        sg = sgs[ci[0] % 3]
                        t2 = t2s[ci[0] % 2]
                        ci[0] += 1
                        ACT(sg.t[:, :n], pg.t[:, :n], AF.Sigmoid, [pg], [sg])
                        TT(t2.t[:, :n], pq.t[:, :n], sg.t[:, :n], ALU.mult, [pq, sg], [t2])
                        TT(X.t[:, m, s:s + n], X.t[:, m, s:s + n], t2.t[:, :n], ALU.add, [X1(m, s), t2], [X1(m, s)])

                    k.whold = k.wj
                    sls = [ple_w(m) for m in range(4)]
                    for (s, n) in TILES:
                        for m in range(4):
                            ple_c(sls[m], m, s, n)
                    k.whold = None
                    for m in range(4, 8):
                        sl = ple_w(m)
                        for (s, n) in TILES:
                            ple_c(sl, m, s, n)

            def conv_mixer(i, j):
                with k.phase():
                    XN = k.buf("XN", [128, 8, NT], BF16)
                    Vb = k.buf("Vb", [128, 8, NT], BF16)
                    rts = [k.buf(f"rt{a}", [128, 512], F32) for a in range(2)]
                    sgs = [k.buf(f"sg{a}", [128, 512], F32) for a in range(3)]
                    UL = k.buf("ul", [128, 8, 32], F32)
                    USN = k.buf("usn", [128, 8, NSMP], F32)
                    XNb = k.sub(XN, 5, "t")
                    Vbb = k.sub(Vb, 8, "c")
                    with k.phase():
                        sqs = [k.buf(f"sq{a}", [128, 8, 512], BF16) for a in range(2)]
                        rmsnorm(f"norm_mix{i}", lambda do: [XNb[tix_of(do)]], lambda kk, do, n: XN.t[:, kk, do:do + n], [(s, s, n) for s, n in TILES], sqs, rts)
                    c1 = k.phase()
                    c1.__enter__()
                    UE = [k.buf(f"ue{a}", [128, 30 + T], BF16) for a in range(2)]
                    US = [k.buf(f"us{a}", [128, NSMP, CK], F32) for a in range(2)]
                    DG = [k.buf(f"dg{a}", [128, CK, 128], BF16) for a in range(2)]
                    STSS = [k.buf(f"sts{a}", [128, 4, 128], F32) for a in range(2)]
                    tmp = k.buf("ctmp", [128, NSMP, CK], F32)
                    red = k.buf("cred", [128, NSMP], F32)
                    if not k.dry:
                        nc.sync.dma_start(out=ocs[j, :, 0:29, :], in_=stc[j, :, :].rearrange("(b r) c -> b r c", r=30)[:, 1:30, :]).then_inc(dsem_dd, 16)
                        k.ddcnt += 16
                    sgi = 0
                    for c in range(8):
                        sl = wload([(W["cm_w_in"][j, :, c * 128:(c + 1) * 128], 8, 128, 0), (W["cm_w_in"][j, :, D + c * 128:D + (c + 1) * 128], 8, 128, 1024)])
                        ue = UE[c % 2]
                        us = US[c % 2]
                        dg = DG[c % 2]
                        MEMSET(ue.t[:, 0:30], 0.0, [ue])
                        TT(dg.t[:, :, :], identb.unsqueeze(1).to_broadcast([128, CK, 128]), V(f"cdw{j}", c * CK, CK).unsqueeze(2).to_broadcast([128, CK, 128]), ALU.mult, [CB, VEC], [dg])
                        STS = STSS[c % 2]
                        k.dma(k.sp, STS.t[:, 0:3, :], stc[j, 0:384, c * 128:(c + 1) * 128].rearrange("(a p) c -> p a c", p=128), writes=[STS])
                        k.dma(k.sp, STS.t[:96, 3, :], stc[j, 384:480, c * 128:(c + 1) * 128], writes=[STS])
                        pb = ps()
                        for a in range(4):
                            R = 128 if a < 3 else 96
                            TR(pb, pb.t[:, a * 128:a * 128 + R], STS, STS.t[:R, a, :], identf[:R, :R], inc=(a == 3), extra=[CST])
                        CP(us.t[:, :, 0:30], pb.t[:, 0:480].rearrange("p (b r) -> p b r", r=30), [pb], [us])
                        for (s, n) in TILES:
                            pa = ps()
                            pg = ps()
                            MM(pa, pa.t[:, :n], [(wv(sl, 0, kk, 128), XN.t[:, kk, s:s + n]) for kk in range(8)], [sl, XNb[tix_of(s)]])
                            MM(pg, pg.t[:, :n], [(wv(sl, 1024, kk, 128), XN.t[:, kk, s:s + n]) for kk in range(8)], [sl, XNb[tix_of(s)]])
                            sg = sgs[sgi % 3]
                            sgi += 1
                            ACT(sg.t[:, :n], pg.t[:, :n], AF.Sigmoid, [pg, VEC], [sg], bias=V(f"cbin{j}", 8 + c))
                            if s < T:
                                STT(ue.t[:, 30 + s:30 + s + n], pa.t[:, :n], V(f"cbin{j}", c), sg.t[:, :n], ALU.add, ALU.mult, [pa, sg, VEC], [ue])
                                if s + n == T:
                                    STT(UL.t[:, c, :], pa.t[:, n - 32:n], V(f"cbin{j}", c), sg.t[:, n - 32:n], ALU.add, ALU.mult, [pa, sg, VEC], [UL])
                            else:
                                STT(us.t[:, :, 30], pa.t[:, :n], V(f"cbin{j}", c), sg.t[:, :n], ALU.add, ALU.mult, [pa, sg, VEC], [us])
                        for (s, n) in TILES[:4]:
                            pv = ps()
                            MM(pv, pv.t[:, :n], [(dg.t[:, kk, :], ue.t[:, s + kk:s + kk + n]) for kk in range(CK)], [dg, ue])
                            ACT(Vb.t[:, c, s:s + n], pv.t[:, :n], AF.Identity, [pv, VEC], [Vbb[c]], bias=V(f"cdwb{j}", c))
                        TT(tmp.t[:, :, :], us.t[:, :, :], V(f"cdw{j}", c * CK, CK).unsqueeze(1).to_broadcast([128, NSMP, CK]), ALU.mult, [us, VEC], [tmp])
                        k.op(k.dve, lambda: nc.vector.tensor_reduce(out=red.t[:, :], in_=tmp.t[:, :, :], axis=AX.X, op=ALU.add), [tmp], [red])
                        TS(Vb.t[:, c, T:NT], red.t[:, :], V(f"cdwb{j}", c), None, ALU.add, None, [red, VEC], [Vbb[c]])
                        CP(USN.t[:, c, :], us.t[:, :, 30], [us], [USN])
                    c1.__exit__(None, None, None)
                    c2 = k.phase()
                    c2.__enter__()
                    sqs = [k.buf(f"sq{a}", [128, 8, 512], BF16) for a in range(2)]
                    m1s = [k.buf(f"m1{a}", [128, 512], F32) for a in range(2)]
                    m2s = [k.buf(f"m2{a}", [128, 512], F32) for a in range(2)]
                    dts_ = [k.buf(f"dt{a}", [128, 512], F32) for a in range(2)]
                    for ti, (s, n) in enumerate(TILES):
                        sq = sqs[ti % 2]
                        rt = rts[ti % 2]
                        m1 = m1s[ti % 2]
                        m2 = m2s[ti % 2]
                        ACT(sq.t[:, :, :n], Vb.t[:, :, s:s + n], AF.Square, Vbb, [sq])
                        p1 = ps()
                        p2 = ps()
                        MM(p1, p1.t[:, :n], [(onesb, Vb.t[:, kk, s:s + n]) for kk in range(8)], Vbb + [CB])
                        MM(p2, p2.t[:, :n], [(onesb, sq.t[:, kk, :n]) for kk in range(8)], [sq, CB])
                        ACT(m1.t[:, :n], p1.t[:, :n], AF.Copy, [p1], [m1], scale=1.0 / D)
                        TT(m2.t[:, :n], m1.t[:, :n], m1.t[:, :n], ALU.mult, [m1], [m2])
                        STT(m2.t[:, :n], p2.t[:, :n], 1.0 / D, m2.t[:, :n], ALU.mult, ALU.subtract, [p2, m2], [m2])
                        RSTD(rt, n, m2.t[:, :n], m2, 1.0)
                        for kk in range(8):
                            dtb = dts_[kk % 2]
                            TT(dtb.t[:, :n], Vb.t[:, kk, s:s + n], m1.t[:, :n], ALU.subtract, [Vbb[kk], m1], [dtb])
                            TT(dtb.t[:, :n], dtb.t[:, :n], rt.t[:, :n], ALU.mult, [dtb, rt], [dtb])
                            ACT(XN.t[:, kk, s:s + n], dtb.t[:, :n], AF.Silu, [dtb, VEC], [XNb[tix_of(s)]], scale=V(f"clng{j}", kk), bias=V(f"clnb{j}", kk))
                    def co_w(m):
                        return wload([(W["cm_w_out"][j, :, m * 128:(m + 1) * 128], 8, 128, 0)])

                    def co_c(sl, m, s, n):
                        pc = ps()
                        MM(pc, pc.t[:, :n], [(wv(sl, 0, kk, 128), XN.t[:, kk, s:s + n]) for kk in range(8)], [sl, XNb[tix_of(s)]])
                        STT(X.t[:, m, s:s + n], pc.t[:, :n], V(f"cbout{j}", m), X.t[:, m, s:s + n], ALU.add, ALU.add, [pc, X1(m, s), VEC], [X1(m, s)])

                    k.whold = k.wj
                    sls = [co_w(m) for m in range(4)]
                    for (s, n) in TILES:
                        for m in range(4):
                            co_c(sls[m], m, s, n)
                    k.whold = None
                    for m in range(4, 8):
                        sl = co_w(m)
                        for (s, n) in TILES:
                            co_c(sl, m, s, n)
                    c2.__exit__(None, None, None)
                    OST = k.buf("ost", [32, D], F32)
                    for half in range(2):
                        pb = ps()
                        for q in range(4):
                            c = half * 4 + q
                            TR(pb, pb.t[:32, q * 128:(q + 1) * 128], UL, UL.t[:, c, :], identf, inc=(q == 3), extra=[CST])
                        CP(OST.t[:32, half * 512:(half + 1) * 512], pb.t[:32, :], [pb], [OST])
                    k.dma(k.sp, ocp[j, :, :], OST.t[2:32, :], reads=[OST], final=True)
                    OS2 = k.buf("os2", [NSMP, D], F32)
                    for half in range(2):
                        pb = ps()
                        for q in range(4):
                            c = half * 4 + q
                            TR(pb, pb.t[:NSMP, q * 128:(q + 1) * 128], USN, USN.t[:, c, :], identf, inc=(q == 3), extra=[CST])
                        CP(OS2.t[:, half * 512:(half + 1) * 512], pb.t[:NSMP, :], [pb], [OS2])
                    k.dma(k.sp, ocs[j, :, 29, :], OS2.t[:, :], reads=[OS2], final=True)

            def ssd_mixer(i, j):
                win, wout = W["ssd_w_in"], W["ssd_w_out"]
                with k.phase():
                    NL = 512 + NSMP
                    HN = k.buf("HN", [128, 8, NL], BF16)
                    XB = k.buf("XB", [128, 32, NL], BF16)
                    YT = k.buf("YT", [128, 16, NL], BF16)
                    XBb = k.sub(XB, 32, "c")
                    YTb = k.sub(YT, 16, "c")
                    PRE = [k.buf(f"pre{a}", [128, 515], BF16) for a in range(2)]
                    DGS = [k.buf(f"dgs{a}", [128, 4, 128], BF16) for a in range(2)]
                    HIST = k.buf("hist", [128, 32, 3], F32)
                    HISTb = k.sub(HIST, 32, "c")
                    GS = [k.buf(f"gsq{a}", [128, 2, 512], BF16) for a in range(2)]
                    rts = [k.buf(f"rt{a}", [128, 512], F32) for a in range(2)]
                    sgs = rts
                    ABC = k.buf("abc", [128, 32], F32)
                    S = k.buf("S", [128, DIN], F32)
                    SB = k.buf("SB", [128, DIN], BF16)
                    Sb = k.sub(S, 4, "q")
                    SBb = k.sub(SB, 4, "q")
                    MEMSET(HIST.t[:, :, :], 0.0, HISTb)
                    MEMSET(S.t[:, :], 0.0, Sb)
                    MEMSET(SB.t[:, :], 0.0, SBb)
                    ACT(ABC.t[:, :], V(f"alogbc{j}", 0, 32), AF.Exp, [VEC], [ABC])
                    TS(ABC.t[:, :], ABC.t[:, :], -1.0, None, ALU.mult, None, [ABC], [ABC])
                    ci = 0
                    gi = 0
                    for tix in range(4):
                        s0 = tix * 512
                        smp = tix == 3
                        lsubs = [(0, s0, 512)] + ([(512, T, NSMP)] if smp else [])
                        for (l, g, n) in lsubs:
                            pb = ps()
                            for rnd in range(4):
                                sq = GS[gi % 2]
                                gi += 1
                                ACT(sq.t[:, 0:2, :n], X.t[:, 2 * rnd:2 * rnd + 2, g:g + n], AF.Square, [X1(2 * rnd, g), X1(2 * rnd + 1, g)], [sq])
                                MM(pb, pb.t[:, :n], [(onesb, sq.t[:, e, :n]) for e in range(2)], [sq, CB], first=(rnd == 0), final=(rnd == 3))
                            rt = rts[0]
                            RSTD(rt, n, pb.t[:, :n], pb, 1.0 / D)
                            for kk in range(8):
                                STT(HN.t[:, kk, l:l + n], X.t[:, kk, g:g + n], V(f"norm_mix{i}", kk), rt.t[:, :n], ALU.mult, ALU.mult, [X1(kk, g), rt, VEC], [HN])
                        if smp:
                            xph = k.phase()
                            xph.__enter__()
                            STXS = [k.buf(f"stx{a}", [48, 1024], F32) for a in range(2)]
                            STT_ = k.buf("stT", [128, 32, NSMP, 3], F32)
                            NEWP = k.buf("newp", [128, 32, NSMP], F32)
                            if not k.dry:
                                nc.sync.dma_start(out=oxs[j, :, 0:2, :], in_=stsc[j, :, :].rearrange("(b r) c -> b r c", r=3)[:, 1:3, :]).then_inc(dsem_dd, 16)
                                k.ddcnt += 16
                            for g4 in range(4):
                                STX = STXS[g4 % 2]
                                k.dma(k.sp, STX.t[:, :], stsc[j, :, g4 * 1024:(g4 + 1) * 1024], writes=[STX])
                                pb = ps()
                                for q in range(8):
                                    TR(pb, pb.t[:, q * 48:(q + 1) * 48], STX, STX.t[:, q * 128:(q + 1) * 128], identf[:48, :48], inc=(q == 7), extra=[CST])
                                CP(STT_.t[:, g4 * 8:(g4 + 1) * 8, :, :], pb.t[:, 0:384].rearrange("p (c b r) -> p c b r", c=8, b=NSMP), [pb], [STT_])
                            ctm = k.buf("ctm", [128, NSMP, 3], F32)
                            cr = k.buf("cr", [128, NSMP], F32)
                        sls = {}

                        def emit_proj(cc):
                            it, e = divmod(cc, 2)
                            if e == 0:
                                sls[it] = wload([(win[j, :, DIN + it * 256: DIN + (it + 1) * 256], 8, 256, 0)])
                            sl = sls[it]
                            lw = [sl.t[:, kk * 256 + e * 128: kk * 256 + (e + 1) * 128] for kk in range(8)]
                            pa = ps()
                            MM(pa, pa.t[:, :512], [(lw[kk], HN.t[:, kk, 0:512]) for kk in range(8)], [sl, HN])
                            pq = None
                            if smp:
                                pq = ps()
                                MM(pq, pq.t[:, :NSMP], [(lw[kk], HN.t[:, kk, 512:NL]) for kk in range(8)], [sl, HN])
                            return pa, pq

                        stA = {}
                        stB = {}
                        stC = {}

                        def stage_B(cc):
                            nonlocal ci
                            pa, pq = stA.pop(cc)
                            pre = PRE[ci % 2]
                            dgs = DGS[ci % 2]
                            ci += 1
                            TT(dgs.t[:, :, :], identb.unsqueeze(1).to_broadcast([128, 4, 128]), V(f"scw{j}", cc * 4, 4).unsqueeze(2).to_broadcast([128, 4, 128]), ALU.mult, [CB, VEC], [dgs])
                            CP(pre.t[:, 0:3], HIST.t[:, cc, :], [HISTb[cc]], [pre])
                            ACT(pre.t[:, 3:515], pa.t[:, :512], AF.Copy, [pa], [pre])
                            CP(HIST.t[:, cc, :], pa.t[:, 509:512], [pa], [HISTb[cc]])
                            if smp:
                                CP(NEWP.t[:, cc, :], pq.t[:, :NSMP], [pq], [NEWP])
                                TT(ctm.t[:, :, :], STT_.t[:, cc, :, :], V(f"scw{j}", cc * 4, 3).unsqueeze(1).to_broadcast([128, NSMP, 3]), ALU.mult, [STT_, VEC], [ctm])
                                k.op(k.dve, lambda: nc.vector.tensor_reduce(out=cr.t[:, :], in_=ctm.t[:, :, :], axis=AX.X, op=ALU.add), [ctm], [cr])
                                STT(cr.t[:, :], pq.t[:, :NSMP], V(f"scw{j}", cc * 4 + 3), cr.t[:, :], ALU.mult, ALU.add, [pq, cr, VEC], [cr])
                                ACT(XB.t[:, cc, 512:NL], cr.t[:, :], AF.Silu, [cr, VEC], [XBb[cc]], bias=V(f"scb{j}", cc))
                            stB[cc] = (pre, dgs)

                        def stage_C(cc):
                            pre, dgs = stB.pop(cc)
                            pc = ps()
                            MM(pc, pc.t[:, :512], [(dgs.t[:, kk, :], pre.t[:, kk:kk + 512]) for kk in range(4)], [dgs, pre])
                            stC[cc] = pc

                        def stage_D(cc):
                            pc = stC.pop(cc)
                            ACT(XB.t[:, cc, 0:512], pc.t[:, :512], AF.Silu, [pc, VEC], [XBb[cc]], bias=V(f"scb{j}", cc))

                        for it_ in range(-2, 33):
                            if 0 <= it_ + 2 < 32:
                                stA[it_ + 2] = emit_proj(it_ + 2)
                            if 0 <= it_ + 1 < 32:
                                stage_B(it_ + 1)
                            if 0 <= it_ < 32:
                                stage_C(it_)
                            if 0 <= it_ - 1 < 32:
                                stage_D(it_ - 1)
                        if smp:
                            OX2 = k.buf("ox2", [NSMP, 1024], F32)
                            for g4 in range(4):
                                for hf in range(2):
                                    pb = ps()
                                    for q in range(4):
                                        cc = g4 * 8 + hf * 4 + q
                                        TR(pb, pb.t[:NSMP, q * 128:(q + 1) * 128], NEWP, NEWP.t[:, cc, :], identf, inc=(q == 3), extra=[CST])
                                    CP(OX2.t[:, hf * 512:(hf + 1) * 512], pb.t[:NSMP, :], [pb], [OX2])
                                k.dma(k.sp, oxs[j, :, 2, g4 * 1024:(g4 + 1) * 1024], OX2.t[:, :], reads=[OX2], final=True)
                            xph.__exit__(None, None, None)
                        k.mark(f"  L{i} t{tix} xBC+conv end")
                        sld = wload([(win[j, :, DIN + CONVD: DIN + CONVD + 32], 8, 32, 0)])
                        with k.phase():
                            Rg = [k.buf(f"Rg{a}", [128, 4, 128], F32) for a in range(3)]
                            Lg = [k.buf(f"Lg{a}", [128, 4, 128], BF16) for a in range(3)]
                            MTg = [k.buf(f"MT{a}", [128, 4, 128], BF16) for a in range(3)]
                            CBM = k.buf("cbm", [128, 8, 128], BF16)
                            CBMb = k.sub(CBM, 8, "g")
                            XDT = k.buf("xdt", [128, DIN], BF16)
                            XDD = k.buf("xdd", [128, DIN], BF16)
                            XDTb = k.sub(XDT, 2, "h")
                            XDDb = k.sub(XDD, 2, "h")
                            BTM = k.buf("btm", [128, 1024], BF16)
                            YTM = k.buf("ytm", [128, DIN], BF16)
                            YTMb = k.sub(YTM, 4, "q")
                            T1s = [k.buf(f"t1{a}", [128, 512], BF16) for a in range(1)]
                            sms = [{nm: k.buf(nm + str(a), [128, 32], F32) for nm in ("dt", "a", "acs", "ea", "cd", "dd", "dec", "dtd", "e1")} for a in range(2)]

                            def prologue_stage(q, st):
                                sm = sms[q % 2]
                                lo = q * 128
                                if st == 0:
                                    pd = ps()
                                    MM(pd, pd.t[:, 0:32], [(HN.t[:, kk, lo:lo + 128], sld.t[:, kk * 32:(kk + 1) * 32]) for kk in range(8)], [sld, HN])
                                    TT(sm["e1"].t[:, :], pd.t[:, 0:32], V(f"dtbbc{j}", 0, 32), ALU.add, [pd, VEC], [sm["e1"]])
                                elif st == 1:
                                    ACT(sm["e1"].t[:, :], sm["e1"].t[:, :], AF.Exp, [sm["e1"]], [sm["e1"]])
                                    ACT(sm["dt"].t[:, :], sm["e1"].t[:, :], AF.Ln, [sm["e1"]], [sm["dt"]], bias=1.0)
                                elif st == 2:
                                    TT(sm["a"].t[:, :], sm["dt"].t[:, :], ABC.t[:, :], ALU.mult, [sm["dt"], ABC], [sm["a"]])
                                elif st == 3:
                                    pcs = ps()
                                    sm["pcs"] = pcs
                                    MM(pcs, pcs.t[:, 0:32], [(trif, sm["a"].t[:, :])], [CST, sm["a"]], inc=False)
                                    MM(pcs, pcs.t[:, 32:64], [(onesf, sm["a"].t[:, :])], [CST, sm["a"]])
                                elif st == 4:
                                    pcs = sm["pcs"]
                                    ACT(sm["acs"].t[:, :], pcs.t[:, 0:32], AF.Copy, [pcs], [sm["acs"]])
                                    ACT(sm["ea"].t[:, :], pcs.t[:, 0:32], AF.Exp, [pcs], [sm["ea"]])
                                    ACT(sm["cd"].t[:, :], pcs.t[:, 32:64], AF.Exp, [pcs], [sm["cd"]])
                                elif st == 5:
                                    pcs = sm["pcs"]
                                    TT(sm["dd"].t[:, :], pcs.t[:, 32:64], sm["acs"].t[:, :], ALU.subtract, [pcs, sm["acs"]], [sm["dd"]])
                                elif st == 6:
                                    ACT(sm["dec"].t[:, :], sm["dd"].t[:, :], AF.Exp, [sm["dd"]], [sm["dec"]])
                                elif st == 7:
                                    TT(sm["dtd"].t[:, :], sm["dt"].t[:, :], sm["dec"].t[:, :], ALU.mult, [sm["dt"], sm["dec"]], [sm["dtd"]])

                            def prologue(q):
                                for st in range(8):
                                    prologue_stage(q, st)

                            prologue(0)
                            for q in range(4):
                                sm = sms[q % 2]
                                lo = q * 128
                                pts = []
                                for hb in range(2):
                                    pt = ps()
                                    ptb = pt.t[:, :].bitcast(BF16)
                                    for e in range(8):
                                        hh = hb * 8 + e
                                        TR(pt, ptb[:, e * 128:(e + 1) * 128], XBb[hh], XB.t[:, hh, lo:lo + 128], identb, inc=(e == 7), extra=[CB])
                                    pts.append((pt, ptb))
                                ptB = ps()
                                ptBb = ptB.t[:, :].bitcast(BF16)
                                for g in range(8):
                                    TR(ptB, ptBb[:, g * 128:(g + 1) * 128], XBb[16 + g], XB.t[:, 16 + g, lo:lo + 128], identb, inc=(g == 7), extra=[CB])
                                pcbs = []
                                for hf in range(2):
                                    pcb = ps()
                                    for e in range(4):
                                        g = hf * 4 + e
                                        MM(pcb, pcb.t[:, e * 128:(e + 1) * 128], [(XB.t[:, 16 + g, lo:lo + 128], XB.t[:, 24 + g, lo:lo + 128])], [XBb[16 + g], XBb[24 + g]], inc=(e == 3))
                                    pcbs.append(pcb)
                                rg_of = {}

                                def emit_R(g):
                                    rg = Rg[g % 3]
                                    rg_of[g] = rg
                                    PTT(rg.t[:, :, :], sm["a"].t[:, g * 4:(g + 1) * 4].unsqueeze(2).to_broadcast([128, 4, 128]), trif.unsqueeze(1).to_broadcast([128, 4, 128]), ALU.mult, [sm["a"], CST], [rg])

                                for g in range(3):
                                    emit_R(g)
                                for hb in range(2):
                                    pt, ptb = pts[hb]
                                    pv3 = ptb.rearrange("p (h d) -> p h d", d=64)
                                    TT(XDT.t[:, hb * 1024:(hb + 1) * 1024].rearrange("p (h d) -> p h d", d=64), pv3, sm["dt"].t[:, hb * 16:(hb + 1) * 16].unsqueeze(2).to_broadcast([128, 16, 64]), ALU.mult, [pt, sm["dt"]], [XDTb[hb]])
                                    TT(XDD.t[:, hb * 1024:(hb + 1) * 1024].rearrange("p (h d) -> p h d", d=64), pv3, sm["dtd"].t[:, hb * 16:(hb + 1) * 16].unsqueeze(2).to_broadcast([128, 16, 64]), ALU.mult, [pt, sm["dtd"]], [XDDb[hb]])
                                ACT(BTM.t[:, :], ptBb, AF.Copy, [ptB], [BTM])
                                for hf in range(2):
                                    TT(CBM.t[:, hf * 4:(hf + 1) * 4, :], pcbs[hf].t[:, :].rearrange("p (a b) -> p a b", a=4), trif.unsqueeze(1).to_broadcast([128, 4, 128]), ALU.mult, [pcbs[hf], CST], CBMb[hf * 4:(hf + 1) * 4])
                                psegs = {}

                                def emit_seg(g):
                                    rg = rg_of[g]
                                    pseg = ps()
                                    MM(pseg, pseg.t[:, :], [(ustrf, rg.t[:, :, :].rearrange("p a b -> p (a b)"))], [CST, rg])
                                    if g + 3 < 8:
                                        emit_R(g + 3)
                                    lg = Lg[g % 3]
                                    ACT(lg.t[:, :, :].rearrange("p a b -> p (a b)"), pseg.t[:, :], AF.Exp, [pseg], [lg])
                                    mt = MTg[g % 3]
                                    TT(mt.t[:, :, :], lg.t[:, :, :], CBM.t[:, g, :].unsqueeze(1).to_broadcast([128, 4, 128]), ALU.mult, [lg, CBMb[g]], [mt])
                                    return mt

                                mts = {0: emit_seg(0), 1: emit_seg(1)}
                                pyd = pyo = None
                                for g in range(8):
                                    b4, e2_ = divmod(g, 2)
                                    if q + 1 < 4:
                                        prologue_stage(q + 1, g)
                                    if g + 2 < 8:
                                        mts[g + 2] = emit_seg(g + 2)
                                    if e2_ == 0:
                                        pyd = ps()
                                        pyo = ps()
                                    mt = mts[g]
                                    for r4 in range(4):
                                        h = g * 4 + r4
                                        e = e2_ * 4 + r4
                                        MM(pyd, pyd.t[:, e * 64:(e + 1) * 64], [(mt.t[:, r4, :], XDT.t[:, h * 64:(h + 1) * 64])], [mt, XDTb[h // 16]], inc=(r4 == 3))
                                    MM(pyo, pyo.t[:, e2_ * 256:(e2_ + 1) * 256], [(XB.t[:, 24 + g, lo:lo + 128], SB.t[:, g * 256:(g + 1) * 256])], [XBb[24 + g], SBb[b4]])
                                    if e2_ == 1:
                                        T1 = T1s[0]
                                        TT(T1.t[:, :].rearrange("p (h d) -> p h d", d=64), pyo.t[:, :].rearrange("p (h d) -> p h d", d=64), sm["ea"].t[:, b4 * 8:(b4 + 1) * 8].unsqueeze(2).to_broadcast([128, 8, 64]), ALU.mult, [pyo, sm["ea"]], [T1])
                                        TT(YTM.t[:, b4 * 512:(b4 + 1) * 512], pyd.t[:, :], T1.t[:, :], ALU.add, [pyd, T1], [YTMb[b4]])
                                for b4 in range(4):
                                    pst = ps()
                                    for e in range(2):
                                        g = b4 * 2 + e
                                        MM(pst, pst.t[:, e * 256:(e + 1) * 256], [(BTM.t[:, g * 128:(g + 1) * 128], XDD.t[:, g * 256:(g + 1) * 256])], [BTM, XDDb[g // 4]], inc=(e == 1))
                                    sv = S.t[:, b4 * 512:(b4 + 1) * 512]
                                    TT(sv.rearrange("p (h d) -> p h d", d=64), sv.rearrange("p (h d) -> p h d", d=64), sm["cd"].t[:, b4 * 8:(b4 + 1) * 8].unsqueeze(2).to_broadcast([128, 8, 64]), ALU.mult, [Sb[b4], sm["cd"]], [Sb[b4]])
                                    TT(sv, pst.t[:, :], sv, ALU.add, [pst, Sb[b4]], [Sb[b4]])
                                    ACT(SB.t[:, b4 * 512:(b4 + 1) * 512], sv, AF.Copy, [Sb[b4]], [SBb[b4]])
                                for hb in range(2):
                                    pt = ps()
                                    ptb = pt.t[:, :].bitcast(BF16)
                                    for e in range(8):
                                        hh = hb * 8 + e
                                        TR(pt, ptb[:, e * 128:(e + 1) * 128], YTMb[hh // 4], YTM.t[:, hh * 128:(hh + 1) * 128], identb, inc=(e == 7), extra=[CB])
                                    ACT(YT.t[:, hb * 8:(hb + 1) * 8, lo:lo + 128], ptb.rearrange("p (a b) -> p a b", a=8), AF.Copy, [pt], YTb[hb * 8:(hb + 1) * 8])
                                k.mark(f"    L{i} t{tix} chunk{q} end")
                        if smp:
                            with k.phase():
                                SO = k.buf("so", [128, 16, 128], F32)
                                for g4 in range(4):
                                    pb = ps()
                                    for q in range(4):
                                        blk = g4 * 4 + q
                                        TR(pb, pb.t[:, q * 128:(q + 1) * 128], Sb[g4], S.t[:, blk * 128:(blk + 1) * 128], identf, inc=(q == 3), extra=[CST])
                                    CP(SO.t[:, g4 * 4:(g4 + 1) * 4, :], pb.t[:, :].rearrange("p (a b) -> p a b", a=4), [pb], [SO])
                                k.dma(k.sp, osp[j, :, :].rearrange("(a p) n -> p a n", p=128), SO.t[:, :, :], reads=[SO], final=True)
                                OXS = [k.buf(f"ox{a}", [3, 1024], F32) for a in range(2)]
                                for g4 in range(4):
                                    OX = OXS[g4 % 2]
                                    for hf in range(2):
                                        pb = ps()
                                        for q in range(4):
                                            cc = g4 * 8 + hf * 4 + q
                                            TR(pb, pb.t[:3, q * 128:(q + 1) * 128], HISTb[cc], HIST.t[:, cc, :], identf, inc=(q == 3), extra=[CST])
                                        CP(OX.t[:, hf * 512:(hf + 1) * 512], pb.t[:3, :], [pb], [OX])
                                    k.dma(k.sp, oxp[j, :, g4 * 1024:(g4 + 1) * 1024], OX.t[:, :], reads=[OX], final=True)
                            with k.phase():
                                DBC = k.buf("dbc", [128, 16, 32], F32)
                                XDS = k.buf("xds", [128, 16, NSMP], F32)
                                YS = k.buf("ysm", [128, 16, NSMP], F32)
                                with k.phase():
                                    SEL = k.buf("sel", [32, 2048], F32)
                                    k.dma(k.sp, SEL.t[:, :], sel_d[:, :], writes=[SEL])
                                    DTF = k.buf("dtf", [32, 32], F32)
                                    e2 = k.buf("e2", [32, NSMP], F32)
                                    a32 = k.buf("a32", [32, 1], F32)
                                    ACT(a32.t[:, :], V(f"alog32{j}")[:32, :], AF.Exp, [VEC], [a32])
                                    TS(a32.t[:, :], a32.t[:, :], -1.0, None, ALU.mult, None, [a32], [a32])
                                    pd = ps()
                                    MM(pd, pd.t[:32, 0:NSMP], [(sld.t[:, kk * 32:(kk + 1) * 32], HN.t[:, kk, 512:NL]) for kk in range(8)], [sld, HN])
                                    ACT(e2.t[:, :], pd.t[:32, 0:NSMP], AF.Exp, [pd, VEC], [e2], bias=V(f"dtb32{j}")[:32, :])
                                    ACT(DTF.t[:, 0:16], e2.t[:, :], AF.Ln, [e2], [DTF], bias=1.0)
                                    ACT(DTF.t[:, 16:32], DTF.t[:, 0:16], AF.Exp, [DTF, a32], [DTF], scale=a32.t[:, :])
                                    pbq = ps()
                                    for hh in range(16):
                                        MM(pbq, pbq.t[:, hh * 32:(hh + 1) * 32], [(SEL.t[:, hh * 128:(hh + 1) * 128], DTF.t[:, :])], [SEL, DTF], inc=(hh == 15))
                                    CP(DBC.t[:, :, :], pbq.t[:, :].rearrange("p (a b) -> p a b", a=16), [pbq], [DBC])
                                TT(XDS.t[:, :, :], XB.t[:, 0:16, 512:NL], DBC.t[:, :, 0:16], ALU.mult, XBb[0:16] + [DBC], [XDS])
                                DB = k.buf("dgb", [128, 8, 128], BF16)
                                DC = k.buf("dgc", [128, 8, 128], BF16)
                                CS = k.buf("cbs", [128, 8, 128], BF16)
                                SS = [k.buf(f"ss{a}", [128, 16, 128], F32) for a in range(2)]
                                ssb = [k.sub(SS[a], 16, "h") for a in range(2)]
                                TA = k.buf("ta", [128, 16, 128], BF16)

                                def s_load(b):
                                    k.dma(k.sp, SS[b % 2].t[:, :, :], stss[j, b, :, :].rearrange("(a p) n -> p a n", p=128), writes=ssb[b % 2])

                                def s_prep(b):
                                    PTT(DB.t[:, :, :], identb.unsqueeze(1).to_broadcast([128, 8, 128]), XB.t[:, 16:24, 512 + b:512 + b + 1].to_broadcast([128, 8, 128]), ALU.mult, [CB] + XBb[16:24], [DB])
                                    PTT(DC.t[:, :, :], identb.unsqueeze(1).to_broadcast([128, 8, 128]), XB.t[:, 24:32, 512 + b:512 + b + 1].to_broadcast([128, 8, 128]), ALU.mult, [CB] + XBb[24:32], [DC])
                                    pbc = [ps() for _ in range(4)]
                                    for a in range(2):
                                        MM(pbc[a], pbc[a].t[:, :], [(onesb, DB.t[:, a * 4:(a + 1) * 4, :].rearrange("p a b -> p (a b)"))], [CB, DB])
                                    for a in range(2):
                                        MM(pbc[2 + a], pbc[2 + a].t[:, :], [(onesb, DC.t[:, a * 4:(a + 1) * 4, :].rearrange("p a b -> p (a b)"))], [CB, DC])
                                    return pbc

                                s_load(0)
                                pbc_next = s_prep(0)
                                def s_reduce(b):
                                    k.op(k.dve, lambda b=b: nc.vector.tensor_reduce(out=YS.t[:, :, b], in_=TA.t[:, :, :], axis=AX.X, op=ALU.add), [TA], [YS])

                                for b in range(NSMP):
                                    ss = SS[b % 2]
                                    if b + 1 < NSMP:
                                        s_load(b + 1)
                                    pbc = pbc_next
                                    if b + 1 < NSMP:
                                        pbc_next = s_prep(b + 1)
                                    for hh in range(16):
                                        ACT(ss.t[:, hh, :], ss.t[:, hh, :], AF.Copy, [ssb[b % 2][hh], DBC], [ssb[b % 2][hh]], scale=DBC.t[:, hh, 16 + b:17 + b])
                                    for hh in range(16):
                                        g = hh // 2
                                        STT(ss.t[:, hh, :], pbc[g // 4].t[:, (g % 4) * 128:(g % 4 + 1) * 128], XDS.t[:, hh, b:b + 1], ss.t[:, hh, :], ALU.mult, ALU.add, [pbc[g // 4], XDS, ssb[b % 2][hh]], [ssb[b % 2][hh]])
                                    if b >= 1:
                                        s_reduce(b - 1)
                                    k.dma(k.sp, oss[j, b, :, :].rearrange("(a p) n -> p a n", p=128), ss.t[:, :, :], reads=ssb[b % 2], final=True)
                                    for a in range(2):
                                        ACT(CS.t[:, a * 4:(a + 1) * 4, :], pbc[2 + a].t[:, :].rearrange("p (g n) -> p g n", n=128), AF.Copy, [pbc[2 + a]], [CS])
                                    PTT(TA.t[:, :, :].rearrange("p (g e) n -> p g e n", e=2),
                                        ss.t[:, :, :].rearrange("p (g e) n -> p g e n", e=2),
                                        CS.t[:, :, :].unsqueeze(2).to_broadcast([128, 8, 2, 128]),
                                        ALU.mult, ssb[b % 2] + [CS], [TA])
                                    if b == NSMP - 1:
                                        s_reduce(b)
                                CP(YT.t[:, :, 512:NL], YS.t[:, :, :], [YS], YTb)
                        k.mark(f"  L{i} t{tix} core/sample end")
                        zi = 0
                        for it in range(8):
                            sl = wload([(win[j, :, it * 256:(it + 1) * 256], 8, 256, 0)])
                            for e in range(2):
                                zc = it * 2 + e
                                for (l, g, n) in lsubs:
                                    pz = ps()
                                    MM(pz, pz.t[:, :n], [(sl.t[:, kk * 256 + e * 128: kk * 256 + (e + 1) * 128], HN.t[:, kk, l:l + n]) for kk in range(8)], [sl, HN])
                                    gq = GS[zi % 2]
                                    zi += 1
                                    ACT(gq.t[:, 0, :n], pz.t[:, :n], AF.Silu, [pz], [gq])
                                    TS(gq.t[:, 1, :n], XB.t[:, zc, l:l + n], V(f"sD{j}", zc), None, ALU.mult, None, [XBb[zc], VEC, gq], [gq])
                                    TT(YT.t[:, zc, l:l + n], YT.t[:, zc, l:l + n], gq.t[:, 1, :n], ALU.add, [YTb[zc], gq], [YTb[zc]])
                                    TT(YT.t[:, zc, l:l + n], YT.t[:, zc, l:l + n], gq.t[:, 0, :n], ALU.mult, [YTb[zc], gq], [YTb[zc]])
                        k.mark(f"  L{i} t{tix} z end")
                        gjobs = [(g8, l, g, n) for g8 in range(8) for (l, g, n) in lsubs]

                        def gn_sq(idx):
                            g8, l, g, n = gjobs[idx]
                            gq = GS[idx % 2]
                            ACT(gq.t[:, 0:2, :n], YT.t[:, 2 * g8:2 * g8 + 2, l:l + n], AF.Square, YTb[2 * g8:2 * g8 + 2], [gq])
                            pn = ps()
                            MM(pn, pn.t[:, :n], [(onesb, gq.t[:, e, :n]) for e in range(2)], [gq, CB])
                            return pn

                        pn_next = gn_sq(0)
                        for idx, (g8, l, g, n) in enumerate(gjobs):
                            pn = pn_next
                            if idx + 1 < len(gjobs):
                                pn_next = gn_sq(idx + 1)
                            rt = rts[idx % 2]
                            RSTD(rt, n, pn.t[:, :n], pn, 1.0 / 256)
                            for e in range(2):
                                STT(YT.t[:, 2 * g8 + e, l:l + n], YT.t[:, 2 * g8 + e, l:l + n], V(f"snorm{j}", 2 * g8 + e), rt.t[:, :n], ALU.mult, ALU.mult, [YTb[2 * g8 + e], rt, VEC], [YTb[2 * g8 + e]])
                        for m in range(8):
                            sl = wload([(wout[j, :, m * 128:(m + 1) * 128], 16, 128, 0)])
                            for (l, g, n) in lsubs:
                                po = ps()
                                MM(po, po.t[:, :n], [(wv(sl, 0, kk, 128), YT.t[:, kk, l:l + n]) for kk in range(16)], [sl] + YTb)
                                TT(X.t[:, m, g:g + n], po.t[:, :n], X.t[:, m, g:g + n], ALU.add, [po, X1(m, g)], [X1(m, g)])

            k.mark("start")
            for i in range(DEPTH):
                ffn(i, 1)
                k.mark(f"L{i} ffn1 end")
                if i % 2 == 0:
                    conv_mixer(i, i // 2)
                else:
                    ssd_mixer(i, i // 2)
                k.mark(f"L{i} mixer end")
                ffn(i, 2)
                k.mark(f"L{i} ffn2 end")
                ple(i)
                k.mark(f"L{i} ple end")

            with k.phase():
                YN = k.buf("YN", [128, 8, NT], F32)
                sqs = [k.buf(f"sq{a}", [128, 8, 512], BF16) for a in range(2)]
                rts = [k.buf(f"rt{a}", [128, 512], F32) for a in range(2)]
                ost = [k.buf(f"yo{a}", [128, D], F32) for a in range(2)]
                YNb = k.sub(YN, 5, "t")
                rmsnorm("final_norm", lambda do: [YNb[tix_of(do)]], lambda kk, do, n: YN.t[:, kk, do:do + n], [(s, s, n) for s, n in TILES], sqs, rts)
                for r in range(17):
                    R = 128 if r < 16 else NSMP
                    o = ost[r % 2]
                    for half in range(2):
                        pb = ps()
                        for q in range(4):
                            c = half * 4 + q
                            TR(pb, pb.t[:R, q * 128:(q + 1) * 128], YNb[tix_of(r * 128)], YN.t[:, c, r * 128:r * 128 + R], identf, inc=(q == 3), extra=[CST])
                        if half == 0:
                            ACT(o.t[:R, 0:512], pb.t[:R, :], AF.Copy, [pb], [o])
                        else:
                            CP(o.t[:R, 512:1024], pb.t[:R, :], [pb], [o])
                    dst = yp[r * 128:(r + 1) * 128, :] if r < 16 else ysd[:, :]
                    k.dma(k.sp, dst, o.t[:R, :], reads=[o], final=True)
                if not k.dry:
                    for dep in k.final.values():
                        k._wait(k.sp, dep)
                    if k.ddcnt:
                        nc.sync.wait_ge(dsem_dd, k.ddcnt)

        k.dry = True
        k.wplan = []
        k.ddcnt = 0
        body()
        k.dry = False
        k.wissued = 0
        k.ddcnt = 0
        body()
        build.marks = getattr(k, "marks", [])
    return nc


_NC_CACHE = {}


def make_in_maps(inp):
    inp = {k_: np.asarray(v) for k_, v in inp.items()}
    vecs = np.ascontiguousarray(np.concatenate([a for (_, _, a) in vec_entries(inp)], axis=1), dtype=np.float32)
    cst = const_array()
    sel = sel_array()
    wnames = ["w_ffn1_gate", "w_ffn1_up", "w_ffn1_down", "w_ffn2_gate", "w_ffn2_up", "w_ffn2_down", "w_ple_gate", "w_ple_proj",
              "cm_w_in", "cm_w_out", "ssd_w_in", "ssd_w_out"]
    shared = {nm: np.ascontiguousarray(inp[nm], dtype=np.float32) for nm in wnames}
    shared.update(vecs=vecs, cst=cst, sel=sel)
    in_maps = []
    for c in range(NCORES):
        b0 = c * NSMP
        m = dict(shared)
        m["xp"] = np.ascontiguousarray(inp["x_prompt"][c], dtype=np.float32)
        m["xs"] = np.ascontiguousarray(inp["x_sample"][b0:b0 + NSMP, 0], dtype=np.float32)
        m["stc"] = np.ascontiguousarray(inp["state_conv"][:, b0:b0 + NSMP].reshape(2, NSMP * 30, D), dtype=np.float32)
        m["stsc"] = np.ascontiguousarray(inp["state_ssd_conv"][:, b0:b0 + NSMP].reshape(2, NSMP * 3, CONVD), dtype=np.float32)
        m["stss"] = np.ascontiguousarray(inp["state_ssd"][:, b0:b0 + NSMP].reshape(2, NSMP, NH * 64, 128), dtype=np.float32)
        m["pp"] = np.ascontiguousarray(inp["p_prompt"][:, c], dtype=np.float32)
        m["psm"] = np.ascontiguousarray(inp["p_sample"][:, b0:b0 + NSMP, 0], dtype=np.float32)
        in_maps.append(m)
    return in_maps


def kernel(**inp):
    if "nc" not in _NC_CACHE:
        _NC_CACHE["nc"] = build()
    nc = _NC_CACHE["nc"]
    in_maps = make_in_maps(inp)
    res = run_bass_kernel_spmd(nc, in_maps, core_ids=list(range(NCORES)))
    R = res.results
    y_prompt = np.stack([R[c]["yp"] for c in range(NCORES)], 0).astype(np.float32)
    y_sample = np.concatenate([R[c]["ys"] for c in range(NCORES)], 0).reshape(NCORES * NSMP, 1, D).astype(np.float32)
    conv_p = np.stack([R[c]["ocp"] for c in range(NCORES)], 1).astype(np.float32)
    xbc_p = np.stack([R[c]["oxp"] for c in range(NCORES)], 1).astype(np.float32)
    ssm_p = np.stack([R[c]["osp"].reshape(2, NH, 64, 128) for c in range(NCORES)], 1).astype(np.float32)
    conv_s = np.concatenate([R[c]["ocs"] for c in range(NCORES)], 1).astype(np.float32)
    xbc_s = np.concatenate([R[c]["oxs"] for c in range(NCORES)], 1).astype(np.float32)
    ssm_s = np.concatenate([R[c]["oss"].reshape(2, NSMP, NH, 64, 128) for c in range(NCORES)], 1).astype(np.float32)
    return (y_prompt, y_sample, conv_p, xbc_p, ssm_p, conv_s, xbc_s, ssm_s)
```
